# Optimizing a Trainium2 kernel written in Bass

```python
import jax, jax.numpy as jnp
from jax import lax
import numpy as np

D_MODEL = 1024
BATCH = 8
SEQ = 2048
DEPTH = 1
DEC_BATCH = 128
DEC_SEQ = 1
PAST_LEN = 2048
PAGE_SIZE = 128

H_ATT = 8
DH_ATT = 64
ATT_WIDTH = H_ATT * DH_ATT
MOBA_BLOCK = 256
MOBA_TOPK = 3
Q_CHUNK = 32
H_MLSTM = 4
DK_MLSTM = 128
DV_MLSTM = 128
MLSTM_WIDTH = H_MLSTM * DV_MLSTM
MLSTM_CHUNK = 64
MIX_WIDTH = ATT_WIDTH + MLSTM_WIDTH
D_FF = ((8 * D_MODEL + 3 * 256 - 1) // (3 * 256)) * 256
ALPHA = (2.0 * DEPTH) ** 0.25
BETA = (8.0 * DEPTH) ** -0.25
LN_EPS = 1e-5
NEG = -1e30
IN_SIZES = (ATT_WIDTH, ATT_WIDTH, ATT_WIDTH, H_MLSTM * DK_MLSTM, H_MLSTM * DK_MLSTM,
            MLSTM_WIDTH, MLSTM_WIDTH, H_MLSTM, H_MLSTM)
IN_SPLITS = tuple(int(s) for s in np.cumsum(IN_SIZES)[:-1])
IN_COLS = sum(IN_SIZES)

kernel_name = 'hymba_moba_mlstm_deepnorm_adaln_step'


def _layernorm(x, g, b):
    xf = x.astype(jnp.float32)
    mu = xf.mean(-1, keepdims=True)
    var = jnp.square(xf - mu).mean(-1, keepdims=True)
    return ((xf - mu) * lax.rsqrt(var + LN_EPS) * g + b).astype(x.dtype)


def _adaln(c, w_ada, b_ada):
    return jnp.split(jax.nn.silu(c) @ w_ada + b_ada, 6, axis=-1)


def _in_proj(x, shift, scale, w_in, b_if):
    B, S, _ = x.shape
    h = x * (1.0 + scale[:, None, :]) + shift[:, None, :]
    aq, ak, av, mq, mk, mv, mo, mi, mf = jnp.split(h @ w_in, IN_SPLITS, axis=-1)
    att = lambda t: t.reshape(B, S, H_ATT, DH_ATT)
    mq = mq.reshape(B, S, H_MLSTM, DK_MLSTM).astype(jnp.float32)
    mk = (mk.reshape(B, S, H_MLSTM, DK_MLSTM) * DK_MLSTM ** -0.5).astype(jnp.float32)
    mv = mv.reshape(B, S, H_MLSTM, DV_MLSTM).astype(jnp.float32)
    ig = (mi + b_if[:H_MLSTM]).astype(jnp.float32)
    lf = jax.nn.log_sigmoid((mf + b_if[H_MLSTM:]).astype(jnp.float32))
    return att(aq), att(ak), att(av), mq, mk, mv, mo, ig, lf


def _moba_core(q, qblk, kmean, kb, vb, s_own, own_mask):
    B, Q, H, Dh = q.shape
    nb = kb.shape[1]
    scale = DH_ATT ** -0.5
    blk_score = jnp.einsum('bqhd,bnhd->bqhn', q.astype(jnp.float32), kmean)
    past = jnp.arange(nb)[None, :] < qblk[:, None]
    blk_score = jnp.where(past[None, :, None, :], blk_score, NEG)
    _, sel = lax.top_k(blk_score, min(MOBA_TOPK, nb))
    valid = sel < qblk[None, :, None, None]
    bi = jnp.arange(B)[:, None, None, None]
    hi = jnp.arange(H)[None, None, :, None]
    k_sel = kb[bi, sel, :, hi]
    v_sel = vb[bi, sel, :, hi]
    n_sel = sel.shape[-1] * MOBA_BLOCK
    s_sel = jnp.einsum('bqhd,bqhjsd->bqhjs', q, k_sel,
                       preferred_element_type=jnp.float32).reshape(B, Q, H, n_sel)
    mask_sel = jnp.broadcast_to(valid[..., None], k_sel.shape[:-1]).reshape(B, Q, H, n_sel)
    s = jnp.concatenate([jnp.where(own_mask, s_own * scale, NEG),
                         jnp.where(mask_sel, s_sel * scale, NEG)], axis=-1)
    p = jax.nn.softmax(s, axis=-1).astype(vb.dtype)
    p_own = p[..., :MOBA_BLOCK]
    p_sel = p[..., MOBA_BLOCK:].reshape(k_sel.shape[:-1])
    o_sel = jnp.einsum('bqhjs,bqhjsd->bqhd', p_sel, v_sel)
    return p_own, o_sel


def _moba_prompt(q, k, v):
    B, S, H, Dh = q.shape
    nb = -(-S // MOBA_BLOCK)
    pad = ((0, 0), (0, nb * MOBA_BLOCK - S), (0, 0), (0, 0))
    kb = jnp.pad(k, pad).reshape(B, nb, MOBA_BLOCK, H, Dh)
    vb = jnp.pad(v, pad).reshape(B, nb, MOBA_BLOCK, H, Dh)
    kmean = kb.astype(jnp.float32).mean(axis=2)

    def q_block(ci):
        start = ci * Q_CHUNK
        qc = lax.dynamic_slice_in_dim(q, start, Q_CHUNK, axis=1)
        pos = start + jnp.arange(Q_CHUNK)
        blk = start // MOBA_BLOCK
        k_own = lax.dynamic_index_in_dim(kb, blk, axis=1, keepdims=False)
        v_own = lax.dynamic_index_in_dim(vb, blk, axis=1, keepdims=False)
        s_own = jnp.einsum('bqhd,bshd->bqhs', qc, k_own, preferred_element_type=jnp.float32)
        own_mask = (blk * MOBA_BLOCK + jnp.arange(MOBA_BLOCK))[None, :] <= pos[:, None]
        p_own, o_sel = _moba_core(qc, pos // MOBA_BLOCK, kmean, kb, vb, s_own,
                                  own_mask[None, :, None, :])
        return (o_sel + jnp.einsum('bqhs,bshd->bqhd', p_own, v_own)).astype(q.dtype)

    out = lax.map(q_block, jnp.arange(S // Q_CHUNK))
    return jnp.moveaxis(out, 0, 1).reshape(B, S, H * Dh)


def _moba_sample(q, k, v, cache_k, cache_v, page_table):
    B, T, H, Dh = q.shape
    past_len = page_table.shape[1] * PAGE_SIZE
    total = past_len + T
    nb = -(-total // MOBA_BLOCK)
    fill = jnp.zeros((B, nb * MOBA_BLOCK - total, H, Dh), k.dtype)
    kb = jnp.concatenate([cache_k[page_table].reshape(B, past_len, H, Dh).astype(k.dtype), k, fill],
                         axis=1).reshape(B, nb, MOBA_BLOCK, H, Dh)
    vb = jnp.concatenate([cache_v[page_table].reshape(B, past_len, H, Dh).astype(v.dtype), v, fill],
                         axis=1).reshape(B, nb, MOBA_BLOCK, H, Dh)
    kmean = kb.astype(jnp.float32).mean(axis=2)
    pos = past_len + jnp.arange(T)
    qblk = pos // MOBA_BLOCK
    k_own = kb[:, qblk]
    v_own = vb[:, qblk]
    s_own = jnp.einsum('bqhd,bqshd->bqhs', q, k_own, preferred_element_type=jnp.float32)
    own_mask = (qblk[:, None] * MOBA_BLOCK + jnp.arange(MOBA_BLOCK)[None, :]) <= pos[:, None]
    p_own, o_sel = _moba_core(q, qblk, kmean, kb, vb, s_own, own_mask[None, :, None, :])
    out = o_sel + jnp.einsum('bqhs,bqshd->bqhd', p_own, v_own)
    return out.astype(q.dtype).reshape(B, T, H * Dh)


def _mlstm_chunk(carry, inp):
    C, n, m = carry
    q, k, v, ig, lf = inp
    L = q.shape[1]
    b = jnp.cumsum(lf, axis=1)
    causal = jnp.tril(jnp.ones((L, L), bool))
    dmat = jnp.where(causal[None, :, :, None],
                     b[:, :, None, :] - b[:, None, :, :] + ig[:, None, :, :], NEG)
    m_inter = b + m[:, None, :]
    m_t = jnp.maximum(m_inter, dmat.max(axis=2))
    w_inter = jnp.exp(m_inter - m_t)
    a = jnp.exp(dmat - m_t[:, :, None, :]) * jnp.einsum('bthd,bjhd->btjh', q, k)
    num = w_inter[..., None] * jnp.einsum('bhvd,bthd->bthv', C, q) + jnp.einsum('btjh,bjhv->bthv', a, v)
    den = w_inter * jnp.einsum('bhd,bthd->bth', n, q) + a.sum(axis=2)
    h = num / jnp.maximum(jnp.abs(den), jnp.exp(-m_t))[..., None]
    m_new = m_t[:, -1]
    g_inter = jnp.exp(b[:, -1] + m - m_new)
    g_in = jnp.exp(b[:, -1:] - b + ig - m_new[:, None])
    C_new = g_inter[..., None, None] * C + jnp.einsum('bjh,bjhv,bjhd->bhvd', g_in, v, k)
    n_new = g_inter[..., None] * n + jnp.einsum('bjh,bjhd->bhd', g_in, k)
    return (C_new, n_new, m_new), h


def _mlstm_prompt(q, k, v, ig, lf):
    B, S = q.shape[:2]
    nc = S // MLSTM_CHUNK
    chunks = lambda a: jnp.moveaxis(a.reshape(B, nc, MLSTM_CHUNK, *a.shape[2:]), 1, 0)
    init = (jnp.zeros((B, H_MLSTM, DV_MLSTM, DK_MLSTM), jnp.float32),
            jnp.zeros((B, H_MLSTM, DK_MLSTM), jnp.float32),
            jnp.zeros((B, H_MLSTM), jnp.float32))
    state, h = lax.scan(_mlstm_chunk, init, (chunks(q), chunks(k), chunks(v), chunks(ig), chunks(lf)))
    return jnp.moveaxis(h, 0, 1).reshape(B, S, H_MLSTM, DV_MLSTM), state


def _mlstm_out(h, o, g):
    mu = h.mean(-1, keepdims=True)
    var = jnp.square(h - mu).mean(-1, keepdims=True)
    hn = ((h - mu) * lax.rsqrt(var + LN_EPS)).reshape(*h.shape[:2], MLSTM_WIDTH)
    return (hn * g * jax.nn.sigmoid(o.astype(jnp.float32))).astype(o.dtype)


def _finish(x, att, mem, g1, sh2, sc2, g2, w_out, ln1_g, ln1_b, w_gate, w_up, w_down, ln2_g, ln2_b):
    mix = jnp.concatenate([att, mem], axis=-1) @ w_out
    x = _layernorm(ALPHA * x + (1.0 + g1[:, None, :]) * mix, ln1_g, ln1_b)
    h = x * (1.0 + sc2[:, None, :]) + sh2[:, None, :]
    f = (jax.nn.silu(h @ w_gate) * (h @ w_up)) @ w_down
    return _layernorm(ALPHA * x + (1.0 + g2[:, None, :]) * f, ln2_g, ln2_b)


def _layer_prompt(x, c, weights):
    (w_ada, b_ada, w_in, b_if, mlstm_norm_g, w_out, ln1_g, ln1_b,
     w_gate, w_up, w_down, ln2_g, ln2_b) = weights
    sh1, sc1, g1, sh2, sc2, g2 = _adaln(c, w_ada, b_ada)
    aq, ak, av, mq, mk, mv, mo, ig, lf = _in_proj(x, sh1, sc1, w_in, b_if)
    att = _moba_prompt(aq, ak, av)
    h, (C, n, m) = _mlstm_prompt(mq, mk, mv, ig, lf)
    y = _finish(x, att, _mlstm_out(h, mo, mlstm_norm_g), g1, sh2, sc2, g2,
                w_out, ln1_g, ln1_b, w_gate, w_up, w_down, ln2_g, ln2_b)
    return y, ak, av, C.astype(x.dtype), n.astype(x.dtype), m.astype(x.dtype)


def _layer_sample(x, c, cache_k, cache_v, page_table, C0, n0, m0, weights):
    (w_ada, b_ada, w_in, b_if, mlstm_norm_g, w_out, ln1_g, ln1_b,
     w_gate, w_up, w_down, ln2_g, ln2_b) = weights
    sh1, sc1, g1, sh2, sc2, g2 = _adaln(c, w_ada, b_ada)
    aq, ak, av, mq, mk, mv, mo, ig, lf = _in_proj(x, sh1, sc1, w_in, b_if)
    att = _moba_sample(aq, ak, av, cache_k, cache_v, page_table)
    carry = (C0.astype(jnp.float32), n0.astype(jnp.float32), m0.astype(jnp.float32))
    (C, n, m), h = _mlstm_chunk(carry, (mq, mk, mv, ig, lf))
    y = _finish(x, att, _mlstm_out(h, mo, mlstm_norm_g), g1, sh2, sc2, g2,
                w_out, ln1_g, ln1_b, w_gate, w_up, w_down, ln2_g, ln2_b)
    return y, ak, av, C.astype(x.dtype), n.astype(x.dtype), m.astype(x.dtype)


def setup_inputs(seed: int = 0) -> dict:
    key = jax.random.key(seed)
    ks = jax.random.split(key, 26)
    n_pages = PAST_LEN // PAGE_SIZE
    n_used = DEC_BATCH * n_pages
    n_phys = n_used + max(1, n_used // 4)
    nrm = lambda k, shape, s: jax.random.normal(k, shape, jnp.float32) * s
    page_table = jax.random.permutation(ks[7], n_phys)[:n_used].reshape(DEC_BATCH, n_pages).astype(jnp.int32)
    f_bias = jnp.broadcast_to(jnp.linspace(3.0, 6.0, H_MLSTM), (DEPTH, H_MLSTM))
    b_if = jnp.concatenate([jnp.full((DEPTH, H_MLSTM), -2.0), f_bias], axis=-1) + nrm(ks[13], (DEPTH, 2 * H_MLSTM), 0.1)
    return {
        'x_prompt': nrm(ks[0], (BATCH, SEQ, D_MODEL), 1.0),
        'x_sample': nrm(ks[1], (DEC_BATCH, DEC_SEQ, D_MODEL), 1.0),
        'cache_k': nrm(ks[2], (DEPTH, n_phys, PAGE_SIZE, H_ATT, DH_ATT), 1.0),
        'cache_v': nrm(ks[3], (DEPTH, n_phys, PAGE_SIZE, H_ATT, DH_ATT), 1.0),
        'state_C': nrm(ks[4], (DEPTH, DEC_BATCH, H_MLSTM, DV_MLSTM, DK_MLSTM), 0.5),
        'state_n': nrm(ks[5], (DEPTH, DEC_BATCH, H_MLSTM, DK_MLSTM), 0.5),
        'state_m': nrm(ks[6], (DEPTH, DEC_BATCH, H_MLSTM), 1.0),
        'page_table': page_table,
        'c_prompt': nrm(ks[8], (BATCH, D_MODEL), 1.0),
        'c_sample': nrm(ks[9], (DEC_BATCH, D_MODEL), 1.0),
        'w_ada': nrm(ks[10], (DEPTH, D_MODEL, 6 * D_MODEL), 0.1 * D_MODEL ** -0.5),
        'b_ada': nrm(ks[11], (DEPTH, 6 * D_MODEL), 0.01),
        'w_in': nrm(ks[12], (DEPTH, D_MODEL, IN_COLS), D_MODEL ** -0.5),
        'b_if': b_if,
        'mlstm_norm_g': 1.0 + nrm(ks[14], (DEPTH, MLSTM_WIDTH), 0.01),
        'w_out': nrm(ks[15], (DEPTH, MIX_WIDTH, D_MODEL), BETA * MIX_WIDTH ** -0.5),
        'ln1_g': 1.0 + nrm(ks[16], (DEPTH, D_MODEL), 0.01),
        'ln1_b': nrm(ks[17], (DEPTH, D_MODEL), 0.01),
        'w_gate': nrm(ks[18], (DEPTH, D_MODEL, D_FF), D_MODEL ** -0.5),
        'w_up': nrm(ks[19], (DEPTH, D_MODEL, D_FF), D_MODEL ** -0.5),
        'w_down': nrm(ks[20], (DEPTH, D_FF, D_MODEL), BETA * D_FF ** -0.5),
        'ln2_g': 1.0 + nrm(ks[21], (DEPTH, D_MODEL), 0.01),
        'ln2_b': nrm(ks[22], (DEPTH, D_MODEL), 0.01),
    }


def reference(x_prompt, x_sample, cache_k, cache_v, state_C, state_n, state_m, page_table,
              c_prompt, c_sample, w_ada, b_ada, w_in, b_if, mlstm_norm_g, w_out,
              ln1_g, ln1_b, w_gate, w_up, w_down, ln2_g, ln2_b):
    y_prompt, y_sample = x_prompt, x_sample
    new_p, new_s = [], []
    for l in range(DEPTH):
        wl = (w_ada[l], b_ada[l], w_in[l], b_if[l], mlstm_norm_g[l], w_out[l], ln1_g[l], ln1_b[l],
              w_gate[l], w_up[l], w_down[l], ln2_g[l], ln2_b[l])
        y_prompt, *sp = _layer_prompt(y_prompt, c_prompt, wl)
        y_sample, *ss = _layer_sample(y_sample, c_sample, cache_k[l], cache_v[l], page_table,
                                      state_C[l], state_n[l], state_m[l], wl)
        new_p.append(sp)
        new_s.append(ss)
    k_prompt, v_prompt, C_prompt, n_prompt, m_prompt = (jnp.stack(a) for a in zip(*new_p))
    k_sample, v_sample, C_sample, n_sample, m_sample = (jnp.stack(a) for a in zip(*new_s))
    return (y_prompt, y_sample, k_prompt, v_prompt, C_prompt, n_prompt, m_prompt,
            k_sample, v_sample, C_sample, n_sample, m_sample)
```

```python
import numpy as np
from contextlib import ExitStack, contextmanager
import concourse.bass as bass
import concourse.mybir as mybir
from concourse.bass_utils import run_bass_kernel_spmd

F32 = mybir.dt.float32
BF16 = mybir.dt.bfloat16
I32 = mybir.dt.int32
ALU = mybir.AluOpType
AF = mybir.ActivationFunctionType
AX = mybir.AxisListType

D = 1024
S = 2048
NS = 16
T = S + NS
NTT = 16
HA = 8
DFF = 2816
NFF = 22
INC = 3592
NEG = -30000.0
ALPHA = 2.0 ** 0.25
EPS = 1e-5
NPHYS = 2560
SERIAL_SAMPLE = False
SAME_ENG_SYNC = True


class _Reg:
    __slots__ = ("w", "r")

    def __init__(self):
        self.w = None
        self.r = []


class _Eng:
    def __init__(self, name):
        self.name = name
        self.ops = []
        self.count = 0
        self.sem = None
        self.waited = {}


class Sched:
    NDMA = 48

    def __init__(self, nc, es):
        self.nc = nc
        self.E = {n: _Eng(n) for n in ("tensor", "vector", "scalar", "gpsimd", "sync")}
        for n, e in self.E.items():
            e.sem = es.enter_context(nc.semaphore("s_" + n))
        self.dsem = [es.enter_context(nc.semaphore("d%d" % i)) for i in range(self.NDMA)]
        self.dcnt = [0] * self.NDMA
        self.dnext = 0
        self.regs = {}
        self.final_events = []

    def reg(self, key):
        r = self.regs.get(key)
        if r is None:
            r = self.regs[key] = _Reg()
        return r

    def _emit_waits(self, e, evs):
        need = {}
        for ev in evs:
            if ev is None:
                continue
            sem, val, src = ev
            if src == e.name and (e.name == "tensor" or not SAME_ENG_SYNC):
                continue
            if e.waited.get(sem, 0) >= val:
                continue
            if need.get(sem, 0) < val:
                need[sem] = val
        for sem, val in need.items():
            e.waited[sem] = val
            e.ops.append(("wait", sem, val))

    def _deps(self, R, W):
        evs = []
        for r in R:
            evs.append(r.w)
        for w in W:
            evs.append(w.w)
            evs.extend(w.r)
        return evs

    def _commit(self, ev, R, W):
        for r in R:
            r.r.append(ev)
        for w in W:
            w.w = ev
            w.r = []

    def op(self, eng, fn, reads=(), writes=()):
        e = self.E[eng]
        pr = [k for k in reads if isinstance(k, tuple) and k[0] == "pb"]
        if pr:
            reads = [k for k in reads if not (isinstance(k, tuple) and k[0] == "pb")]
            writes = list(writes) + [k for k in pr if k not in writes]
        R = [self.reg(k) for k in reads]
        W = [self.reg(k) for k in writes]
        self._emit_waits(e, self._deps(R, W))
        e.count += 1
        ev = (e.sem, e.count, eng)
        e.ops.append(("op", fn, e.sem, 1))
        self._commit(ev, R, W)
        return ev

    def dma(self, queue, fn, reads=(), writes=(), final=False):
        e = self.E[queue]
        R = [self.reg(k) for k in reads]
        W = [self.reg(k) for k in writes]
        s = self.dnext
        self.dnext = (self.dnext + 1) % self.NDMA
        sem = self.dsem[s]
        evs = self._deps(R, W)
        if self.dcnt[s] > 0:
            evs.append((sem, self.dcnt[s], "dma"))
        self._emit_waits(e, evs)
        self.dcnt[s] += 16
        ev = (sem, self.dcnt[s], "dma")
        e.ops.append(("op", fn, sem, 16))
        self._commit(ev, R, W)
        if final:
            self.final_events.append(ev)
        return ev

    def barrier(self):
        evs = [(e.sem, e.count, n) for n, e in self.E.items() if e.count > 0]
        evs += [(self.dsem[i], self.dcnt[i], "dma") for i in range(self.NDMA) if self.dcnt[i] > 0]
        for n, e in self.E.items():
            self._emit_waits(e, [ev for ev in evs if ev[2] != n])

    def finish(self):
        self._emit_waits(self.E["sync"], self.final_events)

    def replay(self):
        with self.nc.Block() as block:
            def mk(name):
                ops = self.E[name].ops

                def body(eng):
                    for o in ops:
                        if o[0] == "wait":
                            eng.wait_ge(o[1], o[2])
                        else:
                            o[1](eng).then_inc(o[2], o[3])
                return body

            block.tensor(mk("tensor"))
            block.vector(mk("vector"))
            block.scalar(mk("scalar"))
            block.gpsimd(mk("gpsimd"))
            block.sync(mk("sync"))


class Builder:
    def __init__(self, stage=99, nphys=NPHYS, debug=False):
        self.stage = stage
        self.debug = debug
        self.nphys = nphys
        self.nc = bass.Bass("TRN2", target_bir_lowering=False)
        self.es = ExitStack()
        self.ins = {}
        self.outs = {}

    def din(self, name, shape, dt=F32):
        ap = self.nc.dram_tensor(name, list(shape), dt, kind="ExternalInput").ap()
        self.ins[name] = ap
        return ap

    def dout(self, name, shape, dt=F32):
        ap = self.nc.dram_tensor(name, list(shape), dt, kind="ExternalOutput").ap()
        self.outs[name] = ap
        return ap

    def sb(self, es, name, shape, dt):
        self._uid = getattr(self, "_uid", 0) + 1
        return es.enter_context(self.nc.sbuf_tensor("%s_u%d" % (name, self._uid), list(shape), dt))

    @contextmanager
    def scope(self):
        es = ExitStack()
        try:
            yield es
        finally:
            es.close()
            self.S.barrier()

    def mm(self, out, lhsT, rhs, start, stop, reads, writes):
        self.S.op("tensor", lambda e: e.matmul(out, lhsT=lhsT, rhs=rhs, start=start, stop=stop),
                  reads=reads, writes=writes)

    def tp(self, out, in_, ident, reads, writes):
        self.S.op("tensor", lambda e: e.transpose(out=out, in_=in_, identity=ident),
                  reads=list(reads) + ["identf"], writes=writes)

    def load(self, out, in_, writes, queue="sync", reads=()):
        self.S.dma(queue, lambda e: e.dma_start(out=out, in_=in_), reads=reads, writes=writes)

    def store(self, out, in_, reads, queue="sync"):
        self.S.dma(queue, lambda e: e.dma_start(out=out, in_=in_), reads=reads, final=True)

    def V(self, fn, reads=(), writes=()):
        self.S.op("vector", fn, reads=reads, writes=writes)

    def A(self, fn, reads=(), writes=()):
        self.S.op("scalar", fn, reads=reads, writes=writes)

    def G(self, fn, reads=(), writes=()):
        self.S.op("gpsimd", fn, reads=reads, writes=writes)

    def build(self):
        nc = self.nc
        es = self.es
        self.S = Sched(nc, es)
        self.xp = self.din("xp", [S, D])
        self.xs = self.din("xs", [NS, D])
        self.cT = self.din("cT", [128, 8, 33])
        self.w_ada = self.din("w_ada", [D, 6 * D])
        self.b_adaT = self.din("b_adaT", [128, 48])
        self.b_ada = self.din("b_ada", [1, 6 * D])
        self.w_in = self.din("w_in", [D, INC])
        self.b_if = self.din("b_if", [1, 8])
        self.mgT = self.din("mgT", [128, 4])
        self.mg = self.din("mg", [1, 512])
        self.w_out = self.din("w_out", [D, D])
        self.ln1_g = self.din("ln1_g", [1, D])
        self.ln1_b = self.din("ln1_b", [1, D])
        self.w_gate = self.din("w_gate", [D, DFF])
        self.w_up = self.din("w_up", [D, DFF])
        self.w_down = self.din("w_down", [DFF, D])
        self.ln2_g = self.din("ln2_g", [1, D])
        self.ln2_b = self.din("ln2_b", [1, D])
        self.cache_k = self.din("cache_k", [self.nphys * 128, 512])
        self.cache_v = self.din("cache_v", [self.nphys * 128, 512])
        self.pt = self.din("pt", [1, NS * 16], I32)
        self.st_C = self.din("st_C", [NS, 4, 128, 128])
        self.st_n = self.din("st_n", [NS, 512])
        self.st_m = self.din("st_m", [NS, 4])

        self.y_p = self.dout("y_p", [S, D])
        self.y_s = self.dout("y_s", [NS, D])
        self.k_p = self.dout("k_p", [S, 512])
        self.v_p = self.dout("v_p", [S, 512])
        self.C_p = self.dout("C_p", [4, 128, 128])
        self.n_p = self.dout("n_p", [4, 128])
        self.m_p = self.dout("m_p", [4, 1])
        self.k_s = self.dout("k_s", [NS, 512])
        self.v_s = self.dout("v_s", [NS, 512])
        self.C_s = self.dout("C_s", [NS, 4, 128, 128])
        self.n_s = self.dout("n_s", [NS, 512])
        self.m_s = self.dout("m_s", [NS, 4])
        if self.debug:
            self.dbg = self.dout("dbg", [128, 8, T], BF16)

        self.pb = [es.enter_context(nc.psum_tensor("pb%d" % i, [128, 512], F32)) for i in range(8)]

        self.consts()
        if self.stage >= -1:
            self.adaln()
        with self.scope() as es0:
            self.attn_consts(es0)
            with self.scope() as es1:
                if self.stage >= 0:
                    self.make_hT(es1)
                if self.stage == 0 and self.debug:
                    self.store(self.dbg, self.hT[:], reads=[("hT", t) for t in range(NTT + 1)])
                if self.stage >= 1:
                    with self.scope() as esm:
                        side = None
                        if self.stage >= 5:
                            side = self.sample_gen(esm)
                            next(side)
                        for hg in range(2):
                            with self.scope() as es2:
                                self.attn_inproj(es2, hg)
                                if side is not None and hg == 1:
                                    next(side)
                                if self.stage >= 2:
                                    self.moba(hg, side if hg == 1 else None, final=True)
                if self.stage >= 3:
                    with self.scope() as es2:
                        self.mlstm_gates(es2)
                        for hp in range(2):
                            with self.scope() as es3:
                                self.mlstm_inproj(es3, hp)
                                self.mlstm_heads(es3, hp)
            if self.stage >= 5:
                self.sample_mlstm()
        if self.stage >= 4:
            self.post()
        if self.stage >= 1 and self.debug:
            self.store(self.dbg, self.concatT[:], reads=self.cc_keys())
        self.S.finish()
        self.S.replay()
        es.close()
        return nc

    def consts(self):
        nc, es = self.nc, self.es
        sb = lambda n, s, d: self.sb(es, n, s, d)
        self.identf = sb("identf", [128, 128], F32)
        self.identb = sb("identb", [128, 128], BF16)
        self.onesb = sb("onesb", [128, 128], BF16)
        self.onesf = sb("onesf", [128, 128], F32)
        self.Utri = sb("Utri", [128, 128], F32)
        self.concatT = sb("concatT", [128, 8, T], BF16)
        self.epsb = sb("epsb", [128, 1], F32)
        G = self.G
        G(lambda e: e.memset(self.epsb[:], EPS), writes=["epsb"])
        G(lambda e: e.memset(self.identf[:], 1.0), writes=["identf"])
        G(lambda e: e.affine_select(out=self.identf[:], in_=self.identf[:], pattern=[[-1, 128]],
                                    compare_op=ALU.is_equal, fill=0.0, base=0, channel_multiplier=1),
          reads=["identf"], writes=["identf"])
        self.V(lambda e: e.tensor_copy(out=self.identb[:], in_=self.identf[:]), reads=["identf"], writes=["identb"])
        G(lambda e: e.memset(self.onesb[:], 1.0), writes=["onesb"])
        G(lambda e: e.memset(self.onesf[:], 1.0), writes=["onesf"])
        G(lambda e: e.memset(self.Utri[:], 1.0), writes=["Utri"])
        G(lambda e: e.affine_select(out=self.Utri[:], in_=self.Utri[:], pattern=[[1, 128]],
                                    compare_op=ALU.is_ge, fill=0.0, base=0, channel_multiplier=-1),
          reads=["Utri"], writes=["Utri"])

    def attn_consts(self, es0):
        sb = lambda n, s, d: self.sb(es0, n, s, d)
        G = self.G
        self.cb = sb("cb", [128, 4, 512], BF16)
        self.aq_s = sb("aq_s", [NS, 512], F32)
        self.ak_s = sb("ak_s", [NS, 512], F32)
        self.av_s = sb("av_s", [NS, 512], F32)
        self.msamp = sb("msamp", [NS, 2056], F32)
        self.blkind = sb("blkind", [8, S], BF16)
        with self.scope() as es_scr:
            scr = self.sb(es_scr, "cscr", [128, 4, 512], F32)
            indf = self.sb(es_scr, "indf", [8, S], F32)
            G(lambda e: e.memset(scr[:], 0.0), writes=["cscr"])
            for r in range(4):
                G(lambda e, r=r: e.affine_select(out=scr[:, r, :], in_=scr[:, r, :], pattern=[[1, 512]],
                                                 compare_op=ALU.is_ge, fill=NEG, base=-r * 128, channel_multiplier=-1),
                  reads=["cscr"], writes=["cscr"])
            self.V(lambda e: e.tensor_copy(out=self.cb[:], in_=scr[:]), reads=["cscr"], writes=["cb"])
            G(lambda e: e.memset(indf[:], 1.0), writes=["indf"])
            G(lambda e: e.affine_select(out=indf[:], in_=indf[:], pattern=[[1, S]], compare_op=ALU.is_ge,
                                        fill=0.0, base=0, channel_multiplier=-256), reads=["indf"], writes=["indf"])
            G(lambda e: e.affine_select(out=indf[:], in_=indf[:], pattern=[[-1, S]], compare_op=ALU.is_ge,
                                        fill=0.0, base=255, channel_multiplier=256), reads=["indf"], writes=["indf"])
            self.V(lambda e: e.tensor_copy(out=self.blkind[:], in_=indf[:]), reads=["indf"], writes=["blkind"])

    def adaln(self):
        nc, es = self.nc, self.es
        self.modT = self.sb(es, "modT", [128, 48, 33], F32)
        self.grow = self.sb(es, "grow", [33, 2 * D], F32)
        self.gbc = self.sb(es, "gbc", [128, 2 * D], F32)
        with self.scope() as es1:
            cTf = self.sb(es1, "cTf", [128, 8, 33], F32)
            sig = self.sb(es1, "csig", [128, 8, 33], F32)
            scT = self.sb(es1, "scT", [128, 8, 33], BF16)
            badT = self.sb(es1, "badT", [128, 48], F32)
            brow = self.sb(es1, "brow", [33, 2 * D], F32)
            sel32 = self.sb(es1, "sel32", [33, 128], F32)
            wbuf = [self.sb(es1, "wada%d" % i, [128, 8, 512], BF16) for i in range(2)]
            self.load(cTf[:], self.cT, ["cTf"])
            self.load(badT[:], self.b_adaT, ["badT"])
            self.load(brow[:, 0:D], self.b_ada[:, 2 * D:3 * D].broadcast_to([33, D]), ["brow0"])
            self.load(brow[:, D:2 * D], self.b_ada[:, 5 * D:6 * D].broadcast_to([33, D]), ["brow1"])
            self.A(lambda e: e.activation(out=sig[:], in_=cTf[:], func=AF.Sigmoid), reads=["cTf"], writes=["csig"])
            self.V(lambda e: e.tensor_tensor(out=scT[:], in0=cTf[:], in1=sig[:], op=ALU.mult),
                   reads=["cTf", "csig"], writes=["scT"])
            wv = self.w_ada.rearrange("(k p) n -> p k n", p=128)
            for blk in range(12):
                wb = wbuf[blk % 2]
                wk = "wada%d" % (blk % 2)
                self.load(wb[:], wv[:, :, blk * 512:(blk + 1) * 512], [wk], queue="gpsimd")
                if blk in (4, 5, 10, 11):
                    gi = {4: 0, 5: 1, 10: 2, 11: 3}[blk]
                    pbk = ("pb", 4 + gi % 2)
                    pt = self.pb[4 + gi % 2]
                    for k in range(8):
                        self.mm(pt[0:33, :], scT[:, k, :], wb[:, k, :], k == 0, k == 7, ["scT", wk], [pbk])
                    self.V(lambda e, pt=pt, gi=gi: e.tensor_tensor(out=self.grow[:, gi * 512:(gi + 1) * 512],
                                                                  in0=pt[0:33, :], in1=brow[:, gi * 512:(gi + 1) * 512],
                                                                  op=ALU.add),
                           reads=[pbk, "brow0", "brow1"], writes=["grow"])
                else:
                    for cc in range(4):
                        c = blk * 4 + cc
                        bank = cc % 4
                        pbk = ("pb", bank)
                        pt = self.pb[bank]
                        for k in range(8):
                            self.mm(pt[:, 0:33], wb[:, k, cc * 128:(cc + 1) * 128], scT[:, k, :], k == 0, k == 7,
                                    ["scT", wk], [pbk])
                        self.A(lambda e, pt=pt, c=c: e.activation(out=self.modT[:, c, :], in_=pt[:, 0:33],
                                                                 func=AF.Identity, bias=badT[:, c:c + 1], scale=1.0),
                               reads=[pbk, "badT"], writes=["modT"])
            for c0 in (8, 32):
                self.V(lambda e, c0=c0: e.tensor_scalar_add(out=self.modT[:, c0:c0 + 8, :], in0=self.modT[:, c0:c0 + 8, :],
                                                            scalar1=1.0), reads=["modT"], writes=["modT"])
            self.V(lambda e: e.tensor_scalar_add(out=self.grow[:], in0=self.grow[:], scalar1=1.0),
                   reads=["grow"], writes=["grow"])
            self.G(lambda e: e.memset(sel32[:], 0.0), writes=["sel32"])
            self.G(lambda e: e.memset(sel32[32:33, :], 1.0), reads=["sel32"], writes=["sel32"])
            for q in range(4):
                pbk = ("pb", 6 + q % 2)
                pt = self.pb[6 + q % 2]
                self.mm(pt[:, :], sel32[:, :], self.grow[:, q * 512:(q + 1) * 512], True, True, ["sel32", "grow"], [pbk])
                self.A(lambda e, pt=pt, q=q: e.copy(out=self.gbc[:, q * 512:(q + 1) * 512], in_=pt[:, :]),
                       reads=[pbk], writes=["gbc"])

    def make_hT(self, es1):
        self.hT = self.sb(es1, "hT", [128, 8, T], BF16)
        with self.scope() as es2:
            xt = [self.sb(es2, "xt%d" % i, [128, D], F32) for i in range(2)]
            xsT = self.sb(es2, "xsT", [128, 8, NS], F32)
            for tt in range(NTT + 1):
                x_sb = xt[tt % 2]
                xk = "xt%d" % (tt % 2)
                if tt < NTT:
                    n = 128
                    self.load(x_sb[:, :], self.xp[tt * 128:(tt + 1) * 128, :], [xk])
                else:
                    n = NS
                    self.load(x_sb[0:NS, :], self.xs[:, :], [xk])
                for k in range(8):
                    bank = k % 4
                    pbk = ("pb", bank)
                    pt = self.pb[bank]
                    self.tp(pt[:, 0:n], x_sb[0:n, k * 128:(k + 1) * 128], self.identf[0:n, 0:n], [xk], [pbk])
                    if tt < NTT and k % 2 == 0:
                        self.A(lambda e, pt=pt, k=k, tt=tt: e.activation(
                            out=self.hT[:, k, tt * 128:(tt + 1) * 128], in_=pt[:, 0:128], func=AF.Identity,
                            bias=self.modT[:, k, 32:33], scale=self.modT[:, 8 + k, 32:33]),
                            reads=[pbk, "modT"], writes=[("hT", tt)])
                    elif tt < NTT:
                        self.V(lambda e, pt=pt, k=k, tt=tt: e.tensor_scalar(
                            out=self.hT[:, k, tt * 128:(tt + 1) * 128], in0=pt[:, 0:128], scalar1=self.modT[:, 8 + k, 32:33],
                            scalar2=self.modT[:, k, 32:33], op0=ALU.mult, op1=ALU.add),
                            reads=[pbk, "modT"], writes=[("hT", tt)])
                    else:
                        self.V(lambda e, pt=pt, k=k: e.tensor_tensor(out=xsT[:, k, :], in0=pt[:, 0:NS],
                                                                    in1=self.modT[:, 8 + k, 0:NS], op=ALU.mult),
                               reads=[pbk, "modT"], writes=["xsT"])
            self.V(lambda e: e.tensor_tensor(out=self.hT[:, :, S:T], in0=xsT[:, :, :], in1=self.modT[:, 0:8, 0:NS],
                                             op=ALU.add), reads=["xsT", "modT"], writes=[("hT", NTT)])

    def attn_inproj(self, es2, hg):
        self.qaug = self.sb(es2, "qaug%d" % hg, [72, 4, S], BF16)
        self.kaug = self.sb(es2, "kaug%d" % hg, [72, 4, S], BF16)
        self.av = self.sb(es2, "av%d" % hg, [128, NTT, 256], BF16)
        self.kmT = self.sb(es2, "kmT%d" % hg, [64, 4, 8], BF16)
        with self.scope() as esw:
            self._attn_inproj_body(esw, hg)

    def _attn_inproj_body(self, esw, hg):
        kmf = self.sb(esw, "kmf%d" % hg, [64, 4, 8], F32)
        wq = self.sb(esw, "wq%d" % hg, [128, 8, 256], BF16)
        wk = self.sb(esw, "wk%d" % hg, [128, 8, 256], BF16)
        wv = self.sb(esw, "wv%d" % hg, [128, 8, 256], BF16)
        stg = [self.sb(esw, "kvstg%d_%d" % (hg, i), [128, 256], F32) for i in range(4)]
        win = self.w_in.rearrange("(k p) n -> p k n", p=128)
        c0 = hg * 256
        self.load(wq[:], win[:, :, c0:c0 + 256], ["wq"], queue="gpsimd")
        self.load(wk[:], win[:, :, 512 + c0:512 + c0 + 256], ["wk"], queue="gpsimd")
        self.load(wv[:], win[:, :, 1024 + c0:1024 + c0 + 256], ["wv"], queue="gpsimd")
        skip = ()
        for h in range(4):
            if "ind" in skip:
                break
            self.load(self.kaug[64:72, h, :], self.blkind[:, :], [("kaug_ind", h)], reads=["blkind"])
        hT = self.hT
        cnt = 0
        for h in range(4):
            if "fm" in skip:
                break
            for g in range(4):
                rd = [("hT", 4 * g + j) for j in range(4)]
                for which in range(2):
                    w = wq if which == 0 else wk
                    wkey = "wq" if which == 0 else "wk"
                    bank = cnt % 4
                    cnt += 1
                    pbk = ("pb", bank)
                    pt = self.pb[bank]
                    for k in range(8):
                        self.mm(pt[0:64, :], w[:, k, h * 64:(h + 1) * 64], hT[:, k, g * 512:(g + 1) * 512],
                                k == 0, k == 7, rd + [wkey], [pbk])
                    if which == 0:
                        self.A(lambda e, pt=pt, h=h, g=g: e.mul(out=self.qaug[0:64, h, g * 512:(g + 1) * 512],
                                                                in_=pt[0:64, :], mul=0.125),
                               reads=[pbk], writes=[("qT", h, g)])
                    else:
                        self.A(lambda e, pt=pt, h=h, g=g: e.copy(out=self.kaug[0:64, h, g * 512:(g + 1) * 512],
                                                                 in_=pt[0:64, :]),
                               reads=[pbk], writes=[("kT", h, g)])
                        self.V(lambda e, pt=pt, h=h, g=g: e.tensor_reduce(
                            out=kmf[:, h, 2 * g:2 * g + 2], in_=pt[0:64, :].rearrange("p (a b) -> p a b", a=2),
                            axis=AX.X, op=ALU.add), reads=[pbk], writes=["kmf"])
        self.V(lambda e: e.tensor_copy(out=self.kmT[:], in_=kmf[:]), reads=["kmf"], writes=["kmT"])
        cnt = 0
        for tt in range(NTT + 1):
            if "tm" in skip:
                break
            if "smp" in skip and tt == NTT:
                break
            if hg == 1 and tt == NTT:
                break
            n = 128 if tt < NTT else NS
            t0 = tt * 128
            for which in range(2):
                w = wk if which == 0 else wv
                wkey = "wk" if which == 0 else "wv"
                bank = 4 + cnt % 4
                sgi = cnt % 4
                cnt += 1
                pbk = ("pb", bank)
                pt = self.pb[bank]
                for k in range(8):
                    self.mm(pt[0:n, 0:256], hT[:, k, t0:t0 + n], w[:, k, :], k == 0, k == 7, [("hT", tt), wkey], [pbk])
                if tt < NTT:
                    st = stg[sgi]
                    sk = ("kvstg", sgi)
                    self.V(lambda e, pt=pt, st=st: e.tensor_copy(out=st[:, :], in_=pt[:, 0:256]), reads=[pbk], writes=[sk])
                    dst = self.k_p if which == 0 else self.v_p
                    if "st" not in skip:
                        self.store(dst[t0:t0 + 128, c0:c0 + 256], st[:, :], [sk])
                    if which == 1:
                        self.A(lambda e, pt=pt, tt=tt: e.copy(out=self.av[:, tt, :], in_=pt[:, 0:256]),
                               reads=[pbk], writes=[("av", tt)])
                else:
                    dsts = self.ak_s if which == 0 else self.av_s
                    dk = "ak_s" if which == 0 else "av_s"
                    self.V(lambda e, pt=pt, dsts=dsts: e.tensor_copy(out=dsts[:, c0:c0 + 256], in_=pt[0:NS, 0:256]),
                           reads=[pbk], writes=[(dk, hg)])
                    dst = self.k_s if which == 0 else self.v_s
                    if "st2" not in skip:
                        self.store(dst[:, c0:c0 + 256], dsts[:, c0:c0 + 256], [(dk, hg)])
        if hg == 1:
            return
        pbk = ("pb", 4)
        pt = self.pb[4]
        for k in range(8):
            self.mm(pt[0:NS, 0:256], hT[:, k, S:T], wq[:, k, :], k == 0, k == 7, [("hT", NTT), "wq"], [pbk])
        self.V(lambda e, pt=pt: e.tensor_copy(out=self.aq_s[:, c0:c0 + 256], in_=pt[0:NS, 0:256]),
               reads=[pbk], writes=[("aq_s", hg)])
        wx = wq
        for which, (dsts, dk, dst) in enumerate(((self.aq_s, "aq_s", None), (self.ak_s, "ak_s", self.k_s),
                                                  (self.av_s, "av_s", self.v_s))):
            self.load(wx[:], win[:, :, which * 512 + 256:which * 512 + 512], ["wq"], queue="gpsimd")
            bank = 5 + which % 2
            pbk = ("pb", bank)
            pt = self.pb[bank]
            for k in range(8):
                self.mm(pt[0:NS, 0:256], hT[:, k, S:T], wx[:, k, :], k == 0, k == 7, [("hT", NTT), "wq"], [pbk])
            od = dsts[:, 256:512]
            self.V(lambda e, pt=pt, od=od: e.tensor_copy(out=od, in_=pt[0:NS, 0:256]), reads=[pbk], writes=[(dk, 1)])
            if dst is not None:
                self.store(dst[:, 256:512], od, [(dk, 1)])

    def _side_tick(self):
        self._side_cnt = getattr(self, "_side_cnt", 0) + 1
        return self._side_cnt % 4 == 0

    def cc_keys(self):
        return [("cc", c, g) for c in range(8) for g in range(5)]

    def moba(self, hg, side=None, final=True):
        pb = self.pb
        with self.scope() as es3:
            sc = self.sb(es3, "mb_sc", [128, 4, 8], F32)
            big = self.sb(es3, "mb_big", [128, 4, 8, 8], F32)
            cntt = self.sb(es3, "mb_cnt", [128, 4, 8], F32)
            bias = self.sb(es3, "mb_bias", [128, 4, 8], F32)
            bT = [self.sb(es3, "mb_bT%d" % i, [32, 128], BF16) for i in range(2)]
            pT = [self.sb(es3, "mb_pT%d" % i, [128, 512], BF16) for i in range(3)]
            rec = [self.sb(es3, "mb_rec%d" % i, [128, 512], F32) for i in range(2)]
            for qt in range(NTT):
                bq = qt // 2
                g = qt // 4
                if bq >= 4:
                    for h in range(4):
                        self.mm(pb[4][:, h * 8:(h + 1) * 8], self.qaug[0:64, h, qt * 128:(qt + 1) * 128],
                                self.kmT[0:64, h, :], True, True, [("qT", h, g), "kmT"], [("pb", 4)])
                    self.V(lambda e: e.tensor_copy(out=sc[:].rearrange("p h n -> p (h n)"), in_=pb[4][:, 0:32]),
                           reads=[("pb", 4)], writes=["mb_sc"])
                    self.V(lambda e, bq=bq: e.memset(sc[:, :, bq:8], -1e30), reads=["mb_sc"], writes=["mb_sc"])
                    self.V(lambda e: e.tensor_tensor(out=big[:], in0=sc[:].unsqueeze(2).broadcast_to([128, 4, 8, 8]),
                                                     in1=sc[:].unsqueeze(3).broadcast_to([128, 4, 8, 8]), op=ALU.is_gt),
                           reads=["mb_sc"], writes=["mb_big"])
                    self.V(lambda e: e.tensor_reduce(out=cntt[:].rearrange("p h n -> p (h n)"),
                                                     in_=big[:].rearrange("p h n m -> p (h n) m"), axis=AX.X, op=ALU.add),
                           reads=["mb_big"], writes=["mb_cnt"])
                    self.V(lambda e: e.tensor_scalar(out=bias[:], in0=cntt[:], scalar1=3.0, scalar2=NEG,
                                                     op0=ALU.is_ge, op1=ALU.mult), reads=["mb_cnt"], writes=["mb_bias"])
                else:
                    self.V(lambda e: e.memset(bias[:], 0.0), writes=["mb_bias"])
                self.V(lambda e, bq=bq: e.memset(bias[:, :, bq:bq + 1], 0.0), reads=["mb_bias"], writes=["mb_bias"])
                if bq < 7:
                    self.V(lambda e, bq=bq: e.memset(bias[:, :, bq + 1:8], NEG), reads=["mb_bias"], writes=["mb_bias"])
                self.tp(pb[5][0:32, 0:128], bias[:].rearrange("p h n -> p (h n)"), self.identf[:, :], ["mb_bias"], [("pb", 5)])
                b_t = bT[qt % 2]
                bk = ("mb_bT", qt % 2)
                self.A(lambda e, b_t=b_t: e.copy(out=b_t[:, :], in_=pb[5][0:32, 0:128]), reads=[("pb", 5)], writes=[bk])
                for h in range(4):
                    self.load(self.qaug[64:72, h, qt * 128:(qt + 1) * 128], b_t[h * 8:(h + 1) * 8, :],
                              [("qB", h, qt)], reads=[bk])
            it = 0
            pc = 0
            for h in range(4):
                for g in range(4):
                    nkt = 4 * g + 4
                    nb = 2 + (it % 2 if side is None else 0)
                    db = 6 + (it % 2 if side is None else 0)
                    it += 1
                    qreads = [("qT", h, g)] + [("qB", h, 4 * g + j) for j in range(4)]

                    def score(kt, h=h, g=g, qreads=qreads):
                        bs = kt % 2
                        diag = kt >= 4 * g
                        self.mm(pb[bs][:, :], self.kaug[0:72, h, kt * 128:(kt + 1) * 128],
                                self.qaug[0:72, h, g * 512:(g + 1) * 512], True, not diag,
                                qreads + [("kT", h, kt // 4), ("kaug_ind", h)], [("pb", bs)])
                        if diag:
                            self.mm(pb[bs][:, :], self.identb[:, :], self.cb[:, kt - 4 * g, :], False, True,
                                    ["identb", "cb"], [("pb", bs)])
                    score(0)
                    for kt in range(nkt):
                        if kt + 1 < nkt:
                            score(kt + 1)
                        if side is not None and self._side_tick():
                            next(side, None)
                        bs = kt % 2
                        p_t = pT[pc % 3]
                        pk = ("mb_pT", pc % 3)
                        pc += 1
                        self.A(lambda e, p_t=p_t, bs=bs: e.activation(out=p_t[:, :], in_=pb[bs][:, :], func=AF.Exp),
                               reads=[("pb", bs)], writes=[pk])
                        hp = h // 2
                        self.mm(pb[nb][:, :], self.av[:, kt, hp * 128:(hp + 1) * 128], p_t[:, :], kt == 0, kt == nkt - 1,
                                [("av", kt), pk], [("pb", nb)])
                        self.mm(pb[db][:, :], self.onesb[:, :], p_t[:, :], kt == 0, kt == nkt - 1,
                                ["onesb", pk], [("pb", db)])
                    r0 = (h % 2) * 64
                    rc = rec[it % 2]
                    rk = ("mb_rec", it % 2)
                    self.V(lambda e, rc=rc, db=db, r0=r0: e.reciprocal(out=rc[r0:r0 + 64, :], in_=pb[db][r0:r0 + 64, :]),
                           reads=[("pb", db)], writes=[rk])
                    c = hg * 2 + h // 2
                    self.V(lambda e, rc=rc, nb=nb, r0=r0, c=c, g=g: e.tensor_tensor(
                        out=self.concatT[r0:r0 + 64, c, g * 512:(g + 1) * 512], in0=pb[nb][r0:r0 + 64, :],
                        in1=rc[r0:r0 + 64, :], op=ALU.mult), reads=[("pb", nb), rk], writes=[("cc", c, g)])
            if side is not None and final:
                for _ in side:
                    pass

    def mlstm_gates(self, es2):
        pb = self.pb
        hT = self.hT
        sb = lambda n, s, d: self.sb(es2, n, s, d)
        self.tri01 = sb("tri01", [128, 4, 512], F32)
        self.utok = utok = sb("ml_utok", [128, NTT, 4], F32)
        self.Brow = Brow = sb("ml_Brow", [4, S], F32)
        self.gtok = gtok = sb("ml_gtok", [128, NTT, 4], F32)
        self.gbf = gbf = sb("ml_gbf", [128, NTT, 4], BF16)
        self.selh = selh = sb("ml_selh", [4, 4, 128], F32)
        self.meanm = meanm = sb("ml_meanm", [128, 128], F32)
        self.mgTs = mgT = sb("ml_mgT", [128, 4], F32)
        V, A, G = self.V, self.A, self.G
        G(lambda e: e.memset(self.tri01[:], 1.0), writes=["tri01"])
        for r in range(4):
            G(lambda e, r=r: e.affine_select(out=self.tri01[:, r, :], in_=self.tri01[:, r, :], pattern=[[1, 512]],
                                             compare_op=ALU.is_ge, fill=0.0, base=-r * 128, channel_multiplier=-1),
              reads=["tri01"], writes=["tri01"])
        self.load(mgT[:], self.mgT, ["ml_mgT"])
        G(lambda e: e.memset(meanm[:], 1.0 / 128.0), writes=["ml_meanm"])
        V(lambda e: e.tensor_copy(out=selh[:], in_=self.identf[0:4, 0:4].unsqueeze(2).broadcast_to([4, 4, 128])),
          reads=["identf"], writes=["ml_selh"])
        win = self.w_in.rearrange("(k p) n -> p k n", p=128)
        with self.scope() as es3:
            sb3 = lambda n, s, d: self.sb(es3, n, s, d)
            wg = sb3("mwg", [128, 8, 8], BF16)
            gt_tok = sb3("gt_tok", [128, NTT, 8], F32)
            bif = sb3("ml_bif", [128, 8], F32)
            ig = sb3("ml_ig", [128, NTT, 4], F32)
            lf = sb3("ml_lf", [128, NTT, 4], F32)
            tmp = sb3("ml_tmp", [128, NTT, 4], F32)
            Btok = sb3("ml_Btok", [128, NTT, 4], F32)
            urow = sb3("ml_urow", [4, S], F32)
            Mx = sb3("ml_Mx", [4, 1], F32)
            MxB = sb3("ml_MxB", [4, 128], F32)
            mfin = sb3("ml_mfin", [4, 1], F32)
            self.load(wg[:], win[:, :, 3584:3592], ["mwg"], queue="gpsimd")
            self.load(bif[:], self.b_if.broadcast_to([128, 8]), ["ml_bif"])
            for tt in range(NTT + 1):
                n = 128 if tt < NTT else NS
                t0 = tt * 128
                bank = 4 + tt % 4
                for k in range(8):
                    self.mm(pb[bank][0:n, 0:8], hT[:, k, t0:t0 + n], wg[:, k, :], k == 0, k == 7, [("hT", tt), "mwg"],
                            [("pb", bank)])
                if tt < NTT:
                    V(lambda e, bank=bank, tt=tt: e.tensor_copy(out=gt_tok[:, tt, :], in_=pb[bank][:, 0:8]),
                      reads=[("pb", bank)], writes=["gt_tok"])
                else:
                    V(lambda e, bank=bank: e.tensor_copy(out=self.msamp[:, 2048:2056], in_=pb[bank][0:NS, 0:8]),
                      reads=[("pb", bank)], writes=[("m_s", "g")])
            V(lambda e: e.tensor_tensor(out=ig[:], in0=gt_tok[:, :, 0:4], in1=bif[:, 0:4].unsqueeze(1).broadcast_to([128, NTT, 4]),
                                        op=ALU.add), reads=["gt_tok", "ml_bif"], writes=["ml_ig"])
            V(lambda e: e.tensor_tensor(out=tmp[:], in0=gt_tok[:, :, 4:8], in1=bif[:, 4:8].unsqueeze(1).broadcast_to([128, NTT, 4]),
                                        op=ALU.add), reads=["gt_tok", "ml_bif"], writes=["ml_tmp"])
            A(lambda e: e.activation(out=tmp[:], in_=tmp[:], func=AF.Exp, scale=-1.0), reads=["ml_tmp"], writes=["ml_tmp"])
            V(lambda e: e.tensor_scalar_add(out=tmp[:], in0=tmp[:], scalar1=1.0), reads=["ml_tmp"], writes=["ml_tmp"])
            A(lambda e: e.activation(out=tmp[:], in_=tmp[:], func=AF.Ln), reads=["ml_tmp"], writes=["ml_tmp"])
            V(lambda e: e.tensor_scalar_mul(out=lf[:], in0=tmp[:], scalar1=-1.0), reads=["ml_tmp"], writes=["ml_lf"])
            for tt in range(NTT):
                bank = 4 + tt % 2
                for j in range(tt + 1):
                    lhs = self.Utri if j == tt else self.onesf
                    self.mm(pb[bank][:, 0:4], lhs[:, :], lf[:, j, :], j == 0, j == tt, ["Utri", "onesf", "ml_lf"], [("pb", bank)])
                V(lambda e, bank=bank, tt=tt: e.tensor_copy(out=Btok[:, tt, :], in_=pb[bank][:, 0:4]),
                  reads=[("pb", bank)], writes=["ml_Btok"])
            V(lambda e: e.tensor_tensor(out=utok[:], in0=ig[:], in1=Btok[:], op=ALU.subtract),
              reads=["ml_ig", "ml_Btok"], writes=["ml_utok"])
            for tt in range(NTT):
                bank = 6 + tt % 2
                self.tp(pb[bank][0:4, 0:128], Btok[:, tt, :], self.identf[:, :], ["ml_Btok"], [("pb", bank)])
                self.tp(pb[bank][0:4, 128:256], utok[:, tt, :], self.identf[:, :], ["ml_utok"], [("pb", bank)])
                A(lambda e, bank=bank, tt=tt: e.copy(out=Brow[:, tt * 128:(tt + 1) * 128], in_=pb[bank][0:4, 0:128]),
                  reads=[("pb", bank)], writes=["ml_Brow"])
                A(lambda e, bank=bank, tt=tt: e.copy(out=urow[:, tt * 128:(tt + 1) * 128], in_=pb[bank][0:4, 128:256]),
                  reads=[("pb", bank)], writes=["ml_urow"])
            V(lambda e: e.tensor_reduce(out=Mx[:], in_=urow[:], axis=AX.X, op=ALU.max), reads=["ml_urow"], writes=["ml_Mx"])
            V(lambda e: e.tensor_scalar_max(out=Mx[:], in0=Mx[:], scalar1=0.0), reads=["ml_Mx"], writes=["ml_Mx"])
            V(lambda e: e.tensor_tensor(out=mfin[:], in0=Mx[:], in1=Brow[:, S - 1:S], op=ALU.add),
              reads=["ml_Mx", "ml_Brow"], writes=["ml_mfin"])
            self.store(self.m_p, mfin[:], ["ml_mfin"])
            V(lambda e: e.tensor_copy(out=MxB[:], in_=Mx[:, 0:1].broadcast_to([4, 128])), reads=["ml_Mx"], writes=["ml_MxB"])
            self.mm(pb[4][:, 0:4], MxB[:, :], self.identf[0:4, 0:4], True, True, ["ml_MxB", "identf"], [("pb", 4)])
            V(lambda e: e.tensor_tensor(out=gtok[:], in0=utok[:], in1=pb[4][:, 0:4].unsqueeze(1).broadcast_to([128, NTT, 4]),
                                        op=ALU.subtract), reads=["ml_utok", ("pb", 4)], writes=["ml_gtok"])
            A(lambda e: e.activation(out=gtok[:], in_=gtok[:], func=AF.Exp), reads=["ml_gtok"], writes=["ml_gtok"])
            V(lambda e: e.tensor_copy(out=gbf[:], in_=gtok[:]), reads=["ml_gtok"], writes=["ml_gbf"])

    def mlstm_inproj(self, es2, hp):
        pb = self.pb
        hT = self.hT
        self.mqT = self.sb(es2, "mqT", [128, 2, S], BF16)
        self.mkT = self.sb(es2, "mkT", [128, 2, S], BF16)
        self.sgT = self.sb(es2, "sgT", [128, 2, S], BF16)
        self.mk_tok = self.sb(es2, "mk_tok", [128, NTT, 256], BF16)
        self.mv_tok = self.sb(es2, "mv_tok", [128, NTT, 256], BF16)
        win = self.w_in.rearrange("(k p) n -> p k n", p=128)
        KS = 128.0 ** -0.5
        with self.scope() as es3:
            wb = [self.sb(es3, "mw%d" % i, [128, 8, 256], BF16) for i in range(2)]
            cnt = 0
            for bi, (name, c0) in enumerate((("mq", 1536), ("mk", 2048), ("mv", 2560), ("mo", 3072))):
                w = wb[bi % 2]
                wk = "mw%d" % (bi % 2)
                c0 = c0 + hp * 256
                self.load(w[:], win[:, :, c0:c0 + 256], [wk], queue="gpsimd")
                if name != "mv":
                    for hl in range(2):
                        for g in range(4):
                            bank = cnt % 4
                            cnt += 1
                            rd = [("hT", 4 * g + j) for j in range(4)] + [wk]
                            for k in range(8):
                                self.mm(pb[bank][:, :], w[:, k, hl * 128:(hl + 1) * 128], hT[:, k, g * 512:(g + 1) * 512],
                                        k == 0, k == 7, rd, [("pb", bank)])
                            sl = slice(g * 512, (g + 1) * 512)
                            if name == "mq":
                                self.A(lambda e, bank=bank, hl=hl, sl=sl: e.copy(out=self.mqT[:, hl, sl], in_=pb[bank][:, :]),
                                       reads=[("pb", bank)], writes=[("mqT", hl, g)])
                            elif name == "mk":
                                self.A(lambda e, bank=bank, hl=hl, sl=sl: e.mul(out=self.mkT[:, hl, sl], in_=pb[bank][:, :], mul=KS),
                                       reads=[("pb", bank)], writes=[("mkT", hl, g)])
                            else:
                                self.A(lambda e, bank=bank, hl=hl, sl=sl: e.activation(out=self.sgT[:, hl, sl], in_=pb[bank][:, :],
                                                                                      func=AF.Sigmoid),
                                       reads=[("pb", bank)], writes=[("sgT", hl, g)])
                for tt in range(NTT + 1):
                    if name in ("mq", "mo") and tt < NTT:
                        continue
                    n = 128 if tt < NTT else NS
                    t0 = tt * 128
                    bank = 4 + cnt % 4
                    cnt += 1
                    for k in range(8):
                        self.mm(pb[bank][0:n, 0:256], hT[:, k, t0:t0 + n], w[:, k, :], k == 0, k == 7, [("hT", tt), wk],
                                [("pb", bank)])
                    if tt < NTT:
                        if name == "mk":
                            self.V(lambda e, bank=bank, tt=tt: e.tensor_scalar_mul(out=self.mk_tok[:, tt, :], in0=pb[bank][:, 0:256],
                                                                                  scalar1=KS),
                                   reads=[("pb", bank)], writes=[("mk_tok", tt)])
                        else:
                            self.V(lambda e, bank=bank, tt=tt: e.tensor_copy(out=self.mv_tok[:, tt, :], in_=pb[bank][:, 0:256]),
                                   reads=[("pb", bank)], writes=[("mv_tok", tt)])
                    else:
                        o0 = {"mq": 0, "mk": 512, "mv": 1024, "mo": 1536}[name] + hp * 256
                        if name == "mk":
                            self.V(lambda e, bank=bank, o0=o0: e.tensor_scalar_mul(out=self.msamp[:, o0:o0 + 256],
                                                                                  in0=pb[bank][0:NS, 0:256], scalar1=KS),
                                   reads=[("pb", bank)], writes=[("m_s", name, hp)])
                        else:
                            self.V(lambda e, bank=bank, o0=o0: e.tensor_copy(out=self.msamp[:, o0:o0 + 256], in_=pb[bank][0:NS, 0:256]),
                                   reads=[("pb", bank)], writes=[("m_s", name, hp)])

    def mlstm_heads(self, es2, hp):
        pb = self.pb
        sb = lambda n, s, d: self.sb(es2, n, s, d)
        utok, Brow, gtok, gbf, selh, meanm, mgT = self.utok, self.Brow, self.gtok, self.gbf, self.selh, self.meanm, self.mgTs
        gv = [sb("ml_gv%d" % i, [128, 128], BF16) for i in range(2)]
        cst = sb("ml_cst", [128, 128], F32)
        nst = sb("ml_nst", [1, 128], F32)
        bbc = [sb("ml_bbc%d" % i, [128, 512], F32) for i in range(2)]
        wT = [sb("ml_wT%d" % i, [128, 512], F32) for i in range(3)]
        aT = [sb("ml_aT%d" % i, [128, 512], BF16) for i in range(3)]
        rd = sb("ml_rd", [128, 512], F32)
        hs = sb("ml_hs", [128, 512], F32)
        xc = sb("ml_xc", [128, 512], F32)
        sq = sb("ml_sq", [128, 512], F32)
        V, A, G = self.V, self.A, self.G
        gi = 0
        for hl in range(2):
            h = 2 * hp + hl
            hs_ = slice(hl * 128, (hl + 1) * 128)
            for tt in range(NTT):
                g_v = gv[gi % 2]
                gk = ("ml_gv", gi % 2)
                gi += 1
                V(lambda e, g_v=g_v, h=h, tt=tt, hs_=hs_: e.tensor_scalar_mul(out=g_v[:, :], in0=self.mv_tok[:, tt, hs_],
                                                                             scalar1=gtok[:, tt, h:h + 1]),
                  reads=[("mv_tok", tt), "ml_gtok"], writes=[gk])
                self.mm(pb[4][:, 0:128], g_v[:, :], self.mk_tok[:, tt, hs_], tt == 0, tt == NTT - 1,
                        [gk, ("mk_tok", tt)], [("pb", 4)])
                self.mm(pb[5][0:1, 0:128], gbf[:, tt, h:h + 1], self.mk_tok[:, tt, hs_], tt == 0, tt == NTT - 1,
                        ["ml_gbf", ("mk_tok", tt)], [("pb", 5)])
            V(lambda e: e.tensor_copy(out=cst[:], in_=pb[4][:, 0:128]), reads=[("pb", 4)], writes=["ml_cst"])
            self.store(self.C_p[h], cst[:], ["ml_cst"])
            V(lambda e: e.tensor_copy(out=nst[:], in_=pb[5][0:1, 0:128]), reads=[("pb", 5)], writes=["ml_nst"])
            self.store(self.n_p[h:h + 1, :], nst[:], ["ml_nst"])
        it = 0
        pc = 0
        pending = iter(())
        for hl in range(2):
            h = 2 * hp + hl
            hs_ = slice(hl * 128, (hl + 1) * 128)
            for g in range(4):
                nkt = 4 * g + 4
                nb = 2 + it % 2
                db = 6 + it % 2
                b_c = bbc[it % 2]
                bck = ("ml_bbc", it % 2)
                it += 1
                sl = slice(g * 512, (g + 1) * 512)
                self.mm(pb[4][:, :], selh[0:4, h, :], Brow[0:4, sl], True, True, ["ml_selh", "ml_Brow"], [("pb", 4)])
                A(lambda e, b_c=b_c: e.copy(out=b_c[:, :], in_=pb[4][:, :]), reads=[("pb", 4)], writes=[bck])

                def score(kt, hl=hl, g=g, sl=sl):
                    bs = kt % 2
                    self.mm(pb[bs][:, :], self.mkT[:, hl, kt * 128:(kt + 1) * 128], self.mqT[:, hl, sl], True, True,
                            [("mkT", hl, kt // 4), ("mqT", hl, g)], [("pb", bs)])
                score(0)
                for kt in range(nkt):
                    if kt + 1 < nkt:
                        score(kt + 1)
                    bs = kt % 2
                    w_t = wT[pc % 3]
                    a_t = aT[pc % 3]
                    wk = ("ml_wT", pc % 3)
                    ak = ("ml_aT", pc % 3)
                    pc += 1
                    A(lambda e, w_t=w_t, b_c=b_c, kt=kt, h=h: e.activation(out=w_t[:, :], in_=b_c[:, :], func=AF.Exp,
                                                                         bias=utok[:, kt, h:h + 1], scale=1.0),
                      reads=[bck, "ml_utok"], writes=[wk])
                    if kt >= 4 * g:
                        G(lambda e, w_t=w_t, r=kt - 4 * g: e.tensor_tensor(out=w_t[:, :], in0=w_t[:, :], in1=self.tri01[:, r, :],
                                                                         op=ALU.mult), reads=[wk, "tri01"], writes=[wk])
                    V(lambda e, a_t=a_t, w_t=w_t, bs=bs: e.tensor_tensor(out=a_t[:, :], in0=pb[bs][:, :], in1=w_t[:, :], op=ALU.mult),
                      reads=[("pb", bs), wk], writes=[ak])
                    self.mm(pb[nb][:, :], self.mv_tok[:, kt, hs_], a_t[:, :], kt == 0, kt == nkt - 1,
                            [("mv_tok", kt), ak], [("pb", nb)])
                    self.mm(pb[db][:, :], self.onesb[:, :], a_t[:, :], kt == 0, kt == nkt - 1, ["onesb", ak], [("pb", db)])
                    for _ in range(3):
                        next(pending, None)
                for _ in pending:
                    pass
                pending = self._ml_epilogue(h, hl, g, nb, db, sl, rd, hs, xc, sq, meanm, mgT)
        for _ in pending:
            pass

    def _ml_epilogue(self, h, hl, g, nb, db, sl, rd, hs, xc, sq, meanm, mgT):
        pb = self.pb
        V, A = self.V, self.A
        V(lambda e: e.tensor_scalar_mul(out=rd[:], in0=pb[db][:, :], scalar1=-1.0), reads=[("pb", db)], writes=["ml_rd"])
        yield
        V(lambda e: e.scalar_tensor_tensor(out=rd[:], in0=pb[db][:, :], scalar=1.0, in1=rd[:], op0=ALU.max, op1=ALU.max),
          reads=[("pb", db), "ml_rd"], writes=["ml_rd"])
        yield
        A(lambda e: e.activation(out=rd[:], in_=rd[:], func=AF.Ln), reads=["ml_rd"], writes=["ml_rd"])
        yield
        A(lambda e: e.activation(out=rd[:], in_=rd[:], func=AF.Exp, scale=-1.0), reads=["ml_rd"], writes=["ml_rd"])
        yield
        V(lambda e: e.tensor_tensor(out=hs[:], in0=pb[nb][:, :], in1=rd[:], op=ALU.mult),
          reads=[("pb", nb), "ml_rd"], writes=["ml_hs"])
        yield
        self.mm(pb[5][:, :], meanm[:, :], hs[:, :], True, True, ["ml_meanm", "ml_hs"], [("pb", 5)])
        yield
        V(lambda e: e.tensor_tensor(out=xc[:], in0=hs[:], in1=pb[5][:, :], op=ALU.subtract),
          reads=["ml_hs", ("pb", 5)], writes=["ml_xc"])
        yield
        A(lambda e: e.activation(out=sq[:], in_=xc[:], func=AF.Square), reads=["ml_xc"], writes=["ml_sq"])
        yield
        self.mm(pb[5][:, :], meanm[:, :], sq[:, :], True, True, ["ml_meanm", "ml_sq"], [("pb", 5)])
        yield
        A(lambda e: e.activation(out=sq[:], in_=pb[5][:, :], func=AF.Ln, bias=self.epsb[:, 0:1], scale=1.0),
          reads=[("pb", 5), "epsb"], writes=["ml_sq"])
        yield
        A(lambda e: e.activation(out=sq[:], in_=sq[:], func=AF.Exp, scale=-0.5), reads=["ml_sq"], writes=["ml_sq"])
        yield
        V(lambda e: e.tensor_tensor(out=xc[:], in0=xc[:], in1=sq[:], op=ALU.mult), reads=["ml_xc", "ml_sq"], writes=["ml_xc"])
        yield
        V(lambda e: e.scalar_tensor_tensor(out=self.concatT[:, 4 + h, sl], in0=xc[:], scalar=mgT[:, h:h + 1],
                                           in1=self.sgT[:, hl, sl], op0=ALU.mult, op1=ALU.mult),
          reads=["ml_xc", "ml_mgT", ("sgT", hl, g)], writes=[("cc", 4 + h, g)])

    def sample_gen(self, es1):
        pb = self.pb
        V, A, G = self.V, self.A, self.G
        sb = lambda n, s, d: self.sb(es1, n, s, d)
        ohc = sb("s_ohc", [128, 16, 16], F32)
        bmask = sb("s_bmask", [8, 512], F32)
        idf = sb("s_idf", [128, 256], F32)
        idx = sb("s_idx", [128, 256], I32)
        iop = sb("s_iop", [128, 1], F32)
        s_all = sb("s_all", [128, 16, 16, 8], F32)
        kv = [sb("s_kv%d" % i, [128, 512], BF16) for i in range(5)]
        prod = [sb("s_prod%d" % i, [128, 512], BF16) for i in range(2)]
        qbs = [sb("s_qbs%d" % i, [128, 512], BF16) for i in range(2)]
        oh = sb("s_oh", [16, 16, 128], F32)
        sself = sb("s_sself", [NS, 8], F32)
        eself = sb("s_eself", [NS, 8], F32)
        tk = sb("s_tk", [NS, 512], F32)
        bsum = sb("s_bsum", [NS, 16, 8], F32)
        blk = sb("s_blk", [NS, 8, 8], F32)
        big = sb("s_big", [NS, 8, 8, 8], F32)
        cnt = sb("s_cnt", [NS, 8, 8], F32)
        selb = sb("s_selb", [NS, 16, 8], F32)
        et = sb("s_et", [128, 128], F32)
        ee = [sb("s_ee%d" % i, [128, 16, 8], BF16) for i in range(2)]
        om = sb("s_om", [8, 512], F32)
        dcol = sb("s_dcol", [8, 1], F32)
        dd = sb("s_dd", [8, 8], F32)
        den = sb("s_den", [NS, 8], F32)
        att = sb("s_att", [NS, 512], F32)
        AQ = [("aq_s", 0), ("aq_s", 1)]
        yield
        V(lambda e: e.tensor_copy(out=oh[:], in_=self.identf[0:16, 0:16].unsqueeze(2).broadcast_to([16, 16, 128])),
          reads=["identf"], writes=["s_oh"])
        G(lambda e: e.memset(ohc[:], 1.0), writes=["s_ohc"])
        G(lambda e: e.affine_select(out=ohc[:], in_=ohc[:], pattern=[[1, 16], [-1, 16]], compare_op=ALU.is_equal,
                                    fill=0.0, base=0, channel_multiplier=0), reads=["s_ohc"], writes=["s_ohc"])
        G(lambda e: e.memset(bmask[:], 1.0), writes=["s_bmask"])
        G(lambda e: e.affine_select(out=bmask[:], in_=bmask[:], pattern=[[1, 512]], compare_op=ALU.is_ge,
                                    fill=0.0, base=0, channel_multiplier=-64), reads=["s_bmask"], writes=["s_bmask"])
        G(lambda e: e.affine_select(out=bmask[:], in_=bmask[:], pattern=[[-1, 512]], compare_op=ALU.is_ge,
                                    fill=0.0, base=63, channel_multiplier=64), reads=["s_bmask"], writes=["s_bmask"])
        self.load(idx[:], self.pt.broadcast_to([128, 256]), ["s_idx"])
        G(lambda e: e.iota(iop[:], pattern=[[0, 1]], base=0, channel_multiplier=1, allow_small_or_imprecise_dtypes=True),
          writes=["s_iop"])
        V(lambda e: e.tensor_copy(out=idf[:], in_=idx[:]), reads=["s_idx"], writes=["s_idf"])
        V(lambda e: e.tensor_scalar(out=idf[:], in0=idf[:], scalar1=128.0, scalar2=iop[:, 0:1], op0=ALU.mult, op1=ALU.add),
          reads=["s_idf", "s_iop"], writes=["s_idf"])
        V(lambda e: e.tensor_copy(out=idx[:], in_=idf[:]), reads=["s_idf", "s_idx"], writes=["s_idx"])
        V(lambda e: e.tensor_tensor(out=tk[:], in0=self.aq_s[:], in1=self.ak_s[:], op=ALU.mult),
          reads=AQ + [("ak_s", 0), ("ak_s", 1)], writes=["s_tk"])
        V(lambda e: e.tensor_reduce(out=sself[:], in_=tk[:].rearrange("p (h d) -> p h d", h=8), axis=AX.X, op=ALU.add),
          reads=["s_tk"], writes=["s_sself"])
        A(lambda e: e.activation(out=eself[:], in_=sself[:], func=AF.Exp, scale=0.125), reads=["s_sself"], writes=["s_eself"])
        yield
        DEPTH = 4
        NKV = len(kv)
        pages = [(b, j) for b in range(NS) for j in range(16)]
        qinfo = {}

        def k_gather(t):
            b, j = pages[t]
            col = b * 16 + j
            k_t = kv[t % NKV]
            self.S.dma("gpsimd", lambda e, k_t=k_t, col=col: e.indirect_dma_start(
                out=k_t[:], out_offset=None, in_=self.cache_k,
                in_offset=bass.IndirectOffsetOnAxis(ap=idx[:, col:col + 1], axis=0)), reads=["s_idx"], writes=[("s_kv", t % NKV)])

        def k_compute(t):
            b, j = pages[t]
            if j == 0:
                q_s = qbs[b % 2]
                qsk = ("s_qbs", b % 2)
                self.mm(pb[4][:, :], oh[0:16, b, :], self.aq_s[:, :], True, True, ["s_oh"] + AQ, [("pb", 4)])
                A(lambda e, q_s=q_s: e.copy(out=q_s[:, :], in_=pb[4][:, :]), reads=[("pb", 4)], writes=[qsk])
                qinfo[b] = (q_s, qsk)
            q_s, qsk = qinfo[b]
            k_t = kv[t % NKV]
            kk = ("s_kv", t % NKV)
            p_t = prod[t % 2]
            pk = ("s_prod", t % 2)
            V(lambda e, p_t=p_t, k_t=k_t, q_s=q_s: e.tensor_tensor(out=p_t[:], in0=k_t[:], in1=q_s[:, :], op=ALU.mult),
              reads=[kk, qsk], writes=[pk])
            so = s_all[:, b, j, :]
            V(lambda e, so=so, p_t=p_t: e.tensor_reduce(out=so, in_=p_t[:].rearrange("p (h d) -> p h d", h=8),
                                                        axis=AX.X, op=ALU.add), reads=[pk], writes=["s_all"])

        for t in range(len(pages) + DEPTH):
            if t < len(pages):
                k_gather(t)
            if t >= DEPTH:
                k_compute(t - DEPTH)
                if (t - DEPTH) % 16 == 15:
                    yield
        for b in range(NS):
            self.mm(pb[4][0:NS, 0:128], ohc[:, b, :], s_all[:, b, :, :].rearrange("p j h -> p (j h)"), b == 0, b == NS - 1,
                    ["s_ohc", "s_all"], [("pb", 4)])
        V(lambda e: e.tensor_copy(out=bsum[:].rearrange("p j h -> p (j h)"), in_=pb[4][0:NS, 0:128]),
          reads=[("pb", 4)], writes=["s_bsum"])
        b4 = bsum[:].rearrange("p (n t) h -> p n t h", t=2)
        V(lambda e: e.tensor_tensor(out=blk[:], in0=b4[:, :, 0, :], in1=b4[:, :, 1, :], op=ALU.add),
          reads=["s_bsum"], writes=["s_blk"])
        bhn = blk[:].rearrange("p n h -> p h n")
        V(lambda e: e.tensor_tensor(out=big[:], in0=bhn.unsqueeze(2).broadcast_to([NS, 8, 8, 8]),
                                    in1=bhn.unsqueeze(3).broadcast_to([NS, 8, 8, 8]), op=ALU.is_gt),
          reads=["s_blk"], writes=["s_big"])
        V(lambda e: e.tensor_reduce(out=cnt[:].rearrange("p h n -> p (h n)"), in_=big[:].rearrange("p h n m -> p (h n) m"),
                                    axis=AX.X, op=ALU.add), reads=["s_big"], writes=["s_cnt"])
        V(lambda e: e.tensor_scalar(out=cnt[:], in0=cnt[:], scalar1=3.0, scalar2=NEG, op0=ALU.is_ge, op1=ALU.mult),
          reads=["s_cnt"], writes=["s_cnt"])
        cnh = cnt[:].rearrange("p h n -> p n h")
        for t in range(2):
            so = selb[:].rearrange("p (n t) h -> p n t h", t=2)[:, :, t, :]
            V(lambda e, so=so: e.tensor_copy(out=so, in_=cnh), reads=["s_cnt"], writes=["s_selb"])
        yield
        einfo = {}

        def v_gather(t):
            b, j = pages[t]
            col = b * 16 + j
            v_t = kv[t % NKV]
            self.S.dma("gpsimd", lambda e, v_t=v_t, col=col: e.indirect_dma_start(
                out=v_t[:], out_offset=None, in_=self.cache_v,
                in_offset=bass.IndirectOffsetOnAxis(ap=idx[:, col:col + 1], axis=0)), reads=["s_idx"], writes=[("s_kv", t % NKV)])

        def v_compute(t):
            b, j = pages[t]
            if j == 0:
                self.mm(pb[4][:, 128:256], oh[0:16, b, :], selb[:].rearrange("p j h -> p (j h)"), True, True,
                        ["s_oh", "s_selb"], [("pb", 4)])
                e_t = ee[b % 2]
                ek = ("s_ee", b % 2)
                ef = e_t[:].rearrange("p j h -> p (j h)")
                sa = s_all[:, b, :, :].rearrange("p j h -> p (j h)")
                V(lambda e, sa=sa: e.scalar_tensor_tensor(out=et[:], in0=sa, scalar=0.125, in1=pb[4][:, 128:256],
                                                          op0=ALU.mult, op1=ALU.add),
                  reads=["s_all", ("pb", 4)], writes=["s_et"])
                A(lambda e, ef=ef: e.activation(out=ef, in_=et[:], func=AF.Exp), reads=["s_et"], writes=[ek])
                einfo[b] = (e_t, ek)
                for jj in range(16):
                    self.mm(pb[4][0:8, 256:257], e_t[:, jj, :], self.onesb[:, 0:1], jj == 0, jj == 15, [ek, "onesb"], [("pb", 4)])
            e_t, ek = einfo[b]
            v_t = kv[t % NKV]
            vk = ("s_kv", t % NKV)
            self.mm(pb[5][0:8, :], e_t[:, j, :], v_t[:, :], j == 0, j == 15, [ek, vk], [("pb", 5)])
            if j == 15:
                V(lambda e: e.tensor_tensor(out=om[:], in0=pb[5][0:8, :], in1=bmask[:], op=ALU.mult),
                  reads=[("pb", 5), "s_bmask"], writes=["s_om"])
                V(lambda e: e.tensor_copy(out=dcol[:], in_=pb[4][0:8, 256:257]), reads=[("pb", 4)], writes=["s_dcol"])
                V(lambda e: e.tensor_scalar_mul(out=dd[:], in0=self.identf[0:8, 0:8], scalar1=dcol[:, 0:1]),
                  reads=["s_dcol", "identf"], writes=["s_dd"])
                self.mm(pb[3][0:NS, :], ohc[0:8, b, :], om[:, :], b == 0, b == NS - 1, ["s_ohc", "s_om"], [("pb", 3)])
                self.mm(pb[7][0:NS, 0:8], ohc[0:8, b, :], dd[:, :], b == 0, b == NS - 1, ["s_ohc", "s_dd"], [("pb", 7)])

        for t in range(len(pages) + DEPTH):
            if t < len(pages):
                v_gather(t)
            if t >= DEPTH:
                v_compute(t - DEPTH)
                if (t - DEPTH) % 16 == 15:
                    yield
        V(lambda e: e.tensor_tensor(out=den[:], in0=pb[7][0:NS, 0:8], in1=eself[:], op=ALU.add),
          reads=[("pb", 7), "s_eself"], writes=["s_den"])
        V(lambda e: e.reciprocal(out=den[:], in_=den[:]), reads=["s_den"], writes=["s_den"])
        V(lambda e: e.tensor_tensor(out=tk[:].rearrange("p (h d) -> p h d", h=8),
                                    in0=self.av_s[:].rearrange("p (h d) -> p h d", h=8),
                                    in1=eself[:].unsqueeze(2).broadcast_to([NS, 8, 64]), op=ALU.mult),
          reads=[("av_s", 0), ("av_s", 1), "s_eself", "s_sself"], writes=["s_tk"])
        V(lambda e: e.tensor_tensor(out=tk[:], in0=tk[:], in1=pb[3][0:NS, :], op=ALU.add),
          reads=["s_tk", ("pb", 3)], writes=["s_tk"])
        V(lambda e: e.tensor_tensor(out=att[:].rearrange("p (h d) -> p h d", h=8),
                                    in0=tk[:].rearrange("p (h d) -> p h d", h=8),
                                    in1=den[:].unsqueeze(2).broadcast_to([NS, 8, 64]), op=ALU.mult),
          reads=["s_tk", "s_den"], writes=["s_att"])
        for c in range(4):
            bank = 4 + c % 2
            self.tp(pb[bank][:, 0:NS], att[:, c * 128:(c + 1) * 128], self.identf[0:NS, 0:NS], ["s_att"], [("pb", bank)])
            A(lambda e, bank=bank, c=c: e.copy(out=self.concatT[:, c, S:T], in_=pb[bank][:, 0:NS]),
              reads=[("pb", bank)], writes=[("cc", c, 4)])

    def sample_mlstm(self):
        pb = self.pb
        V, A, G = self.V, self.A, self.G
        ms = self.msamp
        with self.scope() as es1:
            sb = lambda n, s, d: self.sb(es1, n, s, d)
            oh = sb("m_oh", [16, 16, 128], F32)
            bif = sb("m_bif", [NS, 8], F32)
            m0 = sb("m_m0", [NS, 4], F32)
            n0 = sb("m_n0", [NS, 512], F32)
            ig = sb("m_ig", [NS, 4], F32)
            lf = sb("m_lf", [NS, 4], F32)
            mi_ = sb("m_mi", [NS, 4], F32)
            mt = sb("m_mt", [NS, 4], F32)
            w = sb("m_w", [NS, 4], F32)
            gi_ = sb("m_gi", [NS, 4], F32)
            emt = sb("m_emt", [NS, 4], F32)
            qk = sb("m_qk", [NS, 4], F32)
            nq = sb("m_nq", [NS, 4], F32)
            a = sb("m_a", [NS, 4], F32)
            dn = sb("m_dn", [NS, 4], F32)
            t512 = sb("m_t512", [NS, 512], F32)
            rows = sb("m_rows", [NS, 1024], F32)
            nnew = sb("m_nnew", [NS, 512], F32)
            c0 = [sb("m_c0_%d" % i, [128, 512], F32) for i in range(2)]
            cn = [sb("m_cn_%d" % i, [128, 512], F32) for i in range(2)]
            pr = sb("m_pr", [128, 512], F32)
            CqT = sb("m_CqT", [128, 4, NS], F32)
            mvT = sb("m_mvT", [128, 4, NS], F32)
            cq = sb("m_cq", [NS, 512], F32)
            hh = sb("m_hh", [NS, 512], F32)
            st = sb("m_st", [NS, 8], F32)
            mgb = sb("m_mgb", [NS, 512], F32)
            sg = sb("m_sg", [NS, 512], F32)
            v3 = lambda ap: ap.rearrange("p (h d) -> p h d", h=4)
            bc = lambda ap: ap.unsqueeze(2).broadcast_to([NS, 4, 128])
            mq, mk, mv, mo = ms[:, 0:512], ms[:, 512:1024], ms[:, 1024:1536], ms[:, 1536:2048]
            MS = [("m_s", nm, hp) for nm in ("mq", "mk", "mv", "mo") for hp in range(2)] + [("m_s", "g")]
            self.load(bif[:], self.b_if.broadcast_to([NS, 8]), ["m_bif"])
            self.load(m0[:], self.st_m, ["m_m0"])
            self.load(n0[:], self.st_n, ["m_n0"])
            self.load(mgb[:], self.mg.broadcast_to([NS, 512]), ["m_mgb"])
            V(lambda e: e.tensor_copy(out=oh[:], in_=self.identf[0:16, 0:16].unsqueeze(2).broadcast_to([16, 16, 128])),
              reads=["identf"], writes=["m_oh"])
            V(lambda e: e.tensor_tensor(out=ig[:], in0=ms[:, 2048:2052], in1=bif[:, 0:4], op=ALU.add), reads=MS + ["m_bif"], writes=["m_ig"])
            V(lambda e: e.tensor_tensor(out=lf[:], in0=ms[:, 2052:2056], in1=bif[:, 4:8], op=ALU.add), reads=MS + ["m_bif"], writes=["m_lf"])
            A(lambda e: e.activation(out=lf[:], in_=lf[:], func=AF.Exp, scale=-1.0), reads=["m_lf"], writes=["m_lf"])
            A(lambda e: e.activation(out=lf[:], in_=lf[:], func=AF.Ln, bias=1.0), reads=["m_lf"], writes=["m_lf"])
            V(lambda e: e.tensor_scalar_mul(out=lf[:], in0=lf[:], scalar1=-1.0), reads=["m_lf"], writes=["m_lf"])
            V(lambda e: e.tensor_tensor(out=mi_[:], in0=lf[:], in1=m0[:], op=ALU.add), reads=["m_lf", "m_m0"], writes=["m_mi"])
            V(lambda e: e.tensor_tensor(out=mt[:], in0=mi_[:], in1=ig[:], op=ALU.max), reads=["m_mi", "m_ig"], writes=["m_mt"])
            self.store(self.outs["m_s"], mt[:], ["m_mt"])
            V(lambda e: e.tensor_tensor(out=w[:], in0=mi_[:], in1=mt[:], op=ALU.subtract), reads=["m_mi", "m_mt"], writes=["m_w"])
            A(lambda e: e.activation(out=w[:], in_=w[:], func=AF.Exp), reads=["m_w"], writes=["m_w"])
            V(lambda e: e.tensor_tensor(out=gi_[:], in0=ig[:], in1=mt[:], op=ALU.subtract), reads=["m_ig", "m_mt"], writes=["m_gi"])
            A(lambda e: e.activation(out=gi_[:], in_=gi_[:], func=AF.Exp), reads=["m_gi"], writes=["m_gi"])
            A(lambda e: e.activation(out=emt[:], in_=mt[:], func=AF.Exp, scale=-1.0), reads=["m_mt"], writes=["m_emt"])
            V(lambda e: e.tensor_tensor(out=t512[:], in0=mq, in1=mk, op=ALU.mult), reads=MS, writes=["m_t512"])
            V(lambda e: e.tensor_reduce(out=qk[:], in_=v3(t512[:]), axis=AX.X, op=ALU.add), reads=["m_t512"], writes=["m_qk"])
            V(lambda e: e.tensor_tensor(out=t512[:], in0=mq, in1=n0[:], op=ALU.mult), reads=MS + ["m_n0", "m_qk"], writes=["m_t512"])
            V(lambda e: e.tensor_reduce(out=nq[:], in_=v3(t512[:]), axis=AX.X, op=ALU.add), reads=["m_t512"], writes=["m_nq"])
            V(lambda e: e.tensor_tensor(out=a[:], in0=gi_[:], in1=qk[:], op=ALU.mult), reads=["m_gi", "m_qk"], writes=["m_a"])
            V(lambda e: e.tensor_tensor(out=dn[:], in0=w[:], in1=nq[:], op=ALU.mult), reads=["m_w", "m_nq"], writes=["m_dn"])
            V(lambda e: e.tensor_tensor(out=dn[:], in0=dn[:], in1=a[:], op=ALU.add), reads=["m_dn", "m_a"], writes=["m_dn"])
            V(lambda e: e.tensor_scalar_mul(out=nq[:], in0=dn[:], scalar1=-1.0), reads=["m_dn"], writes=["m_nq"])
            V(lambda e: e.tensor_tensor(out=dn[:], in0=dn[:], in1=nq[:], op=ALU.max), reads=["m_dn", "m_nq"], writes=["m_dn"])
            V(lambda e: e.tensor_tensor(out=dn[:], in0=dn[:], in1=emt[:], op=ALU.max), reads=["m_dn", "m_emt"], writes=["m_dn"])
            V(lambda e: e.reciprocal(out=dn[:], in_=dn[:]), reads=["m_dn"], writes=["m_dn"])
            V(lambda e: e.tensor_tensor(out=v3(rows[:, 512:1024]), in0=v3(mk), in1=bc(gi_[:]), op=ALU.mult),
              reads=MS + ["m_gi"], writes=["m_rows"])
            V(lambda e: e.tensor_tensor(out=v3(nnew[:]), in0=v3(n0[:]), in1=bc(w[:]), op=ALU.mult), reads=["m_n0", "m_w"], writes=["m_nnew"])
            V(lambda e: e.tensor_tensor(out=nnew[:], in0=nnew[:], in1=rows[:, 512:1024], op=ALU.add),
              reads=["m_nnew", "m_rows"], writes=["m_nnew"])
            self.store(self.n_s, nnew[:], ["m_nnew"])
            V(lambda e: e.tensor_copy(out=v3(rows[:, 0:512]), in_=bc(w[:])), reads=["m_w", "m_rows"], writes=["m_rows"])
            for h in range(4):
                bank = 6 + h % 2
                self.tp(pb[bank][:, 0:NS], ms[:, 1024 + h * 128:1024 + (h + 1) * 128], self.identf[0:NS, 0:NS], MS, [("pb", bank)])
                A(lambda e, bank=bank, h=h: e.copy(out=mvT[:, h, :], in_=pb[bank][:, 0:NS]), reads=[("pb", bank)], writes=["m_mvT"])
            for b in range(NS):
                c_0 = c0[b % 2]
                ck = ("m_c0", b % 2)
                c_n = cn[b % 2]
                nk = ("m_cn", b % 2)
                self.load(c_0[:].rearrange("p (h d) -> p h d", h=4), self.st_C[b].rearrange("h v d -> v h d"), [ck])
                self.mm(pb[0][:, :], oh[0:16, b, :], ms[:, 0:512], True, True, ["m_oh"] + MS, [("pb", 0)])
                self.mm(pb[1][:, :], oh[0:16, b, :], rows[:, 0:512], True, True, ["m_oh", "m_rows"], [("pb", 1)])
                self.mm(pb[2][:, :], oh[0:16, b, :], rows[:, 512:1024], True, True, ["m_oh", "m_rows"], [("pb", 2)])
                V(lambda e, c_0=c_0: e.tensor_tensor(out=pr[:], in0=c_0[:], in1=pb[0][:, :], op=ALU.mult),
                  reads=[ck, ("pb", 0)], writes=["m_pr"])
                cqo = CqT[:, :, b]
                V(lambda e, cqo=cqo: e.tensor_reduce(out=cqo, in_=pr[:].rearrange("p (h d) -> p h d", h=4), axis=AX.X, op=ALU.add),
                  reads=["m_pr"], writes=["m_CqT"])
                V(lambda e, c_0=c_0, c_n=c_n: e.tensor_tensor(out=c_n[:], in0=c_0[:], in1=pb[1][:, :], op=ALU.mult),
                  reads=[ck, ("pb", 1)], writes=[nk])
                for h in range(4):
                    cs_ = c_n[:, h * 128:(h + 1) * 128]
                    gk_ = pb[2][:, h * 128:(h + 1) * 128]
                    sc_ = mvT[:, h, b:b + 1]
                    V(lambda e, cs_=cs_, gk_=gk_, sc_=sc_: e.scalar_tensor_tensor(out=cs_, in0=gk_, scalar=sc_, in1=cs_,
                                                                                  op0=ALU.mult, op1=ALU.add),
                      reads=[nk, ("pb", 2), "m_mvT"], writes=[nk])
                self.store(self.C_s[b].rearrange("h v d -> v h d"), c_n[:].rearrange("p (h d) -> p h d", h=4), [nk])
            for h in range(4):
                self.tp(pb[4][0:NS, h * 128:(h + 1) * 128], CqT[:, h, :], self.identf[:, :], ["m_CqT"], [("pb", 4)])
            V(lambda e: e.tensor_tensor(out=v3(cq[:]), in0=v3(pb[4][0:NS, :]), in1=bc(w[:]), op=ALU.mult),
              reads=[("pb", 4), "m_w"], writes=["m_cq"])
            V(lambda e: e.tensor_tensor(out=v3(hh[:]), in0=v3(mv), in1=bc(a[:]), op=ALU.mult), reads=MS + ["m_a"], writes=["m_hh"])
            V(lambda e: e.tensor_tensor(out=hh[:], in0=hh[:], in1=cq[:], op=ALU.add), reads=["m_hh", "m_cq"], writes=["m_hh"])
            V(lambda e: e.tensor_tensor(out=v3(hh[:]), in0=v3(hh[:]), in1=bc(dn[:]), op=ALU.mult), reads=["m_hh", "m_dn"], writes=["m_hh"])
            V(lambda e: e.tensor_reduce(out=st[:, 0:4], in_=v3(hh[:]), axis=AX.X, op=ALU.add), reads=["m_hh"], writes=["m_st"])
            V(lambda e: e.tensor_scalar_mul(out=st[:, 0:4], in0=st[:, 0:4], scalar1=1.0 / 128.0), reads=["m_st"], writes=["m_st"])
            V(lambda e: e.tensor_tensor(out=v3(hh[:]), in0=v3(hh[:]), in1=bc(st[:, 0:4]), op=ALU.subtract), reads=["m_hh", "m_st"], writes=["m_hh"])
            V(lambda e: e.tensor_tensor(out=t512[:], in0=hh[:], in1=hh[:], op=ALU.mult), reads=["m_hh", "m_nq"], writes=["m_t512"])
            V(lambda e: e.tensor_reduce(out=st[:, 4:8], in_=v3(t512[:]), axis=AX.X, op=ALU.add), reads=["m_t512"], writes=["m_st"])
            A(lambda e: e.activation(out=st[:, 4:8], in_=st[:, 4:8], func=AF.Sqrt, bias=self.epsb[0:NS, 0:1], scale=1.0 / 128.0),
              reads=["m_st", "epsb"], writes=["m_st"])
            V(lambda e: e.reciprocal(out=st[:, 4:8], in_=st[:, 4:8]), reads=["m_st"], writes=["m_st"])
            V(lambda e: e.tensor_tensor(out=v3(hh[:]), in0=v3(hh[:]), in1=bc(st[:, 4:8]), op=ALU.mult), reads=["m_hh", "m_st"], writes=["m_hh"])
            V(lambda e: e.tensor_tensor(out=hh[:], in0=hh[:], in1=mgb[:], op=ALU.mult), reads=["m_hh", "m_mgb"], writes=["m_hh"])
            A(lambda e: e.activation(out=sg[:], in_=mo, func=AF.Sigmoid), reads=MS, writes=["m_sg"])
            V(lambda e: e.tensor_tensor(out=hh[:], in0=hh[:], in1=sg[:], op=ALU.mult), reads=["m_hh", "m_sg"], writes=["m_hh"])
            for c in range(4):
                bank = 6 + c % 2
                self.tp(pb[bank][:, 0:NS], hh[:, c * 128:(c + 1) * 128], self.identf[0:NS, 0:NS], ["m_hh"], [("pb", bank)])
                A(lambda e, bank=bank, c=c: e.copy(out=self.concatT[:, 4 + c, S:T], in_=pb[bank][:, 0:NS]),
                  reads=[("pb", bank)], writes=[("cc", 4 + c, 4)])

    def layernorm(self, zt, n, gam, bet, xc, st, out, zk="lnz"):
        V, A = self.V, self.A
        s0, s1, s2 = st[0:n, 0:1], st[0:n, 1:2], st[0:n, 2:3]
        xn, gn, bn, en = xc[0:n, :], gam[0:n, :], bet[0:n, :], self.epsb[0:n, 0:1]
        A(lambda e: e.activation(out=xn, in_=zt, func=AF.Identity, accum_out=s0), reads=[zk], writes=["lnxc", "lnst"])
        V(lambda e: e.tensor_scalar_mul(out=s0, in0=s0, scalar1=-1.0 / D), reads=["lnst"], writes=["lnst"])
        A(lambda e: e.activation(out=xn, in_=zt, func=AF.Square, bias=s0, scale=1.0, accum_out=s1),
          reads=[zk, "lnst"], writes=["lnxc", "lnst"])
        A(lambda e: e.activation(out=s1, in_=s1, func=AF.Sqrt, bias=en, scale=1.0 / D), reads=["lnst", "epsb"], writes=["lnst"])
        V(lambda e: e.reciprocal(out=s1, in_=s1), reads=["lnst"], writes=["lnst"])
        V(lambda e: e.tensor_tensor(out=s2, in0=s0, in1=s1, op=ALU.mult), reads=["lnst"], writes=["lnst"])
        A(lambda e: e.activation(out=xn, in_=zt, func=AF.Identity, bias=s2, scale=s1), reads=[zk, "lnst"], writes=["lnxc"])
        V(lambda e: e.tensor_tensor(out=xn, in0=xn, in1=gn, op=ALU.mult), reads=["lnxc", "lnc"], writes=["lnxc"])
        V(lambda e: e.tensor_tensor(out=out, in0=xn, in1=bn, op=ALU.add), reads=["lnxc", "lnc"], writes=[zk])

    def layernorm_multi(self, items, gam, bet, junk, st):
        V, A = self.V, self.A
        C = []
        for zt, n, zk, sl in items:
            C.append((zt, n, zk, st[0:n, 3 * sl:3 * sl + 1], st[0:n, 3 * sl + 1:3 * sl + 2], st[0:n, 3 * sl + 2:3 * sl + 3],
                      ("lnst", sl), ("lnjunk", sl), junk[0:n, :], gam[0:n, :], bet[0:n, :], self.epsb[0:n, 0:1]))
        for zt, n, zk, s0, s1, s2, sk, jk, jn, gn, bn, en in C:
            A(lambda e, jn=jn, zt=zt, s0=s0: e.activation(out=jn, in_=zt, func=AF.Identity, accum_out=s0), reads=[zk], writes=[jk, sk])
        for zt, n, zk, s0, s1, s2, sk, jk, jn, gn, bn, en in C:
            V(lambda e, s0=s0: e.tensor_scalar_mul(out=s0, in0=s0, scalar1=-1.0 / D), reads=[sk], writes=[sk])
        for zt, n, zk, s0, s1, s2, sk, jk, jn, gn, bn, en in C:
            A(lambda e, jn=jn, zt=zt, s0=s0, s1=s1: e.activation(out=jn, in_=zt, func=AF.Square, bias=s0, scale=1.0, accum_out=s1),
              reads=[zk, sk], writes=[jk, sk])
        for zt, n, zk, s0, s1, s2, sk, jk, jn, gn, bn, en in C:
            A(lambda e, s1=s1, en=en: e.activation(out=s1, in_=s1, func=AF.Sqrt, bias=en, scale=1.0 / D), reads=[sk, "epsb"], writes=[sk])
        for zt, n, zk, s0, s1, s2, sk, jk, jn, gn, bn, en in C:
            V(lambda e, s1=s1: e.reciprocal(out=s1, in_=s1), reads=[sk], writes=[sk])
            V(lambda e, s0=s0, s1=s1, s2=s2: e.tensor_tensor(out=s2, in0=s0, in1=s1, op=ALU.mult), reads=[sk], writes=[sk])
        for zt, n, zk, s0, s1, s2, sk, jk, jn, gn, bn, en in C:
            A(lambda e, zt=zt, s1=s1, s2=s2: e.activation(out=zt, in_=zt, func=AF.Identity, bias=s2, scale=s1), reads=[sk], writes=[zk])
        for zt, n, zk, s0, s1, s2, sk, jk, jn, gn, bn, en in C:
            V(lambda e, zt=zt, gn=gn: e.tensor_tensor(out=zt, in0=zt, in1=gn, op=ALU.mult), reads=["lnc"], writes=[zk])
            V(lambda e, zt=zt, bn=bn: e.tensor_tensor(out=zt, in0=zt, in1=bn, op=ALU.add), reads=["lnc"], writes=[zk])

    def post(self):
        pb = self.pb
        V, A, G = self.V, self.A, self.G
        NC2 = 512 + NS
        with self.scope() as es1:
            sb = lambda n, s, d: self.sb(es1, n, s, d)
            wo = sb("wo", [128, 8, D], BF16)
            lnc = sb("lnc", [128, 4, D], F32)
            xt = [sb("pxt%d" % i, [128, D], F32) for i in range(2)]
            x1g = sb("x1g", [128, 5, D], F32)
            xcs = sb("lnxc", [128, D], F32)
            st = sb("lnst", [128, 15], F32)
            h2T = sb("h2T", [128, 8, NC2], BF16)
            actT = sb("actT", [128, NFF, NC2], BF16)
            wg = [sb("wgc%d" % i, [128, 8, 256], BF16) for i in range(2)]
            wu = [sb("wuc%d" % i, [128, 8, 256], BF16) for i in range(2)]
            wd = [sb("wdq%d" % i, [128, NFF, 256], BF16) for i in range(2)]
            sg = [sb("sgt%d" % i, [128, NC2], F32) for i in range(2)]
            tmp = sb("ptmp", [128, 512], F32)
            self.load(wo[:], self.w_out.rearrange("(k p) n -> p k n", p=128), ["wo"], queue="gpsimd")
            for i, src in enumerate((self.ln1_g, self.ln1_b, self.ln2_g, self.ln2_b)):
                self.load(lnc[:, i, :], src.broadcast_to([128, D]), ["lnc"])
            wgv = self.w_gate.rearrange("(k p) n -> p k n", p=128)
            wuv = self.w_up.rearrange("(k p) n -> p k n", p=128)
            wdv = self.w_down.rearrange("(f p) n -> p f n", p=128)
            xi = 0
            wdi = 0

            def ld_wd(q):
                self.load(wd[q % 2][:], wdv[:, :, q * 256:(q + 1) * 256], [("wdq", q % 2)], queue="gpsimd")

            def ldw(f2):
                self.load(wg[f2 % 2][:], wgv[:, :, f2 * 256:(f2 + 1) * 256], [("wgc", f2 % 2)], queue="gpsimd")
                self.load(wu[f2 % 2][:], wuv[:, :, f2 * 256:(f2 + 1) * 256], [("wuc", f2 % 2)], queue="gpsimd")

            for gi in range(4):
                t0 = gi * 512
                tiles = [(t0 + ti * 128, 128, ti * 128, False) for ti in range(4)]
                if gi == 3:
                    tiles.append((S, NS, 512, True))
                ncols = 512 + (NS if gi == 3 else 0)
                ldw(0)
                xinfo = {}

                def s1_mm(ti, gi=gi, tiles=tiles, xinfo=xinfo):
                    nonlocal xi
                    c0, n, lc, samp = tiles[ti]
                    x_sb = xt[xi % 2]
                    xk = ("pxt", xi % 2)
                    xi += 1
                    xinfo[ti] = (x_sb, xk)
                    if samp:
                        self.load(x_sb[0:n, :], self.xs[:, :], [xk])
                    else:
                        self.load(x_sb[0:n, :], self.xp[c0:c0 + n, :], [xk])
                    cg = 4 if samp else gi
                    for half in range(2):
                        bank = half + 2 * (ti % 2)
                        for c in range(8):
                            self.mm(pb[bank][0:n, :], self.concatT[:, c, c0:c0 + n], wo[:, c, half * 512:(half + 1) * 512],
                                    c == 0, c == 7, [("cc", c, cg), "wo"], [("pb", bank)])

                def s1_rest(ti, gi=gi, tiles=tiles, xinfo=xinfo):
                    c0, n, lc, samp = tiles[ti]
                    x_sb, xk = xinfo[ti]
                    zt = x1g[0:n, ti, :]
                    zk = ("x1g", ti)
                    for half in range(2):
                        bank = half + 2 * (ti % 2)
                        gsrc = self.grow[0:NS, half * 512:(half + 1) * 512] if samp else self.gbc[:, half * 512:(half + 1) * 512]
                        tn, pbn = tmp[0:n, :], pb[bank][0:n, :]
                        zh, xh = zt[:, half * 512:(half + 1) * 512], x_sb[0:n, half * 512:(half + 1) * 512]
                        V(lambda e, tn=tn, pbn=pbn, gsrc=gsrc: e.tensor_tensor(out=tn, in0=pbn, in1=gsrc, op=ALU.mult),
                          reads=[("pb", bank), "grow", "gbc"], writes=["ptmp"])
                        V(lambda e, zh=zh, xh=xh, tn=tn: e.scalar_tensor_tensor(out=zh, in0=xh, scalar=ALPHA, in1=tn,
                                                                                op0=ALU.mult, op1=ALU.add),
                          reads=[xk, "ptmp"], writes=[zk])

                def s1_tp(ti, tiles=tiles):
                    c0, n, lc, samp = tiles[ti]
                    zt = x1g[0:n, ti, :]
                    zk = ("x1g", ti)
                    for k in range(8):
                        bank = 4 + k % 4
                        self.tp(pb[bank][:, 0:n], zt[:, k * 128:(k + 1) * 128], self.identf[0:n, 0:n], [zk], [("pb", bank)])
                        if samp:
                            V(lambda e, bank=bank, k=k: e.tensor_tensor(out=tmp[:, 0:NS], in0=pb[bank][:, 0:NS],
                                                                        in1=self.modT[:, 32 + k, 0:NS], op=ALU.mult),
                              reads=[("pb", bank), "modT"], writes=["ptmp"])
                            V(lambda e, k=k: e.tensor_tensor(out=h2T[:, k, 512:NC2], in0=tmp[:, 0:NS], in1=self.modT[:, 24 + k, 0:NS],
                                                             op=ALU.add), reads=["ptmp", "modT"], writes=[("h2T", 4)])
                        else:
                            A(lambda e, bank=bank, k=k, lc=lc: e.activation(
                                out=h2T[:, k, lc:lc + 128], in_=pb[bank][:, 0:128], func=AF.Identity,
                                bias=self.modT[:, 24 + k, 32:33], scale=self.modT[:, 32 + k, 32:33]),
                                reads=[("pb", bank), "modT"], writes=[("h2T", ti)])

                nt = len(tiles)
                s1_mm(0)
                s1_mm(1)
                for ti in range(nt):
                    s1_rest(ti)
                    if ti + 2 < nt:
                        s1_mm(ti + 2)
                self.layernorm_multi([(x1g[0:n, ti, :], n, ("x1g", ti), ti) for ti, (c0, n, lc, samp) in enumerate(tiles)],
                                     lnc[:, 0, :], lnc[:, 1, :], xcs, st)
                for ti in range(nt):
                    s1_tp(ti)
                h2k = [("h2T", ti) for ti in range(len(tiles))]
                ld_wd(0)
                ld_wd(1)
                for f in range(NFF):
                    f2, fl = f // 2, f % 2
                    if fl == 0 and f2 + 1 < NFF // 2:
                        ldw(f2 + 1)
                    gb = f % 2
                    ub = 2 + f % 2
                    wgt, wut = wg[f2 % 2], wu[f2 % 2]
                    for k in range(8):
                        self.mm(pb[gb][:, 0:512], wgt[:, k, fl * 128:(fl + 1) * 128], h2T[:, k, 0:512], k == 0, k == 7,
                                [("wgc", f2 % 2)] + h2k, [("pb", gb)])
                    for k in range(8):
                        self.mm(pb[ub][:, 0:512], wut[:, k, fl * 128:(fl + 1) * 128], h2T[:, k, 0:512], k == 0, k == 7,
                                [("wuc", f2 % 2)] + h2k, [("pb", ub)])
                    s_t = sg[f % 2]
                    sk = "sgt%d" % (f % 2)
                    sa, ga, ua, aa = s_t[:, 0:512], pb[gb][:, 0:512], pb[ub][:, 0:512], actT[:, f, 0:512]
                    A(lambda e, sa=sa, ga=ga: e.activation(out=sa, in_=ga, func=AF.Silu), reads=[("pb", gb)], writes=[sk])
                    V(lambda e, aa=aa, sa=sa, ua=ua: e.tensor_tensor(out=aa, in0=sa, in1=ua, op=ALU.mult),
                      reads=[sk, ("pb", ub)], writes=["actT"])
                    if gi == 3:
                        eb = 4 + f % 2
                        for k in range(8):
                            self.mm(pb[eb][:, 0:NS], wgt[:, k, fl * 128:(fl + 1) * 128], h2T[:, k, 512:NC2], k == 0, k == 7,
                                    [("wgc", f2 % 2)] + h2k, [("pb", eb)])
                        for k in range(8):
                            self.mm(pb[eb][:, 32:32 + NS], wut[:, k, fl * 128:(fl + 1) * 128], h2T[:, k, 512:NC2], k == 0, k == 7,
                                    [("wuc", f2 % 2)] + h2k, [("pb", eb)])
                        sa, ga, ua, aa = s_t[:, 512:NC2], pb[eb][:, 0:NS], pb[eb][:, 32:32 + NS], actT[:, f, 512:NC2]
                        A(lambda e, sa=sa, ga=ga: e.activation(out=sa, in_=ga, func=AF.Silu), reads=[("pb", eb)], writes=[sk])
                        V(lambda e, aa=aa, sa=sa, ua=ua: e.tensor_tensor(out=aa, in0=sa, in1=ua, op=ALU.mult),
                          reads=[sk, ("pb", eb)], writes=["actT"])
                for q in range(4):
                    w_d = wd[q % 2]
                    wk_ = ("wdq", q % 2)
                    for ti, (c0, n, lc, samp) in enumerate(tiles):
                        bank = 4 + wdi % 4
                        wdi += 1
                        for f in range(NFF):
                            self.mm(pb[bank][0:n, 0:256], actT[:, f, lc:lc + n], w_d[:, f, :], f == 0, f == NFF - 1,
                                    ["actT", wk_], [("pb", bank)])
                        gsrc = (self.grow[0:NS, D + q * 256:D + (q + 1) * 256] if samp
                                else self.gbc[:, D + q * 256:D + (q + 1) * 256])
                        tn, pbn = tmp[0:n, 0:256], pb[bank][0:n, 0:256]
                        V(lambda e, tn=tn, pbn=pbn, gsrc=gsrc: e.tensor_tensor(out=tn, in0=pbn, in1=gsrc, op=ALU.mult),
                          reads=[("pb", bank), "grow", "gbc"], writes=["ptmp"])
                        zs = x1g[0:n, ti, q * 256:(q + 1) * 256]
                        V(lambda e, zs=zs, tn=tn: e.scalar_tensor_tensor(out=zs, in0=zs, scalar=ALPHA, in1=tn,
                                                                         op0=ALU.mult, op1=ALU.add),
                          reads=[("x1g", ti), "ptmp"], writes=[("x1g", ti)])
                    if q + 2 < 4:
                        ld_wd(q + 2)
                self.layernorm_multi([(x1g[0:n, ti, :], n, ("x1g", ti), ti) for ti, (c0, n, lc, samp) in enumerate(tiles)],
                                     lnc[:, 2, :], lnc[:, 3, :], xcs, st)
                for ti, (c0, n, lc, samp) in enumerate(tiles):
                    zt = x1g[0:n, ti, :]
                    if samp:
                        self.store(self.y_s[:, :], zt, [("x1g", ti)])
                    else:
                        self.store(self.y_p[c0:c0 + n, :], zt, [("x1g", ti)])


def prep_inputs(inputs, i):
    f = lambda a: np.ascontiguousarray(np.asarray(a))
    c33 = np.zeros((33, D), np.float32)
    c33[0:NS] = inputs["c_sample"][NS * i:NS * (i + 1)]
    c33[32] = inputs["c_prompt"][i]
    cT = f(c33.T.reshape(8, 128, 33).transpose(1, 0, 2))
    m = {
        "xp": f(inputs["x_prompt"][i]),
        "xs": f(inputs["x_sample"][NS * i:NS * (i + 1), 0, :]),
        "cT": cT,
        "w_ada": f(inputs["w_ada"][0]),
        "b_adaT": f(inputs["b_ada"][0].reshape(48, 128).T),
        "b_ada": f(inputs["b_ada"][0].reshape(1, -1)),
        "w_in": f(inputs["w_in"][0]),
        "b_if": f(inputs["b_if"][0].reshape(1, 8)),
        "mgT": f(inputs["mlstm_norm_g"][0].reshape(4, 128).T),
        "mg": f(inputs["mlstm_norm_g"][0].reshape(1, 512)),
        "w_out": f(inputs["w_out"][0]),
        "ln1_g": f(inputs["ln1_g"][0].reshape(1, D)),
        "ln1_b": f(inputs["ln1_b"][0].reshape(1, D)),
        "w_gate": f(inputs["w_gate"][0]),
        "w_up": f(inputs["w_up"][0]),
        "w_down": f(inputs["w_down"][0]),
        "ln2_g": f(inputs["ln2_g"][0].reshape(1, D)),
        "ln2_b": f(inputs["ln2_b"][0].reshape(1, D)),
        "cache_k": inputs["cache_k"][0].reshape(-1, 512),
        "cache_v": inputs["cache_v"][0].reshape(-1, 512),
        "pt": f(inputs["page_table"][NS * i:NS * (i + 1)].reshape(1, NS * 16)).astype(np.int32),
        "st_C": f(inputs["state_C"][0, NS * i:NS * (i + 1)]),
        "st_n": f(inputs["state_n"][0, NS * i:NS * (i + 1)].reshape(NS, 512)),
        "st_m": f(inputs["state_m"][0, NS * i:NS * (i + 1)]),
    }
    return m


def kernel(**inputs):
    inputs = {k: np.asarray(v) for k, v in inputs.items()}
    b = Builder()
    nc = b.build()
    n = 8
    in_maps = [prep_inputs(inputs, i) for i in range(n)]
    res = run_bass_kernel_spmd(nc, in_maps, core_ids=list(range(n)))
    r = res.results
    cat = lambda k: np.stack([r[i][k] for i in range(n)])
    y_prompt = cat("y_p")
    y_sample = np.concatenate([r[i]["y_s"] for i in range(n)])[:, None, :]
    k_prompt = cat("k_p").reshape(1, 8, S, 8, 64)
    v_prompt = cat("v_p").reshape(1, 8, S, 8, 64)
    C_prompt = cat("C_p").reshape(1, 8, 4, 128, 128)
    n_prompt = cat("n_p").reshape(1, 8, 4, 128)
    m_prompt = cat("m_p").reshape(1, 8, 4)
    k_sample = np.concatenate([r[i]["k_s"] for i in range(n)]).reshape(1, 128, 1, 8, 64)
    v_sample = np.concatenate([r[i]["v_s"] for i in range(n)]).reshape(1, 128, 1, 8, 64)
    C_sample = np.concatenate([r[i]["C_s"] for i in range(n)]).reshape(1, 128, 4, 128, 128)
    n_sample = np.concatenate([r[i]["n_s"] for i in range(n)]).reshape(1, 128, 4, 128)
    m_sample = np.concatenate([r[i]["m_s"] for i in range(n)]).reshape(1, 128, 4)
    return (y_prompt, y_sample, k_prompt, v_prompt, C_prompt, n_prompt, m_prompt,
            k_sample, v_sample, C_sample, n_sample, m_sample)
```

```python
import numpy as np
from contextlib import ExitStack, contextmanager
import concourse.bass as bass
import concourse.mybir as mybir
from concourse.bass_utils import run_bass_kernel_spmd

F32 = mybir.dt.float32
BF16 = mybir.dt.bfloat16
I32 = mybir.dt.int32
ALU = mybir.AluOpType
AF = mybir.ActivationFunctionType
AX = mybir.AxisListType

D = 1024
S = 2048
NS = 16
T = S + NS
NTT = 16
HA = 8
DFF = 2816
NFF = 22
INC = 3592
NEG = -30000.0
ALPHA = 2.0 ** 0.25
EPS = 1e-5
NPHYS = 2560
SERIAL_SAMPLE = False
SAME_ENG_SYNC = True


class _Reg:
    __slots__ = ("w", "r")

    def __init__(self):
        self.w = None
        self.r = []


class _Eng:
    def __init__(self, name):
        self.name = name
        self.ops = []
        self.count = 0
        self.sem = None
        self.waited = {}


class Sched:
    NDMA = 48

    def __init__(self, nc, es):
        self.nc = nc
        self.E = {n: _Eng(n) for n in ("tensor", "vector", "scalar", "gpsimd", "sync")}
        for n, e in self.E.items():
            e.sem = es.enter_context(nc.semaphore("s_" + n))
        self.dsem = [es.enter_context(nc.semaphore("d%d" % i)) for i in range(self.NDMA)]
        self.dcnt = [0] * self.NDMA
        self.dnext = 0
        self.regs = {}
        self.final_events = []

    def reg(self, key):
        r = self.regs.get(key)
        if r is None:
            r = self.regs[key] = _Reg()
        return r

    def _emit_waits(self, e, evs):
        need = {}
        for ev in evs:
            if ev is None:
                continue
            sem, val, src = ev
            if src == e.name and (e.name == "tensor" or not SAME_ENG_SYNC):
                continue
            if e.waited.get(sem, 0) >= val:
                continue
            if need.get(sem, 0) < val:
                need[sem] = val
        for sem, val in need.items():
            e.waited[sem] = val
            e.ops.append(("wait", sem, val))

    def _deps(self, R, W):
        evs = []
        for r in R:
            evs.append(r.w)
        for w in W:
            evs.append(w.w)
            evs.extend(w.r)
        return evs

    def _commit(self, ev, R, W):
        for r in R:
            r.r.append(ev)
        for w in W:
            w.w = ev
            w.r = []

    def op(self, eng, fn, reads=(), writes=()):
        e = self.E[eng]
        pr = [k for k in reads if isinstance(k, tuple) and k[0] == "pb"]
        if pr:
            reads = [k for k in reads if not (isinstance(k, tuple) and k[0] == "pb")]
            writes = list(writes) + [k for k in pr if k not in writes]
        R = [self.reg(k) for k in reads]
        W = [self.reg(k) for k in writes]
        self._emit_waits(e, self._deps(R, W))
        e.count += 1
        ev = (e.sem, e.count, eng)
        e.ops.append(("op", fn, e.sem, 1))
        self._commit(ev, R, W)
        return ev

    def dma(self, queue, fn, reads=(), writes=(), final=False):
        e = self.E[queue]
        R = [self.reg(k) for k in reads]
        W = [self.reg(k) for k in writes]
        s = self.dnext
        self.dnext = (self.dnext + 1) % self.NDMA
        sem = self.dsem[s]
        evs = self._deps(R, W)
        if self.dcnt[s] > 0:
            evs.append((sem, self.dcnt[s], "dma"))
        self._emit_waits(e, evs)
        self.dcnt[s] += 16
        ev = (sem, self.dcnt[s], "dma")
        e.ops.append(("op", fn, sem, 16))
        self._commit(ev, R, W)
        if final:
            self.final_events.append(ev)
        return ev

    def barrier(self):
        evs = [(e.sem, e.count, n) for n, e in self.E.items() if e.count > 0]
        evs += [(self.dsem[i], self.dcnt[i], "dma") for i in range(self.NDMA) if self.dcnt[i] > 0]
        for n, e in self.E.items():
            self._emit_waits(e, [ev for ev in evs if ev[2] != n])

    def finish(self):
        self._emit_waits(self.E["sync"], self.final_events)

    def replay(self):
        with self.nc.Block() as block:
            def mk(name):
                ops = self.E[name].ops

                def body(eng):
                    for o in ops:
                        if o[0] == "wait":
                            eng.wait_ge(o[1], o[2])
                        else:
                            o[1](eng).then_inc(o[2], o[3])
                return body

            block.tensor(mk("tensor"))
            block.vector(mk("vector"))
            block.scalar(mk("scalar"))
            block.gpsimd(mk("gpsimd"))
            block.sync(mk("sync"))


class Builder:
    def __init__(self, stage=99, nphys=NPHYS, debug=False):
        self.stage = stage
        self.debug = debug
        self.nphys = nphys
        self.nc = bass.Bass("TRN2", target_bir_lowering=False)
        self.es = ExitStack()
        self.ins = {}
        self.outs = {}

    def din(self, name, shape, dt=F32):
        ap = self.nc.dram_tensor(name, list(shape), dt, kind="ExternalInput").ap()
        self.ins[name] = ap
        return ap

    def dout(self, name, shape, dt=F32):
        ap = self.nc.dram_tensor(name, list(shape), dt, kind="ExternalOutput").ap()
        self.outs[name] = ap
        return ap

    def sb(self, es, name, shape, dt):
        self._uid = getattr(self, "_uid", 0) + 1
        return es.enter_context(self.nc.sbuf_tensor("%s_u%d" % (name, self._uid), list(shape), dt))

    @contextmanager
    def scope(self):
        es = ExitStack()
        try:
            yield es
        finally:
            es.close()
            self.S.barrier()

    def mm(self, out, lhsT, rhs, start, stop, reads, writes):
        self.S.op("tensor", lambda e: e.matmul(out, lhsT=lhsT, rhs=rhs, start=start, stop=stop),
                  reads=reads, writes=writes)

    def tp(self, out, in_, ident, reads, writes):
        self.S.op("tensor", lambda e: e.transpose(out=out, in_=in_, identity=ident),
                  reads=list(reads) + ["identf"], writes=writes)

    def load(self, out, in_, writes, queue="sync", reads=()):
        self.S.dma(queue, lambda e: e.dma_start(out=out, in_=in_), reads=reads, writes=writes)

    def store(self, out, in_, reads, queue="sync"):
        self.S.dma(queue, lambda e: e.dma_start(out=out, in_=in_), reads=reads, final=True)

    def V(self, fn, reads=(), writes=()):
        self.S.op("vector", fn, reads=reads, writes=writes)

    def A(self, fn, reads=(), writes=()):
        self.S.op("scalar", fn, reads=reads, writes=writes)

    def G(self, fn, reads=(), writes=()):
        self.S.op("gpsimd", fn, reads=reads, writes=writes)

    def build(self):
        nc = self.nc
        es = self.es
        self.S = Sched(nc, es)
        self.xp = self.din("xp", [S, D])
        self.xs = self.din("xs", [NS, D])
        self.cT = self.din("cT", [128, 8, 33])
        self.w_ada = self.din("w_ada", [D, 6 * D])
        self.b_adaT = self.din("b_adaT", [128, 48])
        self.b_ada = self.din("b_ada", [1, 6 * D])
        self.w_in = self.din("w_in", [D, INC])
        self.b_if = self.din("b_if", [1, 8])
        self.mgT = self.din("mgT", [128, 4])
        self.mg = self.din("mg", [1, 512])
        self.w_out = self.din("w_out", [D, D])
        self.ln1_g = self.din("ln1_g", [1, D])
        self.ln1_b = self.din("ln1_b", [1, D])
        self.w_gate = self.din("w_gate", [D, DFF])
        self.w_up = self.din("w_up", [D, DFF])
        self.w_down = self.din("w_down", [DFF, D])
        self.ln2_g = self.din("ln2_g", [1, D])
        self.ln2_b = self.din("ln2_b", [1, D])
        self.cache_k = self.din("cache_k", [self.nphys * 128, 512])
        self.cache_v = self.din("cache_v", [self.nphys * 128, 512])
        self.pt = self.din("pt", [1, NS * 16], I32)
        self.st_C = self.din("st_C", [NS, 4, 128, 128])
        self.st_n = self.din("st_n", [NS, 512])
        self.st_m = self.din("st_m", [NS, 4])

        self.y_p = self.dout("y_p", [S, D])
        self.y_s = self.dout("y_s", [NS, D])
        self.k_p = self.dout("k_p", [S, 512])
        self.v_p = self.dout("v_p", [S, 512])
        self.C_p = self.dout("C_p", [4, 128, 128])
        self.n_p = self.dout("n_p", [4, 128])
        self.m_p = self.dout("m_p", [4, 1])
        self.k_s = self.dout("k_s", [NS, 512])
        self.v_s = self.dout("v_s", [NS, 512])
        self.C_s = self.dout("C_s", [NS, 4, 128, 128])
        self.n_s = self.dout("n_s", [NS, 512])
        self.m_s = self.dout("m_s", [NS, 4])
        if self.debug:
            self.dbg = self.dout("dbg", [128, 8, T], BF16)

        self.pb = [es.enter_context(nc.psum_tensor("pb%d" % i, [128, 512], F32)) for i in range(8)]

        self.consts()
        if self.stage >= -1:
            self.adaln()
        with self.scope() as es0:
            self.attn_consts(es0)
            with self.scope() as es1:
                if self.stage >= 0:
                    self.make_hT(es1)
                if self.stage == 0 and self.debug:
                    self.store(self.dbg, self.hT[:], reads=[("hT", t) for t in range(NTT + 1)])
                if self.stage >= 1:
                    with self.scope() as esm:
                        side = None
                        if self.stage >= 5:
                            side = self.sample_gen(esm)
                            next(side)
                        for hg in range(2):
                            with self.scope() as es2:
                                self.attn_inproj(es2, hg)
                                if side is not None and hg == 1:
                                    next(side)
                                if self.stage >= 2:
                                    self.moba(hg, side if hg == 1 else None, final=True)
                if self.stage >= 3:
                    with self.scope() as es2:
                        self.mlstm_gates(es2)
                        for hp in range(2):
                            with self.scope() as es3:
                                self.mlstm_inproj(es3, hp)
                                self.mlstm_heads(es3, hp)
            if self.stage >= 5:
                self.sample_mlstm()
        if self.stage >= 4:
            self.post()
        if self.stage >= 1 and self.debug:
            self.store(self.dbg, self.concatT[:], reads=self.cc_keys())
        self.S.finish()
        self.S.replay()
        es.close()
        return nc

    def consts(self):
        nc, es = self.nc, self.es
        sb = lambda n, s, d: self.sb(es, n, s, d)
        self.identf = sb("identf", [128, 128], F32)
        self.identb = sb("identb", [128, 128], BF16)
        self.onesb = sb("onesb", [128, 128], BF16)
        self.onesf = sb("onesf", [128, 128], F32)
        self.Utri = sb("Utri", [128, 128], F32)
        self.concatT = sb("concatT", [128, 8, T], BF16)
        self.epsb = sb("epsb", [128, 1], F32)
        G = self.G
        G(lambda e: e.memset(self.epsb[:], EPS), writes=["epsb"])
        G(lambda e: e.memset(self.identf[:], 1.0), writes=["identf"])
        G(lambda e: e.affine_select(out=self.identf[:], in_=self.identf[:], pattern=[[-1, 128]],
                                    compare_op=ALU.is_equal, fill=0.0, base=0, channel_multiplier=1),
          reads=["identf"], writes=["identf"])
        self.V(lambda e: e.tensor_copy(out=self.identb[:], in_=self.identf[:]), reads=["identf"], writes=["identb"])
        G(lambda e: e.memset(self.onesb[:], 1.0), writes=["onesb"])
        G(lambda e: e.memset(self.onesf[:], 1.0), writes=["onesf"])
        G(lambda e: e.memset(self.Utri[:], 1.0), writes=["Utri"])
        G(lambda e: e.affine_select(out=self.Utri[:], in_=self.Utri[:], pattern=[[1, 128]],
                                    compare_op=ALU.is_ge, fill=0.0, base=0, channel_multiplier=-1),
          reads=["Utri"], writes=["Utri"])

    def attn_consts(self, es0):
        sb = lambda n, s, d: self.sb(es0, n, s, d)
        G = self.G
        self.cb = sb("cb", [128, 4, 512], BF16)
        self.aq_s = sb("aq_s", [NS, 512], F32)
        self.ak_s = sb("ak_s", [NS, 512], F32)
        self.av_s = sb("av_s", [NS, 512], F32)
        self.msamp = sb("msamp", [NS, 2056], F32)
        self.blkind = sb("blkind", [8, S], BF16)
        with self.scope() as es_scr:
            scr = self.sb(es_scr, "cscr", [128, 4, 512], F32)
            indf = self.sb(es_scr, "indf", [8, S], F32)
            G(lambda e: e.memset(scr[:], 0.0), writes=["cscr"])
            for r in range(4):
                G(lambda e, r=r: e.affine_select(out=scr[:, r, :], in_=scr[:, r, :], pattern=[[1, 512]],
                                                 compare_op=ALU.is_ge, fill=NEG, base=-r * 128, channel_multiplier=-1),
                  reads=["cscr"], writes=["cscr"])
            self.V(lambda e: e.tensor_copy(out=self.cb[:], in_=scr[:]), reads=["cscr"], writes=["cb"])
            G(lambda e: e.memset(indf[:], 1.0), writes=["indf"])
            G(lambda e: e.affine_select(out=indf[:], in_=indf[:], pattern=[[1, S]], compare_op=ALU.is_ge,
                                        fill=0.0, base=0, channel_multiplier=-256), reads=["indf"], writes=["indf"])
            G(lambda e: e.affine_select(out=indf[:], in_=indf[:], pattern=[[-1, S]], compare_op=ALU.is_ge,
                                        fill=0.0, base=255, channel_multiplier=256), reads=["indf"], writes=["indf"])
            self.V(lambda e: e.tensor_copy(out=self.blkind[:], in_=indf[:]), reads=["indf"], writes=["blkind"])

    def adaln(self):
        nc, es = self.nc, self.es
        self.modT = self.sb(es, "modT", [128, 48, 33], F32)
        self.grow = self.sb(es, "grow", [33, 2 * D], F32)
        self.gbc = self.sb(es, "gbc", [128, 2 * D], F32)
        with self.scope() as es1:
            cTf = self.sb(es1, "cTf", [128, 8, 33], F32)
            sig = self.sb(es1, "csig", [128, 8, 33], F32)
            scT = self.sb(es1, "scT", [128, 8, 33], BF16)
            badT = self.sb(es1, "badT", [128, 48], F32)
            brow = self.sb(es1, "brow", [33, 2 * D], F32)
            sel32 = self.sb(es1, "sel32", [33, 128], F32)
            wbuf = [self.sb(es1, "wada%d" % i, [128, 8, 512], BF16) for i in range(2)]
            self.load(cTf[:], self.cT, ["cTf"])
            self.load(badT[:], self.b_adaT, ["badT"])
            self.load(brow[:, 0:D], self.b_ada[:, 2 * D:3 * D].broadcast_to([33, D]), ["brow0"])
            self.load(brow[:, D:2 * D], self.b_ada[:, 5 * D:6 * D].broadcast_to([33, D]), ["brow1"])
            self.A(lambda e: e.activation(out=sig[:], in_=cTf[:], func=AF.Sigmoid), reads=["cTf"], writes=["csig"])
            self.V(lambda e: e.tensor_tensor(out=scT[:], in0=cTf[:], in1=sig[:], op=ALU.mult),
                   reads=["cTf", "csig"], writes=["scT"])
            wv = self.w_ada.rearrange("(k p) n -> p k n", p=128)
            for blk in range(12):
                wb = wbuf[blk % 2]
                wk = "wada%d" % (blk % 2)
                self.load(wb[:], wv[:, :, blk * 512:(blk + 1) * 512], [wk], queue="gpsimd")
                if blk in (4, 5, 10, 11):
                    gi = {4: 0, 5: 1, 10: 2, 11: 3}[blk]
                    pbk = ("pb", 4 + gi % 2)
                    pt = self.pb[4 + gi % 2]
                    for k in range(8):
                        self.mm(pt[0:33, :], scT[:, k, :], wb[:, k, :], k == 0, k == 7, ["scT", wk], [pbk])
                    self.V(lambda e, pt=pt, gi=gi: e.tensor_tensor(out=self.grow[:, gi * 512:(gi + 1) * 512],
                                                                  in0=pt[0:33, :], in1=brow[:, gi * 512:(gi + 1) * 512],
                                                                  op=ALU.add),
                           reads=[pbk, "brow0", "brow1"], writes=["grow"])
                else:
                    for cc in range(4):
                        c = blk * 4 + cc
                        bank = cc % 4
                        pbk = ("pb", bank)
                        pt = self.pb[bank]
                        for k in range(8):
                            self.mm(pt[:, 0:33], wb[:, k, cc * 128:(cc + 1) * 128], scT[:, k, :], k == 0, k == 7,
                                    ["scT", wk], [pbk])
                        self.A(lambda e, pt=pt, c=c: e.activation(out=self.modT[:, c, :], in_=pt[:, 0:33],
                                                                 func=AF.Identity, bias=badT[:, c:c + 1], scale=1.0),
                               reads=[pbk, "badT"], writes=["modT"])
            for c0 in (8, 32):
                self.V(lambda e, c0=c0: e.tensor_scalar_add(out=self.modT[:, c0:c0 + 8, :], in0=self.modT[:, c0:c0 + 8, :],
                                                            scalar1=1.0), reads=["modT"], writes=["modT"])
            self.V(lambda e: e.tensor_scalar_add(out=self.grow[:], in0=self.grow[:], scalar1=1.0),
                   reads=["grow"], writes=["grow"])
            self.G(lambda e: e.memset(sel32[:], 0.0), writes=["sel32"])
            self.G(lambda e: e.memset(sel32[32:33, :], 1.0), reads=["sel32"], writes=["sel32"])
            for q in range(4):
                pbk = ("pb", 6 + q % 2)
                pt = self.pb[6 + q % 2]
                self.mm(pt[:, :], sel32[:, :], self.grow[:, q * 512:(q + 1) * 512], True, True, ["sel32", "grow"], [pbk])
                self.A(lambda e, pt=pt, q=q: e.copy(out=self.gbc[:, q * 512:(q + 1) * 512], in_=pt[:, :]),
                       reads=[pbk], writes=["gbc"])

    def make_hT(self, es1):
        self.hT = self.sb(es1, "hT", [128, 8, T], BF16)
        with self.scope() as es2:
            xt = [self.sb(es2, "xt%d" % i, [128, D], F32) for i in range(2)]
            xsT = self.sb(es2, "xsT", [128, 8, NS], F32)
            for tt in range(NTT + 1):
                x_sb = xt[tt % 2]
                xk = "xt%d" % (tt % 2)
                if tt < NTT:
                    n = 128
                    self.load(x_sb[:, :], self.xp[tt * 128:(tt + 1) * 128, :], [xk])
                else:
                    n = NS
                    self.load(x_sb[0:NS, :], self.xs[:, :], [xk])
                for k in range(8):
                    bank = k % 4
                    pbk = ("pb", bank)
                    pt = self.pb[bank]
                    self.tp(pt[:, 0:n], x_sb[0:n, k * 128:(k + 1) * 128], self.identf[0:n, 0:n], [xk], [pbk])
                    if tt < NTT and k % 2 == 0:
                        self.A(lambda e, pt=pt, k=k, tt=tt: e.activation(
                            out=self.hT[:, k, tt * 128:(tt + 1) * 128], in_=pt[:, 0:128], func=AF.Identity,
                            bias=self.modT[:, k, 32:33], scale=self.modT[:, 8 + k, 32:33]),
                            reads=[pbk, "modT"], writes=[("hT", tt)])
                    elif tt < NTT:
                        self.V(lambda e, pt=pt, k=k, tt=tt: e.tensor_scalar(
                            out=self.hT[:, k, tt * 128:(tt + 1) * 128], in0=pt[:, 0:128], scalar1=self.modT[:, 8 + k, 32:33],
                            scalar2=self.modT[:, k, 32:33], op0=ALU.mult, op1=ALU.add),
                            reads=[pbk, "modT"], writes=[("hT", tt)])
                    else:
                        self.V(lambda e, pt=pt, k=k: e.tensor_tensor(out=xsT[:, k, :], in0=pt[:, 0:NS],
                                                                    in1=self.modT[:, 8 + k, 0:NS], op=ALU.mult),
                               reads=[pbk, "modT"], writes=["xsT"])
            self.V(lambda e: e.tensor_tensor(out=self.hT[:, :, S:T], in0=xsT[:, :, :], in1=self.modT[:, 0:8, 0:NS],
                                             op=ALU.add), reads=["xsT", "modT"], writes=[("hT", NTT)])

    def attn_inproj(self, es2, hg):
        self.qaug = self.sb(es2, "qaug%d" % hg, [72, 4, S], BF16)
        self.kaug = self.sb(es2, "kaug%d" % hg, [72, 4, S], BF16)
        self.av = self.sb(es2, "av%d" % hg, [128, NTT, 256], BF16)
        self.kmT = self.sb(es2, "kmT%d" % hg, [64, 4, 8], BF16)
        with self.scope() as esw:
            self._attn_inproj_body(esw, hg)

    def _attn_inproj_body(self, esw, hg):
        kmf = self.sb(esw, "kmf%d" % hg, [64, 4, 8], F32)
        wq = self.sb(esw, "wq%d" % hg, [128, 8, 256], BF16)
        wk = self.sb(esw, "wk%d" % hg, [128, 8, 256], BF16)
        wv = self.sb(esw, "wv%d" % hg, [128, 8, 256], BF16)
        stg = [self.sb(esw, "kvstg%d_%d" % (hg, i), [128, 256], F32) for i in range(4)]
        win = self.w_in.rearrange("(k p) n -> p k n", p=128)
        c0 = hg * 256
        self.load(wq[:], win[:, :, c0:c0 + 256], ["wq"], queue="gpsimd")
        self.load(wk[:], win[:, :, 512 + c0:512 + c0 + 256], ["wk"], queue="gpsimd")
        self.load(wv[:], win[:, :, 1024 + c0:1024 + c0 + 256], ["wv"], queue="gpsimd")
        skip = ()
        for h in range(4):
            if "ind" in skip:
                break
            self.load(self.kaug[64:72, h, :], self.blkind[:, :], [("kaug_ind", h)], reads=["blkind"])
        hT = self.hT
        cnt = 0
        for h in range(4):
            if "fm" in skip:
                break
            for g in range(4):
                rd = [("hT", 4 * g + j) for j in range(4)]
                for which in range(2):
                    w = wq if which == 0 else wk
                    wkey = "wq" if which == 0 else "wk"
                    bank = cnt % 4
                    cnt += 1
                    pbk = ("pb", bank)
                    pt = self.pb[bank]
                    for k in range(8):
                        self.mm(pt[0:64, :], w[:, k, h * 64:(h + 1) * 64], hT[:, k, g * 512:(g + 1) * 512],
                                k == 0, k == 7, rd + [wkey], [pbk])
                    if which == 0:
                        self.A(lambda e, pt=pt, h=h, g=g: e.mul(out=self.qaug[0:64, h, g * 512:(g + 1) * 512],
                                                                in_=pt[0:64, :], mul=0.125),
                               reads=[pbk], writes=[("qT", h, g)])
                    else:
                        self.A(lambda e, pt=pt, h=h, g=g: e.copy(out=self.kaug[0:64, h, g * 512:(g + 1) * 512],
                                                                 in_=pt[0:64, :]),
                               reads=[pbk], writes=[("kT", h, g)])
                        self.V(lambda e, pt=pt, h=h, g=g: e.tensor_reduce(
                            out=kmf[:, h, 2 * g:2 * g + 2], in_=pt[0:64, :].rearrange("p (a b) -> p a b", a=2),
                            axis=AX.X, op=ALU.add), reads=[pbk], writes=["kmf"])
        self.V(lambda e: e.tensor_copy(out=self.kmT[:], in_=kmf[:]), reads=["kmf"], writes=["kmT"])
        cnt = 0
        for tt in range(NTT + 1):
            if "tm" in skip:
                break
            if "smp" in skip and tt == NTT:
                break
            if hg == 1 and tt == NTT:
                break
            n = 128 if tt < NTT else NS
            t0 = tt * 128
            for which in range(2):
                w = wk if which == 0 else wv
                wkey = "wk" if which == 0 else "wv"
                bank = 4 + cnt % 4
                sgi = cnt % 4
                cnt += 1
                pbk = ("pb", bank)
                pt = self.pb[bank]
                for k in range(8):
                    self.mm(pt[0:n, 0:256], hT[:, k, t0:t0 + n], w[:, k, :], k == 0, k == 7, [("hT", tt), wkey], [pbk])
                if tt < NTT:
                    st = stg[sgi]
                    sk = ("kvstg", sgi)
                    self.V(lambda e, pt=pt, st=st: e.tensor_copy(out=st[:, :], in_=pt[:, 0:256]), reads=[pbk], writes=[sk])
                    dst = self.k_p if which == 0 else self.v_p
                    if "st" not in skip:
                        self.store(dst[t0:t0 + 128, c0:c0 + 256], st[:, :], [sk])
                    if which == 1:
                        self.A(lambda e, pt=pt, tt=tt: e.copy(out=self.av[:, tt, :], in_=pt[:, 0:256]),
                               reads=[pbk], writes=[("av", tt)])
                else:
                    dsts = self.ak_s if which == 0 else self.av_s
                    dk = "ak_s" if which == 0 else "av_s"
                    self.V(lambda e, pt=pt, dsts=dsts: e.tensor_copy(out=dsts[:, c0:c0 + 256], in_=pt[0:NS, 0:256]),
                           reads=[pbk], writes=[(dk, hg)])
                    dst = self.k_s if which == 0 else self.v_s
                    if "st2" not in skip:
                        self.store(dst[:, c0:c0 + 256], dsts[:, c0:c0 + 256], [(dk, hg)])
        if hg == 1:
            return
        pbk = ("pb", 4)
        pt = self.pb[4]
        for k in range(8):
            self.mm(pt[0:NS, 0:256], hT[:, k, S:T], wq[:, k, :], k == 0, k == 7, [("hT", NTT), "wq"], [pbk])
        self.V(lambda e, pt=pt: e.tensor_copy(out=self.aq_s[:, c0:c0 + 256], in_=pt[0:NS, 0:256]),
               reads=[pbk], writes=[("aq_s", hg)])
        wx = wq
        for which, (dsts, dk, dst) in enumerate(((self.aq_s, "aq_s", None), (self.ak_s, "ak_s", self.k_s),
                                                  (self.av_s, "av_s", self.v_s))):
            self.load(wx[:], win[:, :, which * 512 + 256:which * 512 + 512], ["wq"], queue="gpsimd")
            bank = 5 + which % 2
            pbk = ("pb", bank)
            pt = self.pb[bank]
            for k in range(8):
                self.mm(pt[0:NS, 0:256], hT[:, k, S:T], wx[:, k, :], k == 0, k == 7, [("hT", NTT), "wq"], [pbk])
            od = dsts[:, 256:512]
            self.V(lambda e, pt=pt, od=od: e.tensor_copy(out=od, in_=pt[0:NS, 0:256]), reads=[pbk], writes=[(dk, 1)])
            if dst is not None:
                self.store(dst[:, 256:512], od, [(dk, 1)])

    def _side_tick(self):
        self._side_cnt = getattr(self, "_side_cnt", 0) + 1
        return self._side_cnt % 4 == 0

    def cc_keys(self):
        return [("cc", c, g) for c in range(8) for g in range(5)]

    def moba(self, hg, side=None, final=True):
        pb = self.pb
        with self.scope() as es3:
            sc = self.sb(es3, "mb_sc", [128, 4, 8], F32)
            big = self.sb(es3, "mb_big", [128, 4, 8, 8], F32)
            cntt = self.sb(es3, "mb_cnt", [128, 4, 8], F32)
            bias = self.sb(es3, "mb_bias", [128, 4, 8], F32)
            bT = [self.sb(es3, "mb_bT%d" % i, [32, 128], BF16) for i in range(2)]
            pT = [self.sb(es3, "mb_pT%d" % i, [128, 512], BF16) for i in range(3)]
            rec = [self.sb(es3, "mb_rec%d" % i, [128, 512], F32) for i in range(2)]
            for qt in range(NTT):
                bq = qt // 2
                g = qt // 4
                if bq >= 4:
                    for h in range(4):
                        self.mm(pb[4][:, h * 8:(h + 1) * 8], self.qaug[0:64, h, qt * 128:(qt + 1) * 128],
                                self.kmT[0:64, h, :], True, True, [("qT", h, g), "kmT"], [("pb", 4)])
                    self.V(lambda e: e.tensor_copy(out=sc[:].rearrange("p h n -> p (h n)"), in_=pb[4][:, 0:32]),
                           reads=[("pb", 4)], writes=["mb_sc"])
                    self.V(lambda e, bq=bq: e.memset(sc[:, :, bq:8], -1e30), reads=["mb_sc"], writes=["mb_sc"])
                    self.V(lambda e: e.tensor_tensor(out=big[:], in0=sc[:].unsqueeze(2).broadcast_to([128, 4, 8, 8]),
                                                     in1=sc[:].unsqueeze(3).broadcast_to([128, 4, 8, 8]), op=ALU.is_gt),
                           reads=["mb_sc"], writes=["mb_big"])
                    self.V(lambda e: e.tensor_reduce(out=cntt[:].rearrange("p h n -> p (h n)"),
                                                     in_=big[:].rearrange("p h n m -> p (h n) m"), axis=AX.X, op=ALU.add),
                           reads=["mb_big"], writes=["mb_cnt"])
                    self.V(lambda e: e.tensor_scalar(out=bias[:], in0=cntt[:], scalar1=3.0, scalar2=NEG,
                                                     op0=ALU.is_ge, op1=ALU.mult), reads=["mb_cnt"], writes=["mb_bias"])
                else:
                    self.V(lambda e: e.memset(bias[:], 0.0), writes=["mb_bias"])
                self.V(lambda e, bq=bq: e.memset(bias[:, :, bq:bq + 1], 0.0), reads=["mb_bias"], writes=["mb_bias"])
                if bq < 7:
                    self.V(lambda e, bq=bq: e.memset(bias[:, :, bq + 1:8], NEG), reads=["mb_bias"], writes=["mb_bias"])
                self.tp(pb[5][0:32, 0:128], bias[:].rearrange("p h n -> p (h n)"), self.identf[:, :], ["mb_bias"], [("pb", 5)])
                b_t = bT[qt % 2]
                bk = ("mb_bT", qt % 2)
                self.A(lambda e, b_t=b_t: e.copy(out=b_t[:, :], in_=pb[5][0:32, 0:128]), reads=[("pb", 5)], writes=[bk])
                for h in range(4):
                    self.load(self.qaug[64:72, h, qt * 128:(qt + 1) * 128], b_t[h * 8:(h + 1) * 8, :],
                              [("qB", h, qt)], reads=[bk])
            it = 0
            pc = 0
            for h in range(4):
                for g in range(4):
                    nkt = 4 * g + 4
                    nb = 2 + (it % 2 if side is None else 0)
                    db = 6 + (it % 2 if side is None else 0)
                    it += 1
                    qreads = [("qT", h, g)] + [("qB", h, 4 * g + j) for j in range(4)]

                    def score(kt, h=h, g=g, qreads=qreads):
                        bs = kt % 2
                        diag = kt >= 4 * g
                        self.mm(pb[bs][:, :], self.kaug[0:72, h, kt * 128:(kt + 1) * 128],
                                self.qaug[0:72, h, g * 512:(g + 1) * 512], True, not diag,
                                qreads + [("kT", h, kt // 4), ("kaug_ind", h)], [("pb", bs)])
                        if diag:
                            self.mm(pb[bs][:, :], self.identb[:, :], self.cb[:, kt - 4 * g, :], False, True,
                                    ["identb", "cb"], [("pb", bs)])
                    score(0)
                    for kt in range(nkt):
                        if kt + 1 < nkt:
                            score(kt + 1)
                        if side is not None and self._side_tick():
                            next(side, None)
                        bs = kt % 2
                        p_t = pT[pc % 3]
                        pk = ("mb_pT", pc % 3)
                        pc += 1
                        self.A(lambda e, p_t=p_t, bs=bs: e.activation(out=p_t[:, :], in_=pb[bs][:, :], func=AF.Exp),
                               reads=[("pb", bs)], writes=[pk])
                        hp = h // 2
                        self.mm(pb[nb][:, :], self.av[:, kt, hp * 128:(hp + 1) * 128], p_t[:, :], kt == 0, kt == nkt - 1,
                                [("av", kt), pk], [("pb", nb)])
                        self.mm(pb[db][:, :], self.onesb[:, :], p_t[:, :], kt == 0, kt == nkt - 1,
                                ["onesb", pk], [("pb", db)])
                    r0 = (h % 2) * 64
                    rc = rec[it % 2]
                    rk = ("mb_rec", it % 2)
                    if side is None:
                        self.V(lambda e, rc=rc, db=db, r0=r0: e.reciprocal(out=rc[r0:r0 + 64, :], in_=pb[db][r0:r0 + 64, :]),
                               reads=[("pb", db)], writes=[rk])
                    else:
                        self.A(lambda e, rc=rc, db=db, r0=r0: e.activation(out=rc[r0:r0 + 64, :], in_=pb[db][r0:r0 + 64, :], func=AF.Ln),
                               reads=[("pb", db)], writes=[rk])
                        self.A(lambda e, rc=rc, r0=r0: e.activation(out=rc[r0:r0 + 64, :], in_=rc[r0:r0 + 64, :], func=AF.Exp, scale=-1.0),
                               reads=[rk], writes=[rk])
                    c = hg * 2 + h // 2
                    self.V(lambda e, rc=rc, nb=nb, r0=r0, c=c, g=g: e.tensor_tensor(
                        out=self.concatT[r0:r0 + 64, c, g * 512:(g + 1) * 512], in0=pb[nb][r0:r0 + 64, :],
                        in1=rc[r0:r0 + 64, :], op=ALU.mult), reads=[("pb", nb), rk], writes=[("cc", c, g)])
            if side is not None and final:
                for _ in side:
                    pass

    def mlstm_gates(self, es2):
        pb = self.pb
        hT = self.hT
        sb = lambda n, s, d: self.sb(es2, n, s, d)
        self.tri01 = sb("tri01", [128, 4, 512], F32)
        self.utok = utok = sb("ml_utok", [128, NTT, 4], F32)
        self.Brow = Brow = sb("ml_Brow", [4, S], F32)
        self.gtok = gtok = sb("ml_gtok", [128, NTT, 4], F32)
        self.gbf = gbf = sb("ml_gbf", [128, NTT, 4], BF16)
        self.selh = selh = sb("ml_selh", [4, 4, 128], F32)
        self.meanm = meanm = sb("ml_meanm", [128, 128], F32)
        self.mgTs = mgT = sb("ml_mgT", [128, 4], F32)
        V, A, G = self.V, self.A, self.G
        G(lambda e: e.memset(self.tri01[:], 1.0), writes=["tri01"])
        for r in range(4):
            G(lambda e, r=r: e.affine_select(out=self.tri01[:, r, :], in_=self.tri01[:, r, :], pattern=[[1, 512]],
                                             compare_op=ALU.is_ge, fill=0.0, base=-r * 128, channel_multiplier=-1),
              reads=["tri01"], writes=["tri01"])
        self.load(mgT[:], self.mgT, ["ml_mgT"])
        G(lambda e: e.memset(meanm[:], 1.0 / 128.0), writes=["ml_meanm"])
        V(lambda e: e.tensor_copy(out=selh[:], in_=self.identf[0:4, 0:4].unsqueeze(2).broadcast_to([4, 4, 128])),
          reads=["identf"], writes=["ml_selh"])
        win = self.w_in.rearrange("(k p) n -> p k n", p=128)
        with self.scope() as es3:
            sb3 = lambda n, s, d: self.sb(es3, n, s, d)
            wg = sb3("mwg", [128, 8, 8], BF16)
            gt_tok = sb3("gt_tok", [128, NTT, 8], F32)
            bif = sb3("ml_bif", [128, 8], F32)
            ig = sb3("ml_ig", [128, NTT, 4], F32)
            lf = sb3("ml_lf", [128, NTT, 4], F32)
            tmp = sb3("ml_tmp", [128, NTT, 4], F32)
            Btok = sb3("ml_Btok", [128, NTT, 4], F32)
            urow = sb3("ml_urow", [4, S], F32)
            Mx = sb3("ml_Mx", [4, 1], F32)
            MxB = sb3("ml_MxB", [4, 128], F32)
            mfin = sb3("ml_mfin", [4, 1], F32)
            self.load(wg[:], win[:, :, 3584:3592], ["mwg"], queue="gpsimd")
            self.load(bif[:], self.b_if.broadcast_to([128, 8]), ["ml_bif"])
            for tt in range(NTT + 1):
                n = 128 if tt < NTT else NS
                t0 = tt * 128
                bank = 4 + tt % 4
                for k in range(8):
                    self.mm(pb[bank][0:n, 0:8], hT[:, k, t0:t0 + n], wg[:, k, :], k == 0, k == 7, [("hT", tt), "mwg"],
                            [("pb", bank)])
                if tt < NTT:
                    V(lambda e, bank=bank, tt=tt: e.tensor_copy(out=gt_tok[:, tt, :], in_=pb[bank][:, 0:8]),
                      reads=[("pb", bank)], writes=["gt_tok"])
                else:
                    V(lambda e, bank=bank: e.tensor_copy(out=self.msamp[:, 2048:2056], in_=pb[bank][0:NS, 0:8]),
                      reads=[("pb", bank)], writes=[("m_s", "g")])
            V(lambda e: e.tensor_tensor(out=ig[:], in0=gt_tok[:, :, 0:4], in1=bif[:, 0:4].unsqueeze(1).broadcast_to([128, NTT, 4]),
                                        op=ALU.add), reads=["gt_tok", "ml_bif"], writes=["ml_ig"])
            V(lambda e: e.tensor_tensor(out=tmp[:], in0=gt_tok[:, :, 4:8], in1=bif[:, 4:8].unsqueeze(1).broadcast_to([128, NTT, 4]),
                                        op=ALU.add), reads=["gt_tok", "ml_bif"], writes=["ml_tmp"])
            A(lambda e: e.activation(out=tmp[:], in_=tmp[:], func=AF.Exp, scale=-1.0), reads=["ml_tmp"], writes=["ml_tmp"])
            V(lambda e: e.tensor_scalar_add(out=tmp[:], in0=tmp[:], scalar1=1.0), reads=["ml_tmp"], writes=["ml_tmp"])
            A(lambda e: e.activation(out=tmp[:], in_=tmp[:], func=AF.Ln), reads=["ml_tmp"], writes=["ml_tmp"])
            V(lambda e: e.tensor_scalar_mul(out=lf[:], in0=tmp[:], scalar1=-1.0), reads=["ml_tmp"], writes=["ml_lf"])
            for tt in range(NTT):
                bank = 4 + tt % 2
                for j in range(tt + 1):
                    lhs = self.Utri if j == tt else self.onesf
                    self.mm(pb[bank][:, 0:4], lhs[:, :], lf[:, j, :], j == 0, j == tt, ["Utri", "onesf", "ml_lf"], [("pb", bank)])
                V(lambda e, bank=bank, tt=tt: e.tensor_copy(out=Btok[:, tt, :], in_=pb[bank][:, 0:4]),
                  reads=[("pb", bank)], writes=["ml_Btok"])
            V(lambda e: e.tensor_tensor(out=utok[:], in0=ig[:], in1=Btok[:], op=ALU.subtract),
              reads=["ml_ig", "ml_Btok"], writes=["ml_utok"])
            for tt in range(NTT):
                bank = 6 + tt % 2
                self.tp(pb[bank][0:4, 0:128], Btok[:, tt, :], self.identf[:, :], ["ml_Btok"], [("pb", bank)])
                self.tp(pb[bank][0:4, 128:256], utok[:, tt, :], self.identf[:, :], ["ml_utok"], [("pb", bank)])
                A(lambda e, bank=bank, tt=tt: e.copy(out=Brow[:, tt * 128:(tt + 1) * 128], in_=pb[bank][0:4, 0:128]),
                  reads=[("pb", bank)], writes=["ml_Brow"])
                A(lambda e, bank=bank, tt=tt: e.copy(out=urow[:, tt * 128:(tt + 1) * 128], in_=pb[bank][0:4, 128:256]),
                  reads=[("pb", bank)], writes=["ml_urow"])
            V(lambda e: e.tensor_reduce(out=Mx[:], in_=urow[:], axis=AX.X, op=ALU.max), reads=["ml_urow"], writes=["ml_Mx"])
            V(lambda e: e.tensor_scalar_max(out=Mx[:], in0=Mx[:], scalar1=0.0), reads=["ml_Mx"], writes=["ml_Mx"])
            V(lambda e: e.tensor_tensor(out=mfin[:], in0=Mx[:], in1=Brow[:, S - 1:S], op=ALU.add),
              reads=["ml_Mx", "ml_Brow"], writes=["ml_mfin"])
            self.store(self.m_p, mfin[:], ["ml_mfin"])
            V(lambda e: e.tensor_copy(out=MxB[:], in_=Mx[:, 0:1].broadcast_to([4, 128])), reads=["ml_Mx"], writes=["ml_MxB"])
            self.mm(pb[4][:, 0:4], MxB[:, :], self.identf[0:4, 0:4], True, True, ["ml_MxB", "identf"], [("pb", 4)])
            V(lambda e: e.tensor_tensor(out=gtok[:], in0=utok[:], in1=pb[4][:, 0:4].unsqueeze(1).broadcast_to([128, NTT, 4]),
                                        op=ALU.subtract), reads=["ml_utok", ("pb", 4)], writes=["ml_gtok"])
            A(lambda e: e.activation(out=gtok[:], in_=gtok[:], func=AF.Exp), reads=["ml_gtok"], writes=["ml_gtok"])
            V(lambda e: e.tensor_copy(out=gbf[:], in_=gtok[:]), reads=["ml_gtok"], writes=["ml_gbf"])

    def mlstm_inproj(self, es2, hp):
        pb = self.pb
        hT = self.hT
        self.mqT = self.sb(es2, "mqT", [128, 2, S], BF16)
        self.mkT = self.sb(es2, "mkT", [128, 2, S], BF16)
        self.sgT = self.sb(es2, "sgT", [128, 2, S], BF16)
        self.mk_tok = self.sb(es2, "mk_tok", [128, NTT, 256], BF16)
        self.mv_tok = self.sb(es2, "mv_tok", [128, NTT, 256], BF16)
        win = self.w_in.rearrange("(k p) n -> p k n", p=128)
        KS = 128.0 ** -0.5
        with self.scope() as es3:
            wb = [self.sb(es3, "mw%d" % i, [128, 8, 256], BF16) for i in range(2)]
            cnt = 0
            for bi, (name, c0) in enumerate((("mq", 1536), ("mk", 2048), ("mv", 2560), ("mo", 3072))):
                w = wb[bi % 2]
                wk = "mw%d" % (bi % 2)
                c0 = c0 + hp * 256
                self.load(w[:], win[:, :, c0:c0 + 256], [wk], queue="gpsimd")
                if name != "mv":
                    for hl in range(2):
                        for g in range(4):
                            bank = cnt % 4
                            cnt += 1
                            rd = [("hT", 4 * g + j) for j in range(4)] + [wk]
                            for k in range(8):
                                self.mm(pb[bank][:, :], w[:, k, hl * 128:(hl + 1) * 128], hT[:, k, g * 512:(g + 1) * 512],
                                        k == 0, k == 7, rd, [("pb", bank)])
                            sl = slice(g * 512, (g + 1) * 512)
                            if name == "mq":
                                self.A(lambda e, bank=bank, hl=hl, sl=sl: e.copy(out=self.mqT[:, hl, sl], in_=pb[bank][:, :]),
                                       reads=[("pb", bank)], writes=[("mqT", hl, g)])
                            elif name == "mk":
                                self.A(lambda e, bank=bank, hl=hl, sl=sl: e.mul(out=self.mkT[:, hl, sl], in_=pb[bank][:, :], mul=KS),
                                       reads=[("pb", bank)], writes=[("mkT", hl, g)])
                            else:
                                self.A(lambda e, bank=bank, hl=hl, sl=sl: e.activation(out=self.sgT[:, hl, sl], in_=pb[bank][:, :],
                                                                                      func=AF.Sigmoid),
                                       reads=[("pb", bank)], writes=[("sgT", hl, g)])
                for tt in range(NTT + 1):
                    if name in ("mq", "mo") and tt < NTT:
                        continue
                    n = 128 if tt < NTT else NS
                    t0 = tt * 128
                    bank = 4 + cnt % 4
                    cnt += 1
                    for k in range(8):
                        self.mm(pb[bank][0:n, 0:256], hT[:, k, t0:t0 + n], w[:, k, :], k == 0, k == 7, [("hT", tt), wk],
                                [("pb", bank)])
                    if tt < NTT:
                        if name == "mk":
                            self.V(lambda e, bank=bank, tt=tt: e.tensor_scalar_mul(out=self.mk_tok[:, tt, :], in0=pb[bank][:, 0:256],
                                                                                  scalar1=KS),
                                   reads=[("pb", bank)], writes=[("mk_tok", tt)])
                        else:
                            self.V(lambda e, bank=bank, tt=tt: e.tensor_copy(out=self.mv_tok[:, tt, :], in_=pb[bank][:, 0:256]),
                                   reads=[("pb", bank)], writes=[("mv_tok", tt)])
                    else:
                        o0 = {"mq": 0, "mk": 512, "mv": 1024, "mo": 1536}[name] + hp * 256
                        if name == "mk":
                            self.V(lambda e, bank=bank, o0=o0: e.tensor_scalar_mul(out=self.msamp[:, o0:o0 + 256],
                                                                                  in0=pb[bank][0:NS, 0:256], scalar1=KS),
                                   reads=[("pb", bank)], writes=[("m_s", name, hp)])
                        else:
                            self.V(lambda e, bank=bank, o0=o0: e.tensor_copy(out=self.msamp[:, o0:o0 + 256], in_=pb[bank][0:NS, 0:256]),
                                   reads=[("pb", bank)], writes=[("m_s", name, hp)])

    def mlstm_heads(self, es2, hp):
        pb = self.pb
        sb = lambda n, s, d: self.sb(es2, n, s, d)
        utok, Brow, gtok, gbf, selh, meanm, mgT = self.utok, self.Brow, self.gtok, self.gbf, self.selh, self.meanm, self.mgTs
        gv = [sb("ml_gv%d" % i, [128, 128], BF16) for i in range(2)]
        cst = sb("ml_cst", [128, 128], F32)
        nst = sb("ml_nst", [1, 128], F32)
        bbc = [sb("ml_bbc%d" % i, [128, 512], F32) for i in range(2)]
        wT = [sb("ml_wT%d" % i, [128, 512], F32) for i in range(3)]
        aT = [sb("ml_aT%d" % i, [128, 512], BF16) for i in range(3)]
        rd = sb("ml_rd", [128, 512], F32)
        hs = sb("ml_hs", [128, 512], F32)
        xc = sb("ml_xc", [128, 512], F32)
        sq = sb("ml_sq", [128, 512], F32)
        V, A, G = self.V, self.A, self.G
        gi = 0
        for hl in range(2):
            h = 2 * hp + hl
            hs_ = slice(hl * 128, (hl + 1) * 128)
            for tt in range(NTT):
                g_v = gv[gi % 2]
                gk = ("ml_gv", gi % 2)
                gi += 1
                V(lambda e, g_v=g_v, h=h, tt=tt, hs_=hs_: e.tensor_scalar_mul(out=g_v[:, :], in0=self.mv_tok[:, tt, hs_],
                                                                             scalar1=gtok[:, tt, h:h + 1]),
                  reads=[("mv_tok", tt), "ml_gtok"], writes=[gk])
                self.mm(pb[4][:, 0:128], g_v[:, :], self.mk_tok[:, tt, hs_], tt == 0, tt == NTT - 1,
                        [gk, ("mk_tok", tt)], [("pb", 4)])
                self.mm(pb[5][0:1, 0:128], gbf[:, tt, h:h + 1], self.mk_tok[:, tt, hs_], tt == 0, tt == NTT - 1,
                        ["ml_gbf", ("mk_tok", tt)], [("pb", 5)])
            V(lambda e: e.tensor_copy(out=cst[:], in_=pb[4][:, 0:128]), reads=[("pb", 4)], writes=["ml_cst"])
            self.store(self.C_p[h], cst[:], ["ml_cst"])
            V(lambda e: e.tensor_copy(out=nst[:], in_=pb[5][0:1, 0:128]), reads=[("pb", 5)], writes=["ml_nst"])
            self.store(self.n_p[h:h + 1, :], nst[:], ["ml_nst"])
        it = 0
        pc = 0
        pending = iter(())
        for hl in range(2):
            h = 2 * hp + hl
            hs_ = slice(hl * 128, (hl + 1) * 128)
            for g in range(4):
                nkt = 4 * g + 4
                nb = 2 + it % 2
                db = 6 + it % 2
                b_c = bbc[it % 2]
                bck = ("ml_bbc", it % 2)
                it += 1
                sl = slice(g * 512, (g + 1) * 512)
                self.mm(pb[4][:, :], selh[0:4, h, :], Brow[0:4, sl], True, True, ["ml_selh", "ml_Brow"], [("pb", 4)])
                A(lambda e, b_c=b_c: e.copy(out=b_c[:, :], in_=pb[4][:, :]), reads=[("pb", 4)], writes=[bck])

                def score(kt, hl=hl, g=g, sl=sl):
                    bs = kt % 2
                    self.mm(pb[bs][:, :], self.mkT[:, hl, kt * 128:(kt + 1) * 128], self.mqT[:, hl, sl], True, True,
                            [("mkT", hl, kt // 4), ("mqT", hl, g)], [("pb", bs)])
                score(0)
                for kt in range(nkt):
                    if kt + 1 < nkt:
                        score(kt + 1)
                    bs = kt % 2
                    w_t = wT[pc % 3]
                    a_t = aT[pc % 3]
                    wk = ("ml_wT", pc % 3)
                    ak = ("ml_aT", pc % 3)
                    pc += 1
                    A(lambda e, w_t=w_t, b_c=b_c, kt=kt, h=h: e.activation(out=w_t[:, :], in_=b_c[:, :], func=AF.Exp,
                                                                         bias=utok[:, kt, h:h + 1], scale=1.0),
                      reads=[bck, "ml_utok"], writes=[wk])
                    if kt >= 4 * g:
                        G(lambda e, w_t=w_t, r=kt - 4 * g: e.tensor_tensor(out=w_t[:, :], in0=w_t[:, :], in1=self.tri01[:, r, :],
                                                                         op=ALU.mult), reads=[wk, "tri01"], writes=[wk])
                    V(lambda e, a_t=a_t, w_t=w_t, bs=bs: e.tensor_tensor(out=a_t[:, :], in0=pb[bs][:, :], in1=w_t[:, :], op=ALU.mult),
                      reads=[("pb", bs), wk], writes=[ak])
                    self.mm(pb[nb][:, :], self.mv_tok[:, kt, hs_], a_t[:, :], kt == 0, kt == nkt - 1,
                            [("mv_tok", kt), ak], [("pb", nb)])
                    self.mm(pb[db][:, :], self.onesb[:, :], a_t[:, :], kt == 0, kt == nkt - 1, ["onesb", ak], [("pb", db)])
                    for _ in range(3):
                        next(pending, None)
                for _ in pending:
                    pass
                pending = self._ml_epilogue(h, hl, g, nb, db, sl, rd, hs, xc, sq, meanm, mgT)
        for _ in pending:
            pass

    def _ml_epilogue(self, h, hl, g, nb, db, sl, rd, hs, xc, sq, meanm, mgT):
        pb = self.pb
        V, A = self.V, self.A
        V(lambda e: e.tensor_scalar_mul(out=rd[:], in0=pb[db][:, :], scalar1=-1.0), reads=[("pb", db)], writes=["ml_rd"])
        yield
        V(lambda e: e.scalar_tensor_tensor(out=rd[:], in0=pb[db][:, :], scalar=1.0, in1=rd[:], op0=ALU.max, op1=ALU.max),
          reads=[("pb", db), "ml_rd"], writes=["ml_rd"])
        yield
        A(lambda e: e.activation(out=rd[:], in_=rd[:], func=AF.Ln), reads=["ml_rd"], writes=["ml_rd"])
        yield
        A(lambda e: e.activation(out=rd[:], in_=rd[:], func=AF.Exp, scale=-1.0), reads=["ml_rd"], writes=["ml_rd"])
        yield
        V(lambda e: e.tensor_tensor(out=hs[:], in0=pb[nb][:, :], in1=rd[:], op=ALU.mult),
          reads=[("pb", nb), "ml_rd"], writes=["ml_hs"])
        yield
        self.mm(pb[5][:, :], meanm[:, :], hs[:, :], True, True, ["ml_meanm", "ml_hs"], [("pb", 5)])
        yield
        V(lambda e: e.tensor_tensor(out=xc[:], in0=hs[:], in1=pb[5][:, :], op=ALU.subtract),
          reads=["ml_hs", ("pb", 5)], writes=["ml_xc"])
        yield
        A(lambda e: e.activation(out=sq[:], in_=xc[:], func=AF.Square), reads=["ml_xc"], writes=["ml_sq"])
        yield
        self.mm(pb[5][:, :], meanm[:, :], sq[:, :], True, True, ["ml_meanm", "ml_sq"], [("pb", 5)])
        yield
        A(lambda e: e.activation(out=sq[:], in_=pb[5][:, :], func=AF.Ln, bias=self.epsb[:, 0:1], scale=1.0),
          reads=[("pb", 5), "epsb"], writes=["ml_sq"])
        yield
        A(lambda e: e.activation(out=sq[:], in_=sq[:], func=AF.Exp, scale=-0.5), reads=["ml_sq"], writes=["ml_sq"])
        yield
        V(lambda e: e.tensor_tensor(out=xc[:], in0=xc[:], in1=sq[:], op=ALU.mult), reads=["ml_xc", "ml_sq"], writes=["ml_xc"])
        yield
        V(lambda e: e.scalar_tensor_tensor(out=self.concatT[:, 4 + h, sl], in0=xc[:], scalar=mgT[:, h:h + 1],
                                           in1=self.sgT[:, hl, sl], op0=ALU.mult, op1=ALU.mult),
          reads=["ml_xc", "ml_mgT", ("sgT", hl, g)], writes=[("cc", 4 + h, g)])

    def sample_gen(self, es1):
        pb = self.pb
        V, A, G = self.V, self.A, self.G
        sb = lambda n, s, d: self.sb(es1, n, s, d)
        ohc = sb("s_ohc", [128, 16, 16], F32)
        bmask = sb("s_bmask", [8, 512], F32)
        idf = sb("s_idf", [128, 256], F32)
        idx = sb("s_idx", [128, 256], I32)
        iop = sb("s_iop", [128, 1], F32)
        s_all = sb("s_all", [128, 16, 16, 8], F32)
        kv = [sb("s_kv%d" % i, [128, 512], BF16) for i in range(5)]
        prod = [sb("s_prod%d" % i, [128, 512], BF16) for i in range(2)]
        qbs = [sb("s_qbs%d" % i, [128, 512], BF16) for i in range(2)]
        oh = sb("s_oh", [16, 16, 128], F32)
        sself = sb("s_sself", [NS, 8], F32)
        eself = sb("s_eself", [NS, 8], F32)
        tk = sb("s_tk", [NS, 512], F32)
        bsum = sb("s_bsum", [NS, 16, 8], F32)
        blk = sb("s_blk", [NS, 8, 8], F32)
        big = sb("s_big", [NS, 8, 8, 8], F32)
        cnt = sb("s_cnt", [NS, 8, 8], F32)
        selb = sb("s_selb", [NS, 16, 8], F32)
        et = sb("s_et", [128, 128], F32)
        ee = [sb("s_ee%d" % i, [128, 16, 8], BF16) for i in range(2)]
        om = sb("s_om", [8, 512], F32)
        dcol = sb("s_dcol", [8, 1], F32)
        dd = sb("s_dd", [8, 8], F32)
        den = sb("s_den", [NS, 8], F32)
        att = sb("s_att", [NS, 512], F32)
        AQ = [("aq_s", 0), ("aq_s", 1)]
        yield
        V(lambda e: e.tensor_copy(out=oh[:], in_=self.identf[0:16, 0:16].unsqueeze(2).broadcast_to([16, 16, 128])),
          reads=["identf"], writes=["s_oh"])
        G(lambda e: e.memset(ohc[:], 1.0), writes=["s_ohc"])
        G(lambda e: e.affine_select(out=ohc[:], in_=ohc[:], pattern=[[1, 16], [-1, 16]], compare_op=ALU.is_equal,
                                    fill=0.0, base=0, channel_multiplier=0), reads=["s_ohc"], writes=["s_ohc"])
        G(lambda e: e.memset(bmask[:], 1.0), writes=["s_bmask"])
        G(lambda e: e.affine_select(out=bmask[:], in_=bmask[:], pattern=[[1, 512]], compare_op=ALU.is_ge,
                                    fill=0.0, base=0, channel_multiplier=-64), reads=["s_bmask"], writes=["s_bmask"])
        G(lambda e: e.affine_select(out=bmask[:], in_=bmask[:], pattern=[[-1, 512]], compare_op=ALU.is_ge,
                                    fill=0.0, base=63, channel_multiplier=64), reads=["s_bmask"], writes=["s_bmask"])
        self.load(idx[:], self.pt.broadcast_to([128, 256]), ["s_idx"])
        G(lambda e: e.iota(iop[:], pattern=[[0, 1]], base=0, channel_multiplier=1, allow_small_or_imprecise_dtypes=True),
          writes=["s_iop"])
        V(lambda e: e.tensor_copy(out=idf[:], in_=idx[:]), reads=["s_idx"], writes=["s_idf"])
        V(lambda e: e.tensor_scalar(out=idf[:], in0=idf[:], scalar1=128.0, scalar2=iop[:, 0:1], op0=ALU.mult, op1=ALU.add),
          reads=["s_idf", "s_iop"], writes=["s_idf"])
        V(lambda e: e.tensor_copy(out=idx[:], in_=idf[:]), reads=["s_idf", "s_idx"], writes=["s_idx"])
        V(lambda e: e.tensor_tensor(out=tk[:], in0=self.aq_s[:], in1=self.ak_s[:], op=ALU.mult),
          reads=AQ + [("ak_s", 0), ("ak_s", 1)], writes=["s_tk"])
        V(lambda e: e.tensor_reduce(out=sself[:], in_=tk[:].rearrange("p (h d) -> p h d", h=8), axis=AX.X, op=ALU.add),
          reads=["s_tk"], writes=["s_sself"])
        A(lambda e: e.activation(out=eself[:], in_=sself[:], func=AF.Exp, scale=0.125), reads=["s_sself"], writes=["s_eself"])
        yield
        DEPTH = 4
        NKV = len(kv)
        pages = [(b, j) for b in range(NS) for j in range(16)]
        qinfo = {}

        def k_gather(t):
            b, j = pages[t]
            col = b * 16 + j
            k_t = kv[t % NKV]
            self.S.dma("gpsimd", lambda e, k_t=k_t, col=col: e.indirect_dma_start(
                out=k_t[:], out_offset=None, in_=self.cache_k,
                in_offset=bass.IndirectOffsetOnAxis(ap=idx[:, col:col + 1], axis=0)), reads=["s_idx"], writes=[("s_kv", t % NKV)])

        def k_compute(t):
            b, j = pages[t]
            if j == 0:
                q_s = qbs[b % 2]
                qsk = ("s_qbs", b % 2)
                self.mm(pb[4][:, :], oh[0:16, b, :], self.aq_s[:, :], True, True, ["s_oh"] + AQ, [("pb", 4)])
                A(lambda e, q_s=q_s: e.copy(out=q_s[:, :], in_=pb[4][:, :]), reads=[("pb", 4)], writes=[qsk])
                qinfo[b] = (q_s, qsk)
            q_s, qsk = qinfo[b]
            k_t = kv[t % NKV]
            kk = ("s_kv", t % NKV)
            p_t = prod[t % 2]
            pk = ("s_prod", t % 2)
            V(lambda e, p_t=p_t, k_t=k_t, q_s=q_s: e.tensor_tensor(out=p_t[:], in0=k_t[:], in1=q_s[:, :], op=ALU.mult),
              reads=[kk, qsk], writes=[pk])
            so = s_all[:, b, j, :]
            V(lambda e, so=so, p_t=p_t: e.tensor_reduce(out=so, in_=p_t[:].rearrange("p (h d) -> p h d", h=8),
                                                        axis=AX.X, op=ALU.add), reads=[pk], writes=["s_all"])

        for t in range(len(pages) + DEPTH):
            if t < len(pages):
                k_gather(t)
            if t >= DEPTH:
                k_compute(t - DEPTH)
                if (t - DEPTH) % 16 == 15:
                    yield
        for b in range(NS):
            self.mm(pb[4][0:NS, 0:128], ohc[:, b, :], s_all[:, b, :, :].rearrange("p j h -> p (j h)"), b == 0, b == NS - 1,
                    ["s_ohc", "s_all"], [("pb", 4)])
        V(lambda e: e.tensor_copy(out=bsum[:].rearrange("p j h -> p (j h)"), in_=pb[4][0:NS, 0:128]),
          reads=[("pb", 4)], writes=["s_bsum"])
        b4 = bsum[:].rearrange("p (n t) h -> p n t h", t=2)
        V(lambda e: e.tensor_tensor(out=blk[:], in0=b4[:, :, 0, :], in1=b4[:, :, 1, :], op=ALU.add),
          reads=["s_bsum"], writes=["s_blk"])
        bhn = blk[:].rearrange("p n h -> p h n")
        V(lambda e: e.tensor_tensor(out=big[:], in0=bhn.unsqueeze(2).broadcast_to([NS, 8, 8, 8]),
                                    in1=bhn.unsqueeze(3).broadcast_to([NS, 8, 8, 8]), op=ALU.is_gt),
          reads=["s_blk"], writes=["s_big"])
        V(lambda e: e.tensor_reduce(out=cnt[:].rearrange("p h n -> p (h n)"), in_=big[:].rearrange("p h n m -> p (h n) m"),
                                    axis=AX.X, op=ALU.add), reads=["s_big"], writes=["s_cnt"])
        V(lambda e: e.tensor_scalar(out=cnt[:], in0=cnt[:], scalar1=3.0, scalar2=NEG, op0=ALU.is_ge, op1=ALU.mult),
          reads=["s_cnt"], writes=["s_cnt"])
        cnh = cnt[:].rearrange("p h n -> p n h")
        for t in range(2):
            so = selb[:].rearrange("p (n t) h -> p n t h", t=2)[:, :, t, :]
            V(lambda e, so=so: e.tensor_copy(out=so, in_=cnh), reads=["s_cnt"], writes=["s_selb"])
        yield
        einfo = {}

        def v_gather(t):
            b, j = pages[t]
            col = b * 16 + j
            v_t = kv[t % NKV]
            self.S.dma("gpsimd", lambda e, v_t=v_t, col=col: e.indirect_dma_start(
                out=v_t[:], out_offset=None, in_=self.cache_v,
                in_offset=bass.IndirectOffsetOnAxis(ap=idx[:, col:col + 1], axis=0)), reads=["s_idx"], writes=[("s_kv", t % NKV)])

        def v_compute(t):
            b, j = pages[t]
            if j == 0:
                self.mm(pb[4][:, 128:256], oh[0:16, b, :], selb[:].rearrange("p j h -> p (j h)"), True, True,
                        ["s_oh", "s_selb"], [("pb", 4)])
                e_t = ee[b % 2]
                ek = ("s_ee", b % 2)
                ef = e_t[:].rearrange("p j h -> p (j h)")
                sa = s_all[:, b, :, :].rearrange("p j h -> p (j h)")
                V(lambda e, sa=sa: e.scalar_tensor_tensor(out=et[:], in0=sa, scalar=0.125, in1=pb[4][:, 128:256],
                                                          op0=ALU.mult, op1=ALU.add),
                  reads=["s_all", ("pb", 4)], writes=["s_et"])
                A(lambda e, ef=ef: e.activation(out=ef, in_=et[:], func=AF.Exp), reads=["s_et"], writes=[ek])
                einfo[b] = (e_t, ek)
                for jj in range(16):
                    self.mm(pb[4][0:8, 256:257], e_t[:, jj, :], self.onesb[:, 0:1], jj == 0, jj == 15, [ek, "onesb"], [("pb", 4)])
            e_t, ek = einfo[b]
            v_t = kv[t % NKV]
            vk = ("s_kv", t % NKV)
            self.mm(pb[5][0:8, :], e_t[:, j, :], v_t[:, :], j == 0, j == 15, [ek, vk], [("pb", 5)])
            if j == 15:
                V(lambda e: e.tensor_tensor(out=om[:], in0=pb[5][0:8, :], in1=bmask[:], op=ALU.mult),
                  reads=[("pb", 5), "s_bmask"], writes=["s_om"])
                V(lambda e: e.tensor_copy(out=dcol[:], in_=pb[4][0:8, 256:257]), reads=[("pb", 4)], writes=["s_dcol"])
                V(lambda e: e.tensor_scalar_mul(out=dd[:], in0=self.identf[0:8, 0:8], scalar1=dcol[:, 0:1]),
                  reads=["s_dcol", "identf"], writes=["s_dd"])
                self.mm(pb[3][0:NS, :], ohc[0:8, b, :], om[:, :], b == 0, b == NS - 1, ["s_ohc", "s_om"], [("pb", 3)])
                self.mm(pb[7][0:NS, 0:8], ohc[0:8, b, :], dd[:, :], b == 0, b == NS - 1, ["s_ohc", "s_dd"], [("pb", 7)])

        for t in range(len(pages) + DEPTH):
            if t < len(pages):
                v_gather(t)
            if t >= DEPTH:
                v_compute(t - DEPTH)
                if (t - DEPTH) % 16 == 15:
                    yield
        V(lambda e: e.tensor_tensor(out=den[:], in0=pb[7][0:NS, 0:8], in1=eself[:], op=ALU.add),
          reads=[("pb", 7), "s_eself"], writes=["s_den"])
        V(lambda e: e.reciprocal(out=den[:], in_=den[:]), reads=["s_den"], writes=["s_den"])
        V(lambda e: e.tensor_tensor(out=tk[:].rearrange("p (h d) -> p h d", h=8),
                                    in0=self.av_s[:].rearrange("p (h d) -> p h d", h=8),
                                    in1=eself[:].unsqueeze(2).broadcast_to([NS, 8, 64]), op=ALU.mult),
          reads=[("av_s", 0), ("av_s", 1), "s_eself", "s_sself"], writes=["s_tk"])
        V(lambda e: e.tensor_tensor(out=tk[:], in0=tk[:], in1=pb[3][0:NS, :], op=ALU.add),
          reads=["s_tk", ("pb", 3)], writes=["s_tk"])
        V(lambda e: e.tensor_tensor(out=att[:].rearrange("p (h d) -> p h d", h=8),
                                    in0=tk[:].rearrange("p (h d) -> p h d", h=8),
                                    in1=den[:].unsqueeze(2).broadcast_to([NS, 8, 64]), op=ALU.mult),
          reads=["s_tk", "s_den"], writes=["s_att"])
        for c in range(4):
            bank = 4 + c % 2
            self.tp(pb[bank][:, 0:NS], att[:, c * 128:(c + 1) * 128], self.identf[0:NS, 0:NS], ["s_att"], [("pb", bank)])
            A(lambda e, bank=bank, c=c: e.copy(out=self.concatT[:, c, S:T], in_=pb[bank][:, 0:NS]),
              reads=[("pb", bank)], writes=[("cc", c, 4)])

    def sample_mlstm(self):
        pb = self.pb
        V, A, G = self.V, self.A, self.G
        ms = self.msamp
        with self.scope() as es1:
            sb = lambda n, s, d: self.sb(es1, n, s, d)
            oh = sb("m_oh", [16, 16, 128], F32)
            bif = sb("m_bif", [NS, 8], F32)
            m0 = sb("m_m0", [NS, 4], F32)
            n0 = sb("m_n0", [NS, 512], F32)
            ig = sb("m_ig", [NS, 4], F32)
            lf = sb("m_lf", [NS, 4], F32)
            mi_ = sb("m_mi", [NS, 4], F32)
            mt = sb("m_mt", [NS, 4], F32)
            w = sb("m_w", [NS, 4], F32)
            gi_ = sb("m_gi", [NS, 4], F32)
            emt = sb("m_emt", [NS, 4], F32)
            qk = sb("m_qk", [NS, 4], F32)
            nq = sb("m_nq", [NS, 4], F32)
            a = sb("m_a", [NS, 4], F32)
            dn = sb("m_dn", [NS, 4], F32)
            t512 = sb("m_t512", [NS, 512], F32)
            rows = sb("m_rows", [NS, 1024], F32)
            nnew = sb("m_nnew", [NS, 512], F32)
            c0 = [sb("m_c0_%d" % i, [128, 512], F32) for i in range(2)]
            cn = [sb("m_cn_%d" % i, [128, 512], F32) for i in range(2)]
            pr = sb("m_pr", [128, 512], F32)
            CqT = sb("m_CqT", [128, 4, NS], F32)
            mvT = sb("m_mvT", [128, 4, NS], F32)
            cq = sb("m_cq", [NS, 512], F32)
            hh = sb("m_hh", [NS, 512], F32)
            st = sb("m_st", [NS, 8], F32)
            mgb = sb("m_mgb", [NS, 512], F32)
            sg = sb("m_sg", [NS, 512], F32)
            v3 = lambda ap: ap.rearrange("p (h d) -> p h d", h=4)
            bc = lambda ap: ap.unsqueeze(2).broadcast_to([NS, 4, 128])
            mq, mk, mv, mo = ms[:, 0:512], ms[:, 512:1024], ms[:, 1024:1536], ms[:, 1536:2048]
            MS = [("m_s", nm, hp) for nm in ("mq", "mk", "mv", "mo") for hp in range(2)] + [("m_s", "g")]
            self.load(bif[:], self.b_if.broadcast_to([NS, 8]), ["m_bif"])
            self.load(m0[:], self.st_m, ["m_m0"])
            self.load(n0[:], self.st_n, ["m_n0"])
            self.load(mgb[:], self.mg.broadcast_to([NS, 512]), ["m_mgb"])
            V(lambda e: e.tensor_copy(out=oh[:], in_=self.identf[0:16, 0:16].unsqueeze(2).broadcast_to([16, 16, 128])),
              reads=["identf"], writes=["m_oh"])
            V(lambda e: e.tensor_tensor(out=ig[:], in0=ms[:, 2048:2052], in1=bif[:, 0:4], op=ALU.add), reads=MS + ["m_bif"], writes=["m_ig"])
            V(lambda e: e.tensor_tensor(out=lf[:], in0=ms[:, 2052:2056], in1=bif[:, 4:8], op=ALU.add), reads=MS + ["m_bif"], writes=["m_lf"])
            A(lambda e: e.activation(out=lf[:], in_=lf[:], func=AF.Exp, scale=-1.0), reads=["m_lf"], writes=["m_lf"])
            A(lambda e: e.activation(out=lf[:], in_=lf[:], func=AF.Ln, bias=1.0), reads=["m_lf"], writes=["m_lf"])
            V(lambda e: e.tensor_scalar_mul(out=lf[:], in0=lf[:], scalar1=-1.0), reads=["m_lf"], writes=["m_lf"])
            V(lambda e: e.tensor_tensor(out=mi_[:], in0=lf[:], in1=m0[:], op=ALU.add), reads=["m_lf", "m_m0"], writes=["m_mi"])
            V(lambda e: e.tensor_tensor(out=mt[:], in0=mi_[:], in1=ig[:], op=ALU.max), reads=["m_mi", "m_ig"], writes=["m_mt"])
            self.store(self.outs["m_s"], mt[:], ["m_mt"])
            V(lambda e: e.tensor_tensor(out=w[:], in0=mi_[:], in1=mt[:], op=ALU.subtract), reads=["m_mi", "m_mt"], writes=["m_w"])
            A(lambda e: e.activation(out=w[:], in_=w[:], func=AF.Exp), reads=["m_w"], writes=["m_w"])
            V(lambda e: e.tensor_tensor(out=gi_[:], in0=ig[:], in1=mt[:], op=ALU.subtract), reads=["m_ig", "m_mt"], writes=["m_gi"])
            A(lambda e: e.activation(out=gi_[:], in_=gi_[:], func=AF.Exp), reads=["m_gi"], writes=["m_gi"])
            A(lambda e: e.activation(out=emt[:], in_=mt[:], func=AF.Exp, scale=-1.0), reads=["m_mt"], writes=["m_emt"])
            V(lambda e: e.tensor_tensor(out=t512[:], in0=mq, in1=mk, op=ALU.mult), reads=MS, writes=["m_t512"])
            V(lambda e: e.tensor_reduce(out=qk[:], in_=v3(t512[:]), axis=AX.X, op=ALU.add), reads=["m_t512"], writes=["m_qk"])
            V(lambda e: e.tensor_tensor(out=t512[:], in0=mq, in1=n0[:], op=ALU.mult), reads=MS + ["m_n0", "m_qk"], writes=["m_t512"])
            V(lambda e: e.tensor_reduce(out=nq[:], in_=v3(t512[:]), axis=AX.X, op=ALU.add), reads=["m_t512"], writes=["m_nq"])
            V(lambda e: e.tensor_tensor(out=a[:], in0=gi_[:], in1=qk[:], op=ALU.mult), reads=["m_gi", "m_qk"], writes=["m_a"])
            V(lambda e: e.tensor_tensor(out=dn[:], in0=w[:], in1=nq[:], op=ALU.mult), reads=["m_w", "m_nq"], writes=["m_dn"])
            V(lambda e: e.tensor_tensor(out=dn[:], in0=dn[:], in1=a[:], op=ALU.add), reads=["m_dn", "m_a"], writes=["m_dn"])
            V(lambda e: e.tensor_scalar_mul(out=nq[:], in0=dn[:], scalar1=-1.0), reads=["m_dn"], writes=["m_nq"])
            V(lambda e: e.tensor_tensor(out=dn[:], in0=dn[:], in1=nq[:], op=ALU.max), reads=["m_dn", "m_nq"], writes=["m_dn"])
            V(lambda e: e.tensor_tensor(out=dn[:], in0=dn[:], in1=emt[:], op=ALU.max), reads=["m_dn", "m_emt"], writes=["m_dn"])
            V(lambda e: e.reciprocal(out=dn[:], in_=dn[:]), reads=["m_dn"], writes=["m_dn"])
            V(lambda e: e.tensor_tensor(out=v3(rows[:, 512:1024]), in0=v3(mk), in1=bc(gi_[:]), op=ALU.mult),
              reads=MS + ["m_gi"], writes=["m_rows"])
            V(lambda e: e.tensor_tensor(out=v3(nnew[:]), in0=v3(n0[:]), in1=bc(w[:]), op=ALU.mult), reads=["m_n0", "m_w"], writes=["m_nnew"])
            V(lambda e: e.tensor_tensor(out=nnew[:], in0=nnew[:], in1=rows[:, 512:1024], op=ALU.add),
              reads=["m_nnew", "m_rows"], writes=["m_nnew"])
            self.store(self.n_s, nnew[:], ["m_nnew"])
            V(lambda e: e.tensor_copy(out=v3(rows[:, 0:512]), in_=bc(w[:])), reads=["m_w", "m_rows"], writes=["m_rows"])
            for h in range(4):
                bank = 6 + h % 2
                self.tp(pb[bank][:, 0:NS], ms[:, 1024 + h * 128:1024 + (h + 1) * 128], self.identf[0:NS, 0:NS], MS, [("pb", bank)])
                A(lambda e, bank=bank, h=h: e.copy(out=mvT[:, h, :], in_=pb[bank][:, 0:NS]), reads=[("pb", bank)], writes=["m_mvT"])
            for b in range(NS):
                c_0 = c0[b % 2]
                ck = ("m_c0", b % 2)
                c_n = cn[b % 2]
                nk = ("m_cn", b % 2)
                self.load(c_0[:].rearrange("p (h d) -> p h d", h=4), self.st_C[b].rearrange("h v d -> v h d"), [ck])
                self.mm(pb[0][:, :], oh[0:16, b, :], ms[:, 0:512], True, True, ["m_oh"] + MS, [("pb", 0)])
                self.mm(pb[1][:, :], oh[0:16, b, :], rows[:, 0:512], True, True, ["m_oh", "m_rows"], [("pb", 1)])
                self.mm(pb[2][:, :], oh[0:16, b, :], rows[:, 512:1024], True, True, ["m_oh", "m_rows"], [("pb", 2)])
                V(lambda e, c_0=c_0: e.tensor_tensor(out=pr[:], in0=c_0[:], in1=pb[0][:, :], op=ALU.mult),
                  reads=[ck, ("pb", 0)], writes=["m_pr"])
                cqo = CqT[:, :, b]
                V(lambda e, cqo=cqo: e.tensor_reduce(out=cqo, in_=pr[:].rearrange("p (h d) -> p h d", h=4), axis=AX.X, op=ALU.add),
                  reads=["m_pr"], writes=["m_CqT"])
                V(lambda e, c_0=c_0, c_n=c_n: e.tensor_tensor(out=c_n[:], in0=c_0[:], in1=pb[1][:, :], op=ALU.mult),
                  reads=[ck, ("pb", 1)], writes=[nk])
                for h in range(4):
                    cs_ = c_n[:, h * 128:(h + 1) * 128]
                    gk_ = pb[2][:, h * 128:(h + 1) * 128]
                    sc_ = mvT[:, h, b:b + 1]
                    V(lambda e, cs_=cs_, gk_=gk_, sc_=sc_: e.scalar_tensor_tensor(out=cs_, in0=gk_, scalar=sc_, in1=cs_,
                                                                                  op0=ALU.mult, op1=ALU.add),
                      reads=[nk, ("pb", 2), "m_mvT"], writes=[nk])
                self.store(self.C_s[b].rearrange("h v d -> v h d"), c_n[:].rearrange("p (h d) -> p h d", h=4), [nk])
            for h in range(4):
                self.tp(pb[4][0:NS, h * 128:(h + 1) * 128], CqT[:, h, :], self.identf[:, :], ["m_CqT"], [("pb", 4)])
            V(lambda e: e.tensor_tensor(out=v3(cq[:]), in0=v3(pb[4][0:NS, :]), in1=bc(w[:]), op=ALU.mult),
              reads=[("pb", 4), "m_w"], writes=["m_cq"])
            V(lambda e: e.tensor_tensor(out=v3(hh[:]), in0=v3(mv), in1=bc(a[:]), op=ALU.mult), reads=MS + ["m_a"], writes=["m_hh"])
            V(lambda e: e.tensor_tensor(out=hh[:], in0=hh[:], in1=cq[:], op=ALU.add), reads=["m_hh", "m_cq"], writes=["m_hh"])
            V(lambda e: e.tensor_tensor(out=v3(hh[:]), in0=v3(hh[:]), in1=bc(dn[:]), op=ALU.mult), reads=["m_hh", "m_dn"], writes=["m_hh"])
            V(lambda e: e.tensor_reduce(out=st[:, 0:4], in_=v3(hh[:]), axis=AX.X, op=ALU.add), reads=["m_hh"], writes=["m_st"])
            V(lambda e: e.tensor_scalar_mul(out=st[:, 0:4], in0=st[:, 0:4], scalar1=1.0 / 128.0), reads=["m_st"], writes=["m_st"])
            V(lambda e: e.tensor_tensor(out=v3(hh[:]), in0=v3(hh[:]), in1=bc(st[:, 0:4]), op=ALU.subtract), reads=["m_hh", "m_st"], writes=["m_hh"])
            V(lambda e: e.tensor_tensor(out=t512[:], in0=hh[:], in1=hh[:], op=ALU.mult), reads=["m_hh", "m_nq"], writes=["m_t512"])
            V(lambda e: e.tensor_reduce(out=st[:, 4:8], in_=v3(t512[:]), axis=AX.X, op=ALU.add), reads=["m_t512"], writes=["m_st"])
            A(lambda e: e.activation(out=st[:, 4:8], in_=st[:, 4:8], func=AF.Sqrt, bias=self.epsb[0:NS, 0:1], scale=1.0 / 128.0),
              reads=["m_st", "epsb"], writes=["m_st"])
            V(lambda e: e.reciprocal(out=st[:, 4:8], in_=st[:, 4:8]), reads=["m_st"], writes=["m_st"])
            V(lambda e: e.tensor_tensor(out=v3(hh[:]), in0=v3(hh[:]), in1=bc(st[:, 4:8]), op=ALU.mult), reads=["m_hh", "m_st"], writes=["m_hh"])
            V(lambda e: e.tensor_tensor(out=hh[:], in0=hh[:], in1=mgb[:], op=ALU.mult), reads=["m_hh", "m_mgb"], writes=["m_hh"])
            A(lambda e: e.activation(out=sg[:], in_=mo, func=AF.Sigmoid), reads=MS, writes=["m_sg"])
            V(lambda e: e.tensor_tensor(out=hh[:], in0=hh[:], in1=sg[:], op=ALU.mult), reads=["m_hh", "m_sg"], writes=["m_hh"])
            for c in range(4):
                bank = 6 + c % 2
                self.tp(pb[bank][:, 0:NS], hh[:, c * 128:(c + 1) * 128], self.identf[0:NS, 0:NS], ["m_hh"], [("pb", bank)])
                A(lambda e, bank=bank, c=c: e.copy(out=self.concatT[:, 4 + c, S:T], in_=pb[bank][:, 0:NS]),
                  reads=[("pb", bank)], writes=[("cc", 4 + c, 4)])

    def layernorm(self, zt, n, gam, bet, xc, st, out, zk="lnz"):
        V, A = self.V, self.A
        s0, s1, s2 = st[0:n, 0:1], st[0:n, 1:2], st[0:n, 2:3]
        xn, gn, bn, en = xc[0:n, :], gam[0:n, :], bet[0:n, :], self.epsb[0:n, 0:1]
        A(lambda e: e.activation(out=xn, in_=zt, func=AF.Identity, accum_out=s0), reads=[zk], writes=["lnxc", "lnst"])
        V(lambda e: e.tensor_scalar_mul(out=s0, in0=s0, scalar1=-1.0 / D), reads=["lnst"], writes=["lnst"])
        A(lambda e: e.activation(out=xn, in_=zt, func=AF.Square, bias=s0, scale=1.0, accum_out=s1),
          reads=[zk, "lnst"], writes=["lnxc", "lnst"])
        A(lambda e: e.activation(out=s1, in_=s1, func=AF.Sqrt, bias=en, scale=1.0 / D), reads=["lnst", "epsb"], writes=["lnst"])
        V(lambda e: e.reciprocal(out=s1, in_=s1), reads=["lnst"], writes=["lnst"])
        V(lambda e: e.tensor_tensor(out=s2, in0=s0, in1=s1, op=ALU.mult), reads=["lnst"], writes=["lnst"])
        A(lambda e: e.activation(out=xn, in_=zt, func=AF.Identity, bias=s2, scale=s1), reads=[zk, "lnst"], writes=["lnxc"])
        V(lambda e: e.tensor_tensor(out=xn, in0=xn, in1=gn, op=ALU.mult), reads=["lnxc", "lnc"], writes=["lnxc"])
        V(lambda e: e.tensor_tensor(out=out, in0=xn, in1=bn, op=ALU.add), reads=["lnxc", "lnc"], writes=[zk])

    def layernorm_multi(self, items, gam, bet, junk, st):
        V, A = self.V, self.A
        C = []
        for zt, n, zk, sl in items:
            C.append((zt, n, zk, st[0:n, 3 * sl:3 * sl + 1], st[0:n, 3 * sl + 1:3 * sl + 2], st[0:n, 3 * sl + 2:3 * sl + 3],
                      ("lnst", sl), ("lnjunk", sl), junk[0:n, :], gam[0:n, :], bet[0:n, :], self.epsb[0:n, 0:1]))
        for zt, n, zk, s0, s1, s2, sk, jk, jn, gn, bn, en in C:
            A(lambda e, jn=jn, zt=zt, s0=s0: e.activation(out=jn, in_=zt, func=AF.Identity, accum_out=s0), reads=[zk], writes=[jk, sk])
        for zt, n, zk, s0, s1, s2, sk, jk, jn, gn, bn, en in C:
            V(lambda e, s0=s0: e.tensor_scalar_mul(out=s0, in0=s0, scalar1=-1.0 / D), reads=[sk], writes=[sk])
        for zt, n, zk, s0, s1, s2, sk, jk, jn, gn, bn, en in C:
            A(lambda e, jn=jn, zt=zt, s0=s0, s1=s1: e.activation(out=jn, in_=zt, func=AF.Square, bias=s0, scale=1.0, accum_out=s1),
              reads=[zk, sk], writes=[jk, sk])
        for zt, n, zk, s0, s1, s2, sk, jk, jn, gn, bn, en in C:
            A(lambda e, s1=s1, en=en: e.activation(out=s1, in_=s1, func=AF.Sqrt, bias=en, scale=1.0 / D), reads=[sk, "epsb"], writes=[sk])
        for zt, n, zk, s0, s1, s2, sk, jk, jn, gn, bn, en in C:
            V(lambda e, s1=s1: e.reciprocal(out=s1, in_=s1), reads=[sk], writes=[sk])
            V(lambda e, s0=s0, s1=s1, s2=s2: e.tensor_tensor(out=s2, in0=s0, in1=s1, op=ALU.mult), reads=[sk], writes=[sk])
        for zt, n, zk, s0, s1, s2, sk, jk, jn, gn, bn, en in C:
            A(lambda e, zt=zt, s1=s1, s2=s2: e.activation(out=zt, in_=zt, func=AF.Identity, bias=s2, scale=s1), reads=[sk], writes=[zk])
        for zt, n, zk, s0, s1, s2, sk, jk, jn, gn, bn, en in C:
            V(lambda e, zt=zt, gn=gn: e.tensor_tensor(out=zt, in0=zt, in1=gn, op=ALU.mult), reads=["lnc"], writes=[zk])
            V(lambda e, zt=zt, bn=bn: e.tensor_tensor(out=zt, in0=zt, in1=bn, op=ALU.add), reads=["lnc"], writes=[zk])

    def post(self):
        pb = self.pb
        V, A, G = self.V, self.A, self.G
        NC2 = 512 + NS
        with self.scope() as es1:
            sb = lambda n, s, d: self.sb(es1, n, s, d)
            wo = sb("wo", [128, 8, D], BF16)
            lnc = sb("lnc", [128, 4, D], F32)
            xt = [sb("pxt%d" % i, [128, D], F32) for i in range(2)]
            x1g = sb("x1g", [128, 5, D], F32)
            xcs = sb("lnxc", [128, D], F32)
            st = sb("lnst", [128, 15], F32)
            h2T = sb("h2T", [128, 8, NC2], BF16)
            actT = sb("actT", [128, NFF, NC2], BF16)
            wg = [sb("wgc%d" % i, [128, 8, 256], BF16) for i in range(2)]
            wu = [sb("wuc%d" % i, [128, 8, 256], BF16) for i in range(2)]
            wd = [sb("wdq%d" % i, [128, NFF, 256], BF16) for i in range(2)]
            sg = [sb("sgt%d" % i, [128, NC2], F32) for i in range(2)]
            tmp = sb("ptmp", [128, 512], F32)
            self.load(wo[:], self.w_out.rearrange("(k p) n -> p k n", p=128), ["wo"], queue="gpsimd")
            for i, src in enumerate((self.ln1_g, self.ln1_b, self.ln2_g, self.ln2_b)):
                self.load(lnc[:, i, :], src.broadcast_to([128, D]), ["lnc"])
            wgv = self.w_gate.rearrange("(k p) n -> p k n", p=128)
            wuv = self.w_up.rearrange("(k p) n -> p k n", p=128)
            wdv = self.w_down.rearrange("(f p) n -> p f n", p=128)
            xi = 0
            wdi = 0

            def ld_wd(q):
                self.load(wd[q % 2][:], wdv[:, :, q * 256:(q + 1) * 256], [("wdq", q % 2)], queue="gpsimd")

            def ldw(f2):
                self.load(wg[f2 % 2][:], wgv[:, :, f2 * 256:(f2 + 1) * 256], [("wgc", f2 % 2)], queue="gpsimd")
                self.load(wu[f2 % 2][:], wuv[:, :, f2 * 256:(f2 + 1) * 256], [("wuc", f2 % 2)], queue="gpsimd")

            for gi in range(4):
                t0 = gi * 512
                tiles = [(t0 + ti * 128, 128, ti * 128, False) for ti in range(4)]
                if gi == 3:
                    tiles.append((S, NS, 512, True))
                ncols = 512 + (NS if gi == 3 else 0)
                ldw(0)
                xinfo = {}

                def s1_mm(ti, gi=gi, tiles=tiles, xinfo=xinfo):
                    nonlocal xi
                    c0, n, lc, samp = tiles[ti]
                    x_sb = xt[xi % 2]
                    xk = ("pxt", xi % 2)
                    xi += 1
                    xinfo[ti] = (x_sb, xk)
                    if samp:
                        self.load(x_sb[0:n, :], self.xs[:, :], [xk])
                    else:
                        self.load(x_sb[0:n, :], self.xp[c0:c0 + n, :], [xk])
                    cg = 4 if samp else gi
                    for half in range(2):
                        bank = half + 2 * (ti % 2)
                        for c in range(8):
                            self.mm(pb[bank][0:n, :], self.concatT[:, c, c0:c0 + n], wo[:, c, half * 512:(half + 1) * 512],
                                    c == 0, c == 7, [("cc", c, cg), "wo"], [("pb", bank)])

                def s1_rest(ti, gi=gi, tiles=tiles, xinfo=xinfo):
                    c0, n, lc, samp = tiles[ti]
                    x_sb, xk = xinfo[ti]
                    zt = x1g[0:n, ti, :]
                    zk = ("x1g", ti)
                    for half in range(2):
                        bank = half + 2 * (ti % 2)
                        gsrc = self.grow[0:NS, half * 512:(half + 1) * 512] if samp else self.gbc[:, half * 512:(half + 1) * 512]
                        tn, pbn = tmp[0:n, :], pb[bank][0:n, :]
                        zh, xh = zt[:, half * 512:(half + 1) * 512], x_sb[0:n, half * 512:(half + 1) * 512]
                        V(lambda e, tn=tn, pbn=pbn, gsrc=gsrc: e.tensor_tensor(out=tn, in0=pbn, in1=gsrc, op=ALU.mult),
                          reads=[("pb", bank), "grow", "gbc"], writes=["ptmp"])
                        V(lambda e, zh=zh, xh=xh, tn=tn: e.scalar_tensor_tensor(out=zh, in0=xh, scalar=ALPHA, in1=tn,
                                                                                op0=ALU.mult, op1=ALU.add),
                          reads=[xk, "ptmp"], writes=[zk])

                def s1_tp(ti, tiles=tiles):
                    c0, n, lc, samp = tiles[ti]
                    zt = x1g[0:n, ti, :]
                    zk = ("x1g", ti)
                    for k in range(8):
                        bank = 4 + k % 4
                        self.tp(pb[bank][:, 0:n], zt[:, k * 128:(k + 1) * 128], self.identf[0:n, 0:n], [zk], [("pb", bank)])
                        if samp:
                            V(lambda e, bank=bank, k=k: e.tensor_tensor(out=tmp[:, 0:NS], in0=pb[bank][:, 0:NS],
                                                                        in1=self.modT[:, 32 + k, 0:NS], op=ALU.mult),
                              reads=[("pb", bank), "modT"], writes=["ptmp"])
                            V(lambda e, k=k: e.tensor_tensor(out=h2T[:, k, 512:NC2], in0=tmp[:, 0:NS], in1=self.modT[:, 24 + k, 0:NS],
                                                             op=ALU.add), reads=["ptmp", "modT"], writes=[("h2T", 4)])
                        else:
                            A(lambda e, bank=bank, k=k, lc=lc: e.activation(
                                out=h2T[:, k, lc:lc + 128], in_=pb[bank][:, 0:128], func=AF.Identity,
                                bias=self.modT[:, 24 + k, 32:33], scale=self.modT[:, 32 + k, 32:33]),
                                reads=[("pb", bank), "modT"], writes=[("h2T", ti)])

                nt = len(tiles)
                s1_mm(0)
                s1_mm(1)
                for ti in range(nt):
                    s1_rest(ti)
                    if ti + 2 < nt:
                        s1_mm(ti + 2)
                self.layernorm_multi([(x1g[0:n, ti, :], n, ("x1g", ti), ti) for ti, (c0, n, lc, samp) in enumerate(tiles)],
                                     lnc[:, 0, :], lnc[:, 1, :], xcs, st)
                for ti in range(nt):
                    s1_tp(ti)
                h2k = [("h2T", ti) for ti in range(len(tiles))]
                ld_wd(0)
                ld_wd(1)
                for f in range(NFF):
                    f2, fl = f // 2, f % 2
                    if fl == 0 and f2 + 1 < NFF // 2:
                        ldw(f2 + 1)
                    gb = f % 2
                    ub = 2 + f % 2
                    wgt, wut = wg[f2 % 2], wu[f2 % 2]
                    for k in range(8):
                        self.mm(pb[gb][:, 0:512], wgt[:, k, fl * 128:(fl + 1) * 128], h2T[:, k, 0:512], k == 0, k == 7,
                                [("wgc", f2 % 2)] + h2k, [("pb", gb)])
                    for k in range(8):
                        self.mm(pb[ub][:, 0:512], wut[:, k, fl * 128:(fl + 1) * 128], h2T[:, k, 0:512], k == 0, k == 7,
                                [("wuc", f2 % 2)] + h2k, [("pb", ub)])
                    s_t = sg[f % 2]
                    sk = "sgt%d" % (f % 2)
                    sa, ga, ua, aa = s_t[:, 0:512], pb[gb][:, 0:512], pb[ub][:, 0:512], actT[:, f, 0:512]
                    A(lambda e, sa=sa, ga=ga: e.activation(out=sa, in_=ga, func=AF.Silu), reads=[("pb", gb)], writes=[sk])
                    V(lambda e, aa=aa, sa=sa, ua=ua: e.tensor_tensor(out=aa, in0=sa, in1=ua, op=ALU.mult),
                      reads=[sk, ("pb", ub)], writes=["actT"])
                    if gi == 3:
                        eb = 4 + f % 2
                        for k in range(8):
                            self.mm(pb[eb][:, 0:NS], wgt[:, k, fl * 128:(fl + 1) * 128], h2T[:, k, 512:NC2], k == 0, k == 7,
                                    [("wgc", f2 % 2)] + h2k, [("pb", eb)])
                        for k in range(8):
                            self.mm(pb[eb][:, 32:32 + NS], wut[:, k, fl * 128:(fl + 1) * 128], h2T[:, k, 512:NC2], k == 0, k == 7,
                                    [("wuc", f2 % 2)] + h2k, [("pb", eb)])
                        sa, ga, ua, aa = s_t[:, 512:NC2], pb[eb][:, 0:NS], pb[eb][:, 32:32 + NS], actT[:, f, 512:NC2]
                        A(lambda e, sa=sa, ga=ga: e.activation(out=sa, in_=ga, func=AF.Silu), reads=[("pb", eb)], writes=[sk])
                        V(lambda e, aa=aa, sa=sa, ua=ua: e.tensor_tensor(out=aa, in0=sa, in1=ua, op=ALU.mult),
                          reads=[sk, ("pb", eb)], writes=["actT"])
                for q in range(4):
                    w_d = wd[q % 2]
                    wk_ = ("wdq", q % 2)
                    for ti, (c0, n, lc, samp) in enumerate(tiles):
                        bank = 4 + wdi % 4
                        wdi += 1
                        for f in range(NFF):
                            self.mm(pb[bank][0:n, 0:256], actT[:, f, lc:lc + n], w_d[:, f, :], f == 0, f == NFF - 1,
                                    ["actT", wk_], [("pb", bank)])
                        gsrc = (self.grow[0:NS, D + q * 256:D + (q + 1) * 256] if samp
                                else self.gbc[:, D + q * 256:D + (q + 1) * 256])
                        tn, pbn = tmp[0:n, 0:256], pb[bank][0:n, 0:256]
                        V(lambda e, tn=tn, pbn=pbn, gsrc=gsrc: e.tensor_tensor(out=tn, in0=pbn, in1=gsrc, op=ALU.mult),
                          reads=[("pb", bank), "grow", "gbc"], writes=["ptmp"])
                        zs = x1g[0:n, ti, q * 256:(q + 1) * 256]
                        V(lambda e, zs=zs, tn=tn: e.scalar_tensor_tensor(out=zs, in0=zs, scalar=ALPHA, in1=tn,
                                                                         op0=ALU.mult, op1=ALU.add),
                          reads=[("x1g", ti), "ptmp"], writes=[("x1g", ti)])
                    if q + 2 < 4:
                        ld_wd(q + 2)
                self.layernorm_multi([(x1g[0:n, ti, :], n, ("x1g", ti), ti) for ti, (c0, n, lc, samp) in enumerate(tiles)],
                                     lnc[:, 2, :], lnc[:, 3, :], xcs, st)
                for ti, (c0, n, lc, samp) in enumerate(tiles):
                    zt = x1g[0:n, ti, :]
                    if samp:
                        self.store(self.y_s[:, :], zt, [("x1g", ti)])
                    else:
                        self.store(self.y_p[c0:c0 + n, :], zt, [("x1g", ti)])


def prep_inputs(inputs, i):
    f = lambda a: np.ascontiguousarray(np.asarray(a))
    c33 = np.zeros((33, D), np.float32)
    c33[0:NS] = inputs["c_sample"][NS * i:NS * (i + 1)]
    c33[32] = inputs["c_prompt"][i]
    cT = f(c33.T.reshape(8, 128, 33).transpose(1, 0, 2))
    m = {
        "xp": f(inputs["x_prompt"][i]),
        "xs": f(inputs["x_sample"][NS * i:NS * (i + 1), 0, :]),
        "cT": cT,
        "w_ada": f(inputs["w_ada"][0]),
        "b_adaT": f(inputs["b_ada"][0].reshape(48, 128).T),
        "b_ada": f(inputs["b_ada"][0].reshape(1, -1)),
        "w_in": f(inputs["w_in"][0]),
        "b_if": f(inputs["b_if"][0].reshape(1, 8)),
        "mgT": f(inputs["mlstm_norm_g"][0].reshape(4, 128).T),
        "mg": f(inputs["mlstm_norm_g"][0].reshape(1, 512)),
        "w_out": f(inputs["w_out"][0]),
        "ln1_g": f(inputs["ln1_g"][0].reshape(1, D)),
        "ln1_b": f(inputs["ln1_b"][0].reshape(1, D)),
        "w_gate": f(inputs["w_gate"][0]),
        "w_up": f(inputs["w_up"][0]),
        "w_down": f(inputs["w_down"][0]),
        "ln2_g": f(inputs["ln2_g"][0].reshape(1, D)),
        "ln2_b": f(inputs["ln2_b"][0].reshape(1, D)),
        "cache_k": inputs["cache_k"][0].reshape(-1, 512),
        "cache_v": inputs["cache_v"][0].reshape(-1, 512),
        "pt": f(inputs["page_table"][NS * i:NS * (i + 1)].reshape(1, NS * 16)).astype(np.int32),
        "st_C": f(inputs["state_C"][0, NS * i:NS * (i + 1)]),
        "st_n": f(inputs["state_n"][0, NS * i:NS * (i + 1)].reshape(NS, 512)),
        "st_m": f(inputs["state_m"][0, NS * i:NS * (i + 1)]),
    }
    return m


def kernel(**inputs):
    inputs = {k: np.asarray(v) for k, v in inputs.items()}
    b = Builder()
    nc = b.build()
    n = 8
    in_maps = [prep_inputs(inputs, i) for i in range(n)]
    res = run_bass_kernel_spmd(nc, in_maps, core_ids=list(range(n)))
    r = res.results
    cat = lambda k: np.stack([r[i][k] for i in range(n)])
    y_prompt = cat("y_p")
    y_sample = np.concatenate([r[i]["y_s"] for i in range(n)])[:, None, :]
    k_prompt = cat("k_p").reshape(1, 8, S, 8, 64)
    v_prompt = cat("v_p").reshape(1, 8, S, 8, 64)
    C_prompt = cat("C_p").reshape(1, 8, 4, 128, 128)
    n_prompt = cat("n_p").reshape(1, 8, 4, 128)
    m_prompt = cat("m_p").reshape(1, 8, 4)
    k_sample = np.concatenate([r[i]["k_s"] for i in range(n)]).reshape(1, 128, 1, 8, 64)
    v_sample = np.concatenate([r[i]["v_s"] for i in range(n)]).reshape(1, 128, 1, 8, 64)
    C_sample = np.concatenate([r[i]["C_s"] for i in range(n)]).reshape(1, 128, 4, 128, 128)
    n_sample = np.concatenate([r[i]["n_s"] for i in range(n)]).reshape(1, 128, 4, 128)
    m_sample = np.concatenate([r[i]["m_s"] for i in range(n)]).reshape(1, 128, 4)
    return (y_prompt, y_sample, k_prompt, v_prompt, C_prompt, n_prompt, m_prompt,
            k_sample, v_sample, C_sample, n_sample, m_sample)
```
